# Optimizing a Trainium2 kernel written in Bass

```python
import math
import jax
import jax.numpy as jnp
from jax import lax
import numpy as np

D_MODEL = 1024
BATCH = 2
SEQ = 8192
DEPTH = 2

CTX_LEN = 256
GRID_W = 64
BR_W = D_MODEL // 2
N_BRANCH = 3
EPS = 1e-6
DA_DH = 64
DA_DV = 2 * DA_DH
DA_HEADS = BR_W // DA_DV
ROPE_BASE = 10000.0
Q_BLOCK = 128
ML_DH = 128
ML_HEADS = BR_W // ML_DH
ML_CHUNK = 64
ML_CONV = 3
NA_DH = 64
NA_HEADS = BR_W // NA_DH
NA_KH_MAX = 8
NA_KW = 16
NA_QB = 16
NA_KBW = NA_QB + NA_KW
PROJ_SIZES = (BR_W,) * 4 + (BR_W,) * 5 + (4 * ML_HEADS,) + (BR_W,) * 4 + (N_BRANCH * D_MODEL,)
D_IN = sum(PROJ_SIZES)

kernel_name = 'hybrid_diffattn_mlstm_natten_prefix_block'


def rms_norm(x, g):
    xf = x.astype(jnp.float32)
    y = xf * lax.rsqrt(jnp.mean(xf * xf, axis=-1, keepdims=True) + EPS)
    return (y * g).astype(x.dtype)


def split_proj(p):
    idx = [int(i) for i in np.cumsum(PROJ_SIZES)[:-1]]
    return jnp.split(p, idx, axis=-1)


def axial_rope_tables(n_tok, dh):
    t = jnp.arange(n_tok, dtype=jnp.int32)
    row = (t // GRID_W).astype(jnp.float32)
    col = (t % GRID_W).astype(jnp.float32)
    inv = ROPE_BASE ** (-jnp.arange(0, dh // 2, 2, dtype=jnp.float32) / (dh // 2))
    ang = jnp.concatenate([row[:, None] * inv, col[:, None] * inv], axis=-1)
    return jnp.cos(ang), jnp.sin(ang)


def apply_rope(x, cos, sin):
    xp = x.reshape(x.shape[:-1] + (x.shape[-1] // 2, 2))
    x1, x2 = xp[..., 0], xp[..., 1]
    shp = (1, cos.shape[0]) + (1,) * (x.ndim - 3) + (cos.shape[1],)
    cs = cos.reshape(shp).astype(x.dtype)
    sn = sin.reshape(shp).astype(x.dtype)
    return jnp.stack([x1 * cs - x2 * sn, x1 * sn + x2 * cs], axis=-1).reshape(x.shape)


def softmax_attn(q, k, v):
    s = jnp.einsum('bqhd,bkhd->bhqk', q, k).astype(jnp.float32) * (q.shape[-1] ** -0.5)
    p = jax.nn.softmax(s, axis=-1).astype(v.dtype)
    return jnp.einsum('bhqk,bkhd->bqhd', p, v)


def diff_attn_dense(q, k, v, lam):
    s = jnp.einsum('bqhcd,bkhcd->bhcqk', q, k).astype(jnp.float32) * (q.shape[-1] ** -0.5)
    p = jax.nn.softmax(s, axis=-1)
    a = (p[:, :, 0] - lam * p[:, :, 1]).astype(v.dtype)
    return jnp.einsum('bhqk,bkhv->bqhv', a, v)


def diff_attn_blocks(q, k, v, lam):
    B, S = q.shape[:2]
    nb = S // Q_BLOCK
    qb = jnp.moveaxis(q.reshape((B, nb, Q_BLOCK) + q.shape[2:]), 1, 0)
    o = lax.map(lambda qq: diff_attn_dense(qq, k, v, lam), qb)
    return jnp.moveaxis(o, 0, 1).reshape((B, S) + o.shape[3:])


def diff_head_norm(o, g, lam_init):
    y = rms_norm(o, g) * (1.0 - lam_init)
    return y.reshape(o.shape[:2] + (-1,))


def dwconv_centred(x, w, b):
    K = w.shape[0]
    y = lax.conv_general_dilated(x, w[:, None, :].astype(x.dtype), window_strides=(1,),
                                 padding=[((K - 1) // 2, K // 2)],
                                 dimension_numbers=('NWC', 'WIO', 'NWC'),
                                 feature_group_count=x.shape[-1])
    return y + b


def mlstm_inputs(q, k, v, g, conv_w, conv_b, gate_b):
    B, T, _ = q.shape
    qk = jax.nn.silu(dwconv_centred(jnp.concatenate([q, k], axis=-1), conv_w, conv_b))
    qq, kk = jnp.split(qk, 2, axis=-1)
    shp = (B, T, ML_HEADS, ML_DH)
    gates = g.reshape(B, T, 2, 2, ML_HEADS).astype(jnp.float32) + gate_b.astype(jnp.float32)
    log_i = gates[:, :, :, 0]
    log_f = jax.nn.log_sigmoid(gates[:, :, :, 1])
    return qq.reshape(shp), kk.reshape(shp) * (ML_DH ** -0.5), v.reshape(shp), log_i, log_f


def mlstm_chunkwise(q, k, v, log_i, log_f, state, with_out):
    B, T, H, d = q.shape
    L = ML_CHUNK
    nc = T // L
    to_chunks = lambda a: jnp.moveaxis(a.reshape((B, nc, L) + a.shape[2:]), 1, 0)
    xs = (to_chunks(q), to_chunks(k), to_chunks(v), to_chunks(log_i), to_chunks(log_f))
    causal = jnp.tril(jnp.ones((L, L), dtype=bool))

    def step(carry, xc):
        C, n, m = carry
        qc, kc, vc, lic, lfc = xc
        qc = qc.astype(jnp.float32)
        kc = kc.astype(jnp.float32)
        vc = vc.astype(jnp.float32)
        b = jnp.moveaxis(jnp.cumsum(lfc.astype(jnp.float32), axis=1), 1, 2)
        li = jnp.moveaxis(lic.astype(jnp.float32), 1, 2)
        bL = b[..., -1]
        g_s = bL[..., None] - b + li
        m_new = jnp.maximum(bL + m, jnp.max(g_s, axis=-1))
        w_s = jnp.exp(g_s - m_new[..., None])
        decay = jnp.exp(bL + m - m_new)
        C_new = decay[..., None, None] * C + jnp.einsum('bhs,bshk,bshv->bhkv', w_s, kc, vc)
        n_new = decay[..., None] * n + jnp.einsum('bhs,bshk->bhk', w_s, kc)
        if with_out:
            dmat = jnp.where(causal, b[..., :, None] - b[..., None, :] + li[..., None, :], -jnp.inf)
            m_inter = b + m[..., None]
            m_t = jnp.maximum(m_inter, jnp.max(dmat, axis=-1))
            a_inter = jnp.exp(m_inter - m_t)
            qk = jnp.einsum('bthd,bshd->bhts', qc, kc) * jnp.exp(dmat - m_t[..., None])
            num = a_inter[..., None] * jnp.einsum('bthk,bhkv->bhtv', qc, C) + jnp.einsum('bhts,bshv->bhtv', qk, vc)
            den = a_inter * jnp.einsum('bthk,bhk->bht', qc, n) + jnp.sum(qk, axis=-1)
            h = num / jnp.maximum(jnp.abs(den), jnp.exp(-m_t))[..., None]
            h = jnp.moveaxis(h, 1, 2)
        else:
            h = None
        return (C_new, n_new, m_new), h

    final, hs = lax.scan(step, state, xs)
    if with_out:
        hs = jnp.moveaxis(hs, 0, 1).reshape(B, T, H, d).astype(q.dtype)
    return final, hs


def mlstm_bidirectional(lat, ctx, with_ctx_out):
    ql, kl, vl, lil, lfl = lat
    qc, kc, vc, lic, lfc = ctx
    B, _, H, d = ql.shape
    zero = (jnp.zeros((B, H, d, d), jnp.float32), jnp.zeros((B, H, d), jnp.float32), jnp.zeros((B, H), jnp.float32))
    flip = lambda a: jnp.flip(a, axis=1)
    st_f, hc_f = mlstm_chunkwise(qc, kc, vc, lic[:, :, 0], lfc[:, :, 0], zero, with_ctx_out)
    _, hl_f = mlstm_chunkwise(ql, kl, vl, lil[:, :, 0], lfl[:, :, 0], st_f, True)
    st_b, hc_b = mlstm_chunkwise(flip(qc), flip(kc), flip(vc), flip(lic[:, :, 1]), flip(lfc[:, :, 1]), zero, with_ctx_out)
    _, hl_b = mlstm_chunkwise(flip(ql), flip(kl), flip(vl), flip(lil[:, :, 1]), flip(lfl[:, :, 1]), st_b, True)
    h_lat = hl_f + flip(hl_b)
    h_ctx = hc_f + flip(hc_b) if with_ctx_out else None
    return h_lat, h_ctx


def mlstm_out(h, o_pre, z, g):
    hn = rms_norm(h, g).reshape(h.shape[:2] + (-1,))
    return jax.nn.sigmoid(o_pre) * hn * jax.nn.silu(z)


def neighbourhood_attn(q, k, v, kc, vc, rpb):
    B, S, H, d = q.shape
    rows = S // GRID_W
    kh = min(NA_KH_MAX, rows)
    ncb = GRID_W // NA_QB
    qcol = np.arange(GRID_W).reshape(ncb, NA_QB)
    kc0 = np.clip(np.arange(ncb) * NA_QB - NA_KW // 2, 0, GRID_W - NA_KBW)
    kcol = kc0[:, None] + np.arange(NA_KBW)[None, :]
    cs = np.clip(qcol - NA_KW // 2, 0, GRID_W - NA_KW)
    col_ok = (kcol[:, None, :] >= cs[:, :, None]) & (kcol[:, None, :] < cs[:, :, None] + NA_KW)
    col_idx = np.clip(kcol[:, None, :] - qcol[:, :, None] + NA_KW - 1, 0, 2 * NA_KW - 2)
    mask = jnp.asarray(col_ok[:, :, None, :])
    qb = q.reshape(B, rows, ncb, NA_QB, H, d)
    kb = k.reshape(B, rows, GRID_W, H, d)[:, :, kcol]
    vb = v.reshape(B, rows, GRID_W, H, d)[:, :, kcol]
    scale = d ** -0.5
    n_loc = kh * NA_KBW

    def row_block(r):
        rs = jnp.clip(r - kh // 2, 0, rows - kh)
        kr = lax.dynamic_slice_in_dim(kb, rs, kh, axis=1)
        vr = lax.dynamic_slice_in_dim(vb, rs, kh, axis=1)
        qr = lax.dynamic_index_in_dim(qb, r, axis=1, keepdims=False)
        row_idx = rs + jnp.arange(kh) - r + NA_KH_MAX - 1
        bias = rpb[:, row_idx[None, None, :, None], col_idx[:, :, None, :]]
        s_loc = jnp.einsum('bnqhd,binkhd->bhnqik', qr, kr).astype(jnp.float32) * scale + bias.astype(jnp.float32)
        s_loc = jnp.where(mask, s_loc, -jnp.inf)
        s_ctx = jnp.einsum('bnqhd,bjhd->bhnqj', qr, kc).astype(jnp.float32) * scale
        s_all = jnp.concatenate([s_loc.reshape(s_loc.shape[:4] + (n_loc,)), s_ctx], axis=-1)
        p = jax.nn.softmax(s_all, axis=-1).astype(v.dtype)
        p_loc = p[..., :n_loc].reshape(s_loc.shape)
        p_ctx = p[..., n_loc:]
        return jnp.einsum('bhnqik,binkhd->bnqhd', p_loc, vr) + jnp.einsum('bhnqj,bjhd->bnqhd', p_ctx, vc)

    o = lax.map(row_block, jnp.arange(rows))
    return jnp.moveaxis(o, 0, 1).reshape(B, S, H, d)


def merge_branches(ys, gm, w_br, w_out):
    y = jnp.stack(ys, axis=2)
    proj = jnp.einsum('btnw,nwd->btnd', y, w_br)
    g = jax.nn.sigmoid(gm.reshape(gm.shape[:2] + (N_BRANCH, -1)))
    return jnp.sum(g * proj, axis=2) @ w_out


def hybrid_layer(l, x, xc, sc, scc, cos, sin, w_mod, b_mod, norm_g, w_in, lam_q, lam_k, da_g,
                 conv_w, conv_b, gate_b, ml_g, rpb, w_br, w_out, update_ctx):
    B, S, _ = x.shape
    L = xc.shape[1]
    shift, scale, gate = jnp.split(sc @ w_mod + b_mod, 3, axis=-1)
    shift_c, scale_c, gate_c = jnp.split(scc @ w_mod + b_mod, 3, axis=-1)
    h = rms_norm(x, norm_g) * (1 + scale[:, None]) + shift[:, None]
    hc = rms_norm(xc, norm_g) * (1 + scale_c) + shift_c
    aq, ak, av, az, bq, bk, bv, bo, bz, bg, cq, ck, cv, cz, gm = split_proj(h @ w_in)
    aqc, akc, avc, azc, bqc, bkc, bvc, boc, bzc, bgc, cqc, ckc, cvc, czc, gmc = split_proj(hc @ w_in)

    lam_init = 0.8 - 0.6 * math.exp(-0.3 * l)
    lam = (jnp.exp(jnp.sum(lam_q[0] * lam_k[0])) - jnp.exp(jnp.sum(lam_q[1] * lam_k[1]))).astype(jnp.float32) + lam_init
    da_qk = lambda t: t.reshape(t.shape[:2] + (DA_HEADS, 2, DA_DH))
    da_v = lambda t: t.reshape(t.shape[:2] + (DA_HEADS, DA_DV))
    k_all = jnp.concatenate([apply_rope(da_qk(ak), cos, sin), da_qk(akc)], axis=1)
    v_all = jnp.concatenate([da_v(av), da_v(avc)], axis=1)
    o_a = diff_attn_blocks(apply_rope(da_qk(aq), cos, sin), k_all, v_all, lam)
    y_a = diff_head_norm(o_a, da_g, lam_init) * jax.nn.silu(az)

    lat_b = mlstm_inputs(bq, bk, bv, bg, conv_w, conv_b, gate_b)
    ctx_b = mlstm_inputs(bqc, bkc, bvc, bgc, conv_w, conv_b, gate_b)
    h_b, h_bc = mlstm_bidirectional(lat_b, ctx_b, update_ctx)
    y_b = mlstm_out(h_b, bo, bz, ml_g)

    na_h = lambda t: t.reshape(t.shape[:2] + (NA_HEADS, NA_DH))
    o_c = neighbourhood_attn(na_h(cq), na_h(ck), na_h(cv), na_h(ckc), na_h(cvc), rpb)
    y_c = o_c.reshape(B, S, BR_W) * jax.nn.silu(cz)

    x = x + gate[:, None] * merge_branches((y_a, y_b, y_c), gm, w_br, w_out)
    if update_ctx:
        o_ac = diff_attn_dense(da_qk(aqc), da_qk(akc), da_v(avc), lam)
        y_ac = diff_head_norm(o_ac, da_g, lam_init) * jax.nn.silu(azc)
        y_bc = mlstm_out(h_bc, boc, bzc, ml_g)
        y_cc = softmax_attn(na_h(cqc), na_h(ckc), na_h(cvc)).reshape(B, L, BR_W) * jax.nn.silu(czc)
        xc = xc + gate_c * merge_branches((y_ac, y_bc, y_cc), gmc, w_br, w_out)
    return x, xc


def setup_inputs(seed: int = 0) -> dict:
    key = jax.random.key(seed)
    ks = jax.random.split(key, 20)
    f32 = jnp.float32
    nrm = lambda k, shape, s: jax.random.normal(k, shape, f32) * s
    f_base = jnp.linspace(3.0, 6.0, ML_HEADS, dtype=f32)
    gate_base = jnp.stack([jnp.zeros((ML_HEADS,), f32), f_base])[None, None]
    return {
        'x': nrm(ks[0], (BATCH, SEQ, D_MODEL), 1.0),
        'c': nrm(ks[1], (BATCH, D_MODEL), 1.0),
        'ctx': nrm(ks[2], (BATCH, CTX_LEN, D_MODEL), 1.0),
        'c_ctx': nrm(ks[3], (D_MODEL,), 1.0),
        'w_mod': nrm(ks[4], (DEPTH, D_MODEL, 3 * D_MODEL), 0.5 * D_MODEL ** -0.5),
        'b_mod': nrm(ks[5], (DEPTH, 3 * D_MODEL), 0.02),
        'norm_g': 1.0 + nrm(ks[6], (DEPTH, D_MODEL), 0.02),
        'w_in': nrm(ks[7], (DEPTH, D_MODEL, D_IN), D_MODEL ** -0.5),
        'da_lam_q': nrm(ks[8], (DEPTH, 2, DA_DH), 0.1),
        'da_lam_k': nrm(ks[9], (DEPTH, 2, DA_DH), 0.1),
        'da_norm_g': 1.0 + nrm(ks[10], (DEPTH, DA_DV), 0.02),
        'ml_conv_w': nrm(ks[11], (DEPTH, ML_CONV, 2 * BR_W), ML_CONV ** -0.5),
        'ml_conv_b': nrm(ks[12], (DEPTH, 2 * BR_W), 0.02),
        'ml_gate_b': gate_base + nrm(ks[13], (DEPTH, 2, 2, ML_HEADS), 0.1),
        'ml_norm_g': 1.0 + nrm(ks[14], (DEPTH, ML_DH), 0.02),
        'na_rpb': nrm(ks[15], (DEPTH, NA_HEADS, 2 * NA_KH_MAX - 1, 2 * NA_KW - 1), 0.05),
        'w_br': nrm(ks[16], (DEPTH, N_BRANCH, BR_W, D_MODEL), BR_W ** -0.5),
        'w_out': nrm(ks[17], (DEPTH, D_MODEL, D_MODEL), D_MODEL ** -0.5),
        'final_g': 1.0 + nrm(ks[18], (D_MODEL,), 0.02),
    }


def reference(x, c, ctx, c_ctx, w_mod, b_mod, norm_g, w_in, da_lam_q, da_lam_k, da_norm_g,
              ml_conv_w, ml_conv_b, ml_gate_b, ml_norm_g, na_rpb, w_br, w_out, final_g):
    cos, sin = axial_rope_tables(x.shape[1], DA_DH)
    sc = jax.nn.silu(c)
    scc = jax.nn.silu(c_ctx)
    xc = ctx
    for l in range(DEPTH):
        x, xc = hybrid_layer(l, x, xc, sc, scc, cos, sin, w_mod[l], b_mod[l], norm_g[l], w_in[l],
                             da_lam_q[l], da_lam_k[l], da_norm_g[l], ml_conv_w[l], ml_conv_b[l],
                             ml_gate_b[l], ml_norm_g[l], na_rpb[l], w_br[l], w_out[l],
                             l < DEPTH - 1)
    return rms_norm(x, final_g)
```

```python
import contextlib
import math
import numpy as np
import ml_dtypes
import concourse.bass as bass
import concourse.mybir as mybir
from concourse.bass_utils import run_bass_kernel_spmd

F32 = mybir.dt.float32
BF16 = mybir.dt.bfloat16
AF = mybir.ActivationFunctionType
ALU = mybir.AluOpType
AX = mybir.AxisListType

D = 1024
B = 2
S = 8192
LC = 256
T = S + LC
NT = T // 128
DEPTH = 2
EPS = 1e-6
NDSEM = 8
GRID_W = 64
NEG = -30000.0
import os
POOLC = os.environ.get("K_POOLC", "pool")
POOLD = os.environ.get("K_POOLD", "pool")

OFF = dict(aq=0, ak=512, av=1024, az=1536, bq=2048, bk=2560, bv=3072, bo=3584, bz=4096, bg=4608,
           cq=4624, ck=5136, cv=5648, cz=6160, gm=6672)
D_IN = 9744

CHUNKS = [(0, 256)] + [(256 + i * 512, 512) for i in range(16)]


class TK:
    __slots__ = ("w", "r", "excl")

    def __init__(self, excl=False):
        self.w = None
        self.r = []
        self.excl = excl


class Prog:
    def __init__(self, nc):
        self.nc = nc
        self.q = {e: [] for e in ("pe", "act", "dve", "pool", "sp")}
        self.cnt = {e: 0 for e in self.q}
        self.dcnt = {e: 0 for e in ("sp", "act", "pool")}
        self.seen = {e: {} for e in self.q}
        self.dma_tokens = []
        self.pending = {e: [] for e in self.q}

    def barrier(self):
        toks = [(("e", e), c) for e, c in self.cnt.items() if c > 0]
        last = {}
        for k, v in self.dma_tokens:
            last[k] = max(last.get(k, 0), v)
        toks += list(last.items())
        for e in self.q:
            self.pending[e] = list(toks)

    def _waits(self, eng, reads, writes):
        w = {}
        seen = self.seen[eng]

        def need(tok):
            if tok is None:
                return
            k, v = tok
            if eng == "pe" and k == ("e", "pe"):
                return
            if seen.get(k, 0) >= v:
                return
            if w.get(k, 0) < v:
                w[k] = v
        me = ("e", eng)
        if self.pending[eng]:
            for tok in self.pending[eng]:
                if tok[0] != me:
                    need(tok)
            self.pending[eng] = []
        for t in reads:
            need(t.w)
            if t.excl:
                for r in t.r:
                    if r[0] != me:
                        need(r)
        for t in writes:
            need(t.w)
            for r in t.r:
                need(r)
        for k, v in w.items():
            seen[k] = v
        return w

    def _update(self, tok, reads, writes):
        for t in reads:
            if len(t.r) > 24:
                m = {}
                for k, v in t.r:
                    if m.get(k, 0) < v:
                        m[k] = v
                t.r = list(m.items())
            t.r.append(tok)
        for t in writes:
            t.w = tok
            t.r = []

    def op(self, eng, meth, reads, writes, *a, **kw):
        w = self._waits(eng, reads, writes)
        self.cnt[eng] += 1
        tok = (("e", eng), self.cnt[eng])
        self.q[eng].append((w, (meth, a, kw), None))
        self._update(tok, reads, writes)
        return tok

    def dma(self, queue, reads, writes, out=None, in_=None, fn=None):
        if fn is None:
            fn = ("dma_start", (), dict(out=out, in_=in_))
        i = self.dcnt[queue]
        self.dcnt[queue] += 1
        s = i % NDSEM
        prev = 16 * (i // NDSEM)
        w = self._waits(queue, reads, writes)
        key = ("d", queue, s)
        if prev > 0 and self.seen[queue].get(key, 0) < prev:
            w[key] = prev
            self.seen[queue][key] = prev
        tok = (key, prev + 16)
        self.q[queue].append((w, fn, key))
        self._update(tok, reads, writes)
        self.dma_tokens.append(tok)
        return tok

    def emit(self):
        nc = self.nc
        with contextlib.ExitStack() as es:
            sems = {}
            for e in self.q:
                sems[("e", e)] = es.enter_context(nc.semaphore("s_" + e))
            for qn in self.dcnt:
                for s in range(NDSEM):
                    sems[("d", qn, s)] = es.enter_context(nc.semaphore(f"d_{qn}_{s}"))
            fin = {}
            for k, v in self.dma_tokens:
                fin[k] = max(fin.get(k, 0), v)
            block = es.enter_context(nc.Block())

            def run(name, e):
                for w, fn, dkey in self.q[name]:
                    for k, v in w.items():
                        e.wait_ge(sems[k], v)
                    inst = getattr(e, fn[0])(*fn[1], **fn[2])
                    if dkey is None:
                        inst.then_inc(sems[("e", name)], 1)
                    else:
                        inst.then_inc(sems[dkey], 16)
                if name == "sp":
                    for k, v in fin.items():
                        e.wait_ge(sems[k], v)

            @block.tensor
            def _(e):
                run("pe", e)

            @block.scalar
            def _(e):
                run("act", e)

            @block.vector
            def _(e):
                run("dve", e)

            @block.gpsimd
            def _(e):
                run("pool", e)

            @block.sync
            def _(e):
                run("sp", e)


class Ctx:
    def __init__(self, nc, es):
        self.nc = nc
        self.es = es
        self.P = Prog(nc)
        self.n = 0

    def sb(self, shape, dt, name=None):
        self.n += 1
        return self.es.enter_context(self.nc.sbuf_tensor(f"sb{self.n}_{name or 'x'}", list(shape), dt))

    def ps(self, shape, dt, name=None):
        self.n += 1
        return self.es.enter_context(self.nc.psum_tensor(f"ps{self.n}_{name or 'x'}", list(shape), dt))

    def dram(self, name, shape, dt, kind):
        return self.nc.dram_tensor(name, list(shape), dt, kind=kind).ap()


def dma_rr(C):
    C._rr = getattr(C, "_rr", 0) + 1
    return "sp" if C._rr % 2 else POOLD


def alloc_common(C):
    G = {}
    G["ident"] = C.sb([128, 128], BF16, "ident_sb"); G["t_ident"] = TK()
    G["identf"] = C.sb([128, 128], F32, "identf")
    G["ps"] = [C.ps([128, 512], F32, f"psb{i}") for i in range(7)]; G["t_ps"] = [TK(True) for _ in range(7)]
    G["pst"] = C.ps([128, 1024], BF16, "pstr"); G["t_pst"] = TK(True)
    G["small"] = C.sb([128, 64], F32, "small"); G["t_small"] = TK()
    return G


def alloc_m1(C, G=None):
    R = dict(G) if G is not None else alloc_common(C)
    R["qT"] = C.sb([128, T], BF16, "qT"); R["t_qT"] = [TK() for _ in range(17)]
    R["kT"] = C.sb([128, T], BF16, "kT"); R["t_kT"] = [TK() for _ in range(17)]
    R["V"] = C.sb([128, NT, 132], BF16, "Vaug"); R["t_V"] = [TK() for _ in range(NT)]
    R["zg"] = C.sb([128, NT, 128], BF16, "zg"); R["t_zg"] = [TK() for _ in range(NT)]
    R["hbuf"] = [C.sb([128, 8, 512], BF16, f"hbuf{i}") for i in range(2)]; R["t_hbuf"] = [TK(), TK()]
    R["wst"] = [C.sb([128, 8, 128], F32, f"wst{i}") for i in range(2)]; R["t_wst"] = [TK(), TK()]
    R["W"] = C.sb([128, 8, 768], BF16, "Wbf"); R["t_W"] = TK()
    R["ystage"] = [C.sb([128, 512], BF16, f"ystage{i}") for i in range(2)]; R["t_ystage"] = [TK(), TK()]
    return R


def load_ident(C, R, ident_d):
    P = C.P
    P.dma("sp", [], [R["t_ident"]], R["identf"][:], ident_d[:, :])
    P.op("dve", "tensor_copy", [R["t_ident"]], [R["t_ident"]], out=R["ident"][:], in_=R["identf"][:])


def load_weights(C, R, w_d, ncols):
    P = C.P
    wv = w_d.rearrange("(k p) n -> p k n", p=128)
    i = 0
    for c0 in range(0, ncols, 128):
        cn = min(128, ncols - c0)
        st = R["wst"][i % 2]; tst = R["t_wst"][i % 2]
        P.dma(dma_rr(C), [], [tst], st[:, :, 0:cn], wv[:, :, c0:c0 + cn])
        P.op(POOLC, "tensor_copy", [tst], [R["t_W"]], out=R["W"][:, :, c0:c0 + cn], in_=st[:, :, 0:cn])
        i += 1


def project(C, R, hT_d, fm_groups, tm_cols, fm_cb, tm_cb, pre_chunk=None):
    P = C.P
    hv = hT_d.rearrange("(k p) t -> p k t", p=128)
    W = R["W"]
    for ci, (t0, n) in enumerate(CHUNKS):
        hb = R["hbuf"][ci % 2]; thb = R["t_hbuf"][ci % 2]
        P.dma("sp", [], [thb], hb[:, 0:4, 0:n], hv[:, 0:4, t0:t0 + n])
        P.dma(POOLD, [], [thb], hb[:, 4:8, 0:n], hv[:, 4:8, t0:t0 + n])
        if pre_chunk is not None:
            pre_chunk(ci, t0, n)
        for j, c0 in enumerate(fm_groups):
            pb = R["ps"][j]; tpb = R["t_ps"][j]
            for k in range(8):
                P.op("pe", "matmul", [R["t_W"], thb], [tpb], pb[:, 0:n], lhsT=W[:, k, c0:c0 + 128], rhs=hb[:, k, 0:n],
                     start=(k == 0), stop=(k == 7))
            fm_cb(j, ci, t0, n, pb, tpb)
        if tm_cols is not None:
            c0, nc_ = tm_cols
            for tt in range(n // 128):
                ti = t0 // 128 + tt
                bi = 4 + (ti % 2)
                pb = R["ps"][bi]; tpb = R["t_ps"][bi]
                for k in range(8):
                    P.op("pe", "matmul", [R["t_W"], thb], [tpb], pb[:, 0:nc_], lhsT=hb[:, k, tt * 128:(tt + 1) * 128],
                         rhs=W[:, k, c0:c0 + nc_], start=(k == 0), stop=(k == 7))
                tm_cb(ti, pb, tpb)


def emit_yT(C, R, src_tiles, t0, n, yT_d, row0):
    P = C.P
    C._ys = getattr(C, "_ys", 0) + 1
    ys = R["ystage"][C._ys % 2]; tys = R["t_ystage"][C._ys % 2]
    for j, (ap, tk) in enumerate(src_tiles):
        P.op("pe", "transpose", [tk, R["t_ident"]], [R["t_pst"]], R["pst"][:, j * 128:(j + 1) * 128], ap, R["ident"][:])
    P.op("dve", "tensor_copy", [R["t_pst"]], [tys], out=ys[:, 0:n], in_=R["pst"][:, 0:n])
    P.dma("sp", [tys], [], yT_d[row0:row0 + 128, t0:t0 + n], ys[:, 0:n])


def pass_A(C, R, l, hT_d, wA_d, cos_d, sin_d, lamq_d, lamk_d, dag_d, yT_d, yrow=0):
    P = C.P
    lam_init = 0.8 - 0.6 * math.exp(-0.3 * l)
    load_weights(C, R, wA_d, 768)
    sm = R["small"]; tsm = R["t_small"]
    lq = C.sb([128, 128], F32, "lq"); lk = C.sb([128, 128], F32, "lk"); dag = C.sb([128, 128], F32, "dag")
    t_l = TK(); t_dag = TK()
    P.dma("sp", [], [t_l], lq[:], lamq_d[:, :])
    P.dma("sp", [], [t_l], lk[:], lamk_d[:, :])
    P.dma("sp", [], [t_dag], dag[:], dag_d[:, :])
    P.op("dve", "tensor_tensor", [t_l], [t_l], out=lq[:], in0=lq[:], in1=lk[:], op=ALU.mult)
    P.op("dve", "reduce_sum", [t_l], [tsm], out=sm[:, 0:2], in_=lq[:].rearrange("p (c d) -> p c d", c=2), axis=AX.X)
    P.op("act", "activation", [tsm], [tsm], out=sm[:, 2:4], in_=sm[:, 0:2], func=AF.Exp)
    P.op("dve", "tensor_tensor", [tsm], [tsm], out=sm[:, 4:5], in0=sm[:, 3:4], in1=sm[:, 2:3], op=ALU.subtract)
    P.op("dve", "tensor_scalar_add", [tsm], [tsm], out=sm[:, 4:5], in0=sm[:, 4:5], scalar1=-lam_init)
    P.op("dve", "tensor_scalar_mul", [t_dag], [t_dag], out=dag[:], in0=dag[:], scalar1=(1.0 - lam_init))
    P.op(POOLC, "memset", [], R["t_V"], R["V"][:, :, 128:129], 1.0)

    cosb = [C.sb([128, 512], F32, f"cosb{i}") for i in range(2)]
    sinb = [C.sb([128, 512], F32, f"sinb{i}") for i in range(2)]
    t_cs = [TK(), TK()]
    tmp1 = C.sb([128, 512], F32, "ropet1"); tmp2 = C.sb([128, 512], F32, "ropet2"); t_tmp = TK()
    ztmp = C.sb([128, 128], F32, "ztmp"); t_ztmp = TK()

    def pre_chunk(ci, t0, n):
        P.dma("sp", [], [t_cs[ci % 2]], cosb[ci % 2][:, 0:n], cos_d[:, t0:t0 + n])
        P.dma(POOLD, [], [t_cs[ci % 2]], sinb[ci % 2][:, 0:n], sin_d[:, t0:t0 + n])

    def fm_cb(j, ci, t0, n, pb, tpb):
        if j % 2 == 0:
            P.op("dve", "tensor_tensor", [tpb, t_cs[ci % 2]], [t_tmp], out=tmp1[:, 0:n], in0=pb[:, 0:n], in1=cosb[ci % 2][:, 0:n], op=ALU.mult)
        else:
            dst = R["qT"] if j == 1 else R["kT"]
            tdst = (R["t_qT"] if j == 1 else R["t_kT"])[ci]
            P.op("dve", "tensor_tensor", [tpb, t_cs[ci % 2]], [t_tmp], out=tmp2[:, 0:n], in0=pb[:, 0:n], in1=sinb[ci % 2][:, 0:n], op=ALU.mult)
            P.op("dve", "tensor_tensor", [t_tmp], [tdst], out=dst[:, t0:t0 + n], in0=tmp1[:, 0:n], in1=tmp2[:, 0:n], op=ALU.add)

    def tm_cb(ti, pb, tpb):
        P.op("act", "copy", [tpb], [R["t_V"][ti]], out=R["V"][:, ti, 0:128], in_=pb[:, 0:128])
        P.op("act", "activation", [tpb], [t_ztmp], out=ztmp[:], in_=pb[:, 128:256], func=AF.Silu)
        P.op("dve", "tensor_tensor", [t_ztmp, t_dag], [R["t_zg"][ti]], out=R["zg"][:, ti, :], in0=ztmp[:], in1=dag[:], op=ALU.mult)

    project(C, R, hT_d, [0, 128, 256, 384], (512, 256), fm_cb, tm_cb, pre_chunk)

    pbuf = [C.sb([128, 512], BF16, f"pbuf{i}") for i in range(3)]; t_pbuf = [TK() for _ in range(3)]
    osb = C.sb([128, 128], F32, "osb"); t_osb = TK()
    ysb = [C.sb([128, 128], BF16, f"ysbA{i}") for i in range(4)]; t_ysb = [TK() for _ in range(4)]
    junk = C.sb([128, 128], F32, "junkA"); t_junk = TK()
    t_acc = [R["t_ps"][3 + a // 3] for a in range(8)]

    def acc_ap(a, w=129):
        bank = 3 + a // 3
        o = (a % 3) * 132
        return R["ps"][bank][:, o:o + w]

    for ci, (t0, n) in enumerate(CHUNKS):
        nj = n // 128
        kts = [0, 1] if ci == 0 else list(range(NT))
        units = [(kt, c) for kt in kts for c in range(2)]
        nu = len(units)
        started = set()

        def s_mm(u):
            kt, c = units[u]
            kci = 0 if kt < 2 else 1 + (kt - 2) // 4
            P.op("pe", "matmul", [R["t_kT"][kci], R["t_qT"][ci]], [R["t_ps"][u % 3]], R["ps"][u % 3][:, 0:n],
                 lhsT=R["kT"][c * 64:(c + 1) * 64, kt * 128:(kt + 1) * 128],
                 rhs=R["qT"][c * 64:(c + 1) * 64, t0:t0 + n], start=True, stop=True)

        for u in range(min(2, nu)):
            s_mm(u)
        for u in range(nu):
            kt, c = units[u]
            if u + 2 < nu:
                s_mm(u + 2)
            pb = R["ps"][u % 3]; pbf = pbuf[u % 3]
            P.op("act", "activation", [R["t_ps"][u % 3]], [t_pbuf[u % 3]], out=pbf[:, 0:n], in_=pb[:, 0:n], func=AF.Exp, scale=0.125)
            for j in range(nj):
                a = c * 4 + j
                bank = 3 + a // 3
                first = (kt == kts[0]) and (bank not in started)
                started.add(bank)
                P.op("pe", "matmul", [t_pbuf[u % 3], R["t_V"][kt]], [t_acc[a]], acc_ap(a), lhsT=pbf[:, j * 128:(j + 1) * 128],
                     rhs=R["V"][:, kt, 0:129], start=first, stop=(kt == kts[-1]), skip_group_check=True)
        tiles = []
        for j in range(nj):
            ti = t0 // 128 + j
            a0, a1 = j, 4 + j
            P.op("dve", "reciprocal", [t_acc[a0]], [tsm], out=sm[:, 8:9], in_=acc_ap(a0)[:, 128:129])
            P.op("dve", "reciprocal", [t_acc[a1]], [tsm], out=sm[:, 9:10], in_=acc_ap(a1)[:, 128:129])
            P.op("dve", "tensor_tensor", [tsm], [tsm], out=sm[:, 9:10], in0=sm[:, 9:10], in1=sm[:, 4:5], op=ALU.mult)
            P.op("dve", "tensor_scalar", [t_acc[a0], tsm], [t_osb], out=osb[:], in0=acc_ap(a0, 128), scalar1=sm[:, 8:9], scalar2=None, op0=ALU.mult)
            P.op("dve", "scalar_tensor_tensor", [t_acc[a1], tsm, t_osb], [t_osb], out=osb[:], in0=acc_ap(a1, 128), scalar=sm[:, 9:10],
                 in1=osb[:], op0=ALU.mult, op1=ALU.add)
            P.op("act", "activation", [t_osb], [t_junk, tsm], out=junk[:], in_=osb[:], func=AF.Square, accum_out=sm[:, 10:11])
            P.op("act", "activation", [tsm], [tsm], out=sm[:, 11:12], in_=sm[:, 10:11], func=AF.Ln, scale=1.0 / 128, bias=EPS)
            P.op("act", "activation", [tsm], [tsm], out=sm[:, 11:12], in_=sm[:, 11:12], func=AF.Exp, scale=-0.5)
            P.op("dve", "scalar_tensor_tensor", [t_osb, tsm, R["t_zg"][ti]], [t_ysb[j]], out=ysb[j][:], in0=osb[:], scalar=sm[:, 11:12],
                 in1=R["zg"][:, ti, :], op0=ALU.mult, op1=ALU.mult)
            tiles.append((ysb[j][:], t_ysb[j]))
        emit_yT(C, R, tiles, t0, n, yT_d, yrow)


def rms_scale(P, sm, tsm, ssq_col, out_col, n_feat):
    P.op("act", "activation", [tsm], [tsm], out=sm[:, out_col:out_col + 1], in_=sm[:, ssq_col:ssq_col + 1], func=AF.Ln, scale=1.0 / n_feat, bias=EPS)
    P.op("act", "activation", [tsm], [tsm], out=sm[:, out_col:out_col + 1], in_=sm[:, out_col:out_col + 1], func=AF.Exp, scale=-0.5)


def pass_B(C, R, l, hT_d, wB_d, convp_d, gateb_d, mlg_d, tri_d, yT_d, yrow=128):
    P = C.P
    sm = R["small"]; tsm = R["t_small"]
    load_weights(C, R, wB_d, 644)
    convp = C.sb([128, 8], F32, "convp"); gateb = C.sb([128, 4], F32, "gateb"); mlg = C.sb([128, 128], F32, "mlg")
    tri = C.sb([128, 384], F32, "tri")
    t_cst = TK()
    P.dma("sp", [], [t_cst], convp[:], convp_d[:, :])
    P.dma("sp", [], [t_cst], gateb[:], gateb_d[:, :])
    P.dma("sp", [], [t_cst], mlg[:], mlg_d[:, :])
    P.dma("sp", [], [t_cst], tri[:], tri_d[:, :])
    P.op(POOLC, "memset", [], R["t_V"], R["V"][:, :, 128:129], 1.0)
    Ktok = C.sb([128, NT, 128], BF16, "Ktok"); t_Ktok = [TK() for _ in range(NT)]
    hs = C.sb([128, NT, 128], F32, "hs"); t_hs = [TK() for _ in range(NT)]
    g4 = C.sb([128, 4, NT], F32, "g4"); t_g4 = TK()
    Rb = [C.sb([128, 520], F32, f"Rb{i}") for i in range(2)]; t_Rb = [TK(), TK()]
    ctmp = C.sb([128, 516], F32, "ctmp"); t_ctmp = TK()
    ctmp2 = C.sb([128, 516], F32, "ctmp2"); t_ctmp2 = TK()
    zt1 = C.sb([128, 128], F32, "zt1"); zt2 = C.sb([128, 128], F32, "zt2"); t_zt = TK()
    for i in range(2):
        P.op(POOLC, "memset", [], [t_Rb[i]], Rb[i][:], 0.0)

    def fm_cb(j, ci, t0, n, pb, tpb):
        rb = Rb[j]; trb = t_Rb[j]
        first = ci in (0, 1)
        last = ci in (0, 16)
        if first:
            P.op(POOLC, "memset", [], [trb], rb[:, 0:2], 0.0)
        if last:
            P.op(POOLC, "memset", [], [trb], rb[:, n + 2:n + 3], 0.0)
        P.op("act", "copy", [tpb], [trb], out=rb[:, 2:2 + n], in_=pb[:, 0:n])
        ta = t0 - 1 + (1 if first else 0)
        tb = t0 + n - 1 + (1 if last else 0)
        m = tb - ta
        o = ta - t0
        wc = 4 * j
        P.op("dve", "tensor_scalar", [trb, t_cst], [t_ctmp], out=ctmp[:, 0:m], in0=rb[:, o + 1:o + 1 + m], scalar1=convp[:, wc:wc + 1], scalar2=None, op0=ALU.mult)
        P.op("dve", "scalar_tensor_tensor", [trb, t_cst, t_ctmp], [t_ctmp], out=ctmp[:, 0:m], in0=rb[:, o + 2:o + 2 + m], scalar=convp[:, wc + 1:wc + 2],
             in1=ctmp[:, 0:m], op0=ALU.mult, op1=ALU.add)
        P.op("dve", "scalar_tensor_tensor", [trb, t_cst, t_ctmp], [t_ctmp], out=ctmp[:, 0:m], in0=rb[:, o + 3:o + 3 + m], scalar=convp[:, wc + 2:wc + 3],
             in1=ctmp[:, 0:m], op0=ALU.mult, op1=ALU.add)
        dst = R["qT"] if j == 0 else R["kT"]
        tds = R["t_qT"] if j == 0 else R["t_kT"]
        wr = [tds[ci]] + ([tds[ci - 1]] if not first else [])
        if j == 0:
            P.op("act", "activation", [t_ctmp, t_cst], [t_ctmp2], out=ctmp2[:, 0:m], in_=ctmp[:, 0:m], func=AF.Silu, bias=convp[:, 3:4])
            P.op("dve", "tensor_scalar_mul", [t_ctmp2], wr, out=dst[:, ta:tb], in0=ctmp2[:, 0:m], scalar1=128.0 ** -0.5)
        else:
            P.op("act", "activation", [t_ctmp, t_cst], wr, out=dst[:, ta:tb], in_=ctmp[:, 0:m], func=AF.Silu, bias=convp[:, 7:8])
        if not last:
            P.op(POOLC, "tensor_copy", [trb], [trb], out=rb[:, 0:2], in_=rb[:, n:n + 2])

    def tm_cb(ti, pb, tpb):
        P.op("act", "copy", [tpb], [R["t_V"][ti]], out=R["V"][:, ti, 0:128], in_=pb[:, 0:128])
        P.op("act", "activation", [tpb], [t_zt], out=zt1[:], in_=pb[:, 128:256], func=AF.Sigmoid)
        P.op("act", "activation", [tpb], [t_zt], out=zt2[:], in_=pb[:, 256:384], func=AF.Silu)
        P.op("dve", "tensor_tensor", [tpb, t_cst], [t_g4], out=g4[:, :, ti], in0=pb[:, 384:388], in1=gateb[:], op=ALU.add)
        P.op("dve", "tensor_tensor", [t_zt], [t_zt], out=zt1[:], in0=zt1[:], in1=zt2[:], op=ALU.mult)
        P.op("dve", "tensor_tensor", [t_zt, t_cst], [R["t_zg"][ti]], out=R["zg"][:, ti, :], in0=zt1[:], in1=mlg[:], op=ALU.mult)

    project(C, R, hT_d, [0, 128], (256, 388), fm_cb, tm_cb)

    for t0 in range(0, NT, 8):
        nn = min(8, NT - t0)
        kci = lambda ti: 0 if ti < 2 else 1 + (ti - 2) // 4
        for j in range(nn):
            ti = t0 + j
            P.op("pe", "transpose", [R["t_kT"][kci(ti)], R["t_ident"]], [R["t_pst"]], R["pst"][:, j * 128:(j + 1) * 128],
                 R["kT"][:, ti * 128:(ti + 1) * 128], R["ident"][:])
        P.op("dve", "tensor_copy", [R["t_pst"]], [t_Ktok[t0 + j] for j in range(nn)], out=Ktok[:, t0:t0 + nn, :],
             in_=R["pst"][:, 0:nn * 128].rearrange("p (a b) -> p a b", b=128))

    LFP = C.sb([128, 2, NT], F32, "LFP"); t_LFP = TK()
    GA = C.sb([128, 2, NT], F32, "GA"); GU = C.sb([128, 2, NT], F32, "GU"); GL = C.sb([128, 2, NT], F32, "GL"); t_G = TK()
    GAn = C.sb([128, 2, NT], F32, "GAn")
    for d in range(2):
        P.op("act", "activation", [t_g4], [t_LFP], out=LFP[:, d, :], in_=g4[:, 2 * d + 1, :], func=AF.Exp, scale=-1.0)
    for d in range(2):
        P.op("act", "activation", [t_LFP], [t_LFP], out=LFP[:, d, :], in_=LFP[:, d, :], func=AF.Ln, bias=1.0)
    pg = R["ps"][6]; tpg = R["t_ps"][6]
    for d in range(2):
        P.op("pe", "matmul", [t_LFP, t_cst], [tpg], pg[:, d * NT:(d + 1) * NT], lhsT=tri[:, d * 128:(d + 1) * 128], rhs=LFP[:, d, :], start=True, stop=True,
             skip_group_check=True)
        P.op("pe", "matmul", [t_LFP, t_cst], [tpg], pg[:, (2 + d) * NT:(3 + d) * NT], lhsT=tri[:, 256:384], rhs=LFP[:, d, :], start=True, stop=True,
             skip_group_check=True)
    for d in range(2):
        P.op("act", "activation", [tpg], [t_G], out=GA[:, d, :], in_=pg[:, d * NT:(d + 1) * NT], func=AF.Exp, scale=-1.0)
        P.op("dve", "tensor_tensor", [tpg, t_g4], [t_G], out=GU[:, d, :], in0=pg[:, d * NT:(d + 1) * NT], in1=g4[:, 2 * d, :], op=ALU.add)
        P.op("act", "activation", [t_G], [t_G], out=GU[:, d, :], in_=GU[:, d, :], func=AF.Exp)
        P.op("act", "activation", [tpg], [t_G], out=GL[:, d, :], in_=pg[:, (2 + d) * NT:(3 + d) * NT], func=AF.Exp, scale=-1.0)
        P.op("dve", "tensor_scalar_mul", [t_G], [t_G], out=GAn[:, d, :], in0=GA[:, d, :], scalar1=-1.0)

    Cst = [C.sb([128, 132], F32, f"Cst{d}") for d in range(2)]; t_Cst = [TK(), TK()]
    Cb = [C.sb([128, 132], BF16, f"Cb{d}") for d in range(2)]; t_Cb = [TK(), TK()]
    Vp = [[C.sb([128, 132], BF16, f"Vp{d}{i}") for i in range(2)] for d in range(2)]; t_Vp = [[TK(), TK()], [TK(), TK()]]
    Sm = [[C.sb([128, 128], BF16, f"Sm{d}{i}") for i in range(2)] for d in range(2)]; t_Sm = [[TK(), TK()], [TK(), TK()]]
    for d in range(2):
        P.op(POOLC, "memset", [], [t_Cst[d]], Cst[d][:], 0.0)
        P.op(POOLC, "memset", [], [t_Cb[d]], Cb[d][:], 0.0)
    order = [list(range(NT)), [1, 0] + list(range(NT - 1, 1, -1))]
    inited = set()
    kci = lambda ti: 0 if ti < 2 else 1 + (ti - 2) // 4
    for step in range(NT):
        for d in range(2):
            ti = order[d][step]
            rr = step % 2
            vp = Vp[d][rr]; tvp = t_Vp[d][rr]; smb = Sm[d][rr]; tsmb = t_Sm[d][rr]
            psS = R["ps"][0 + d]; tpsS = R["t_ps"][0 + d]
            psA = R["ps"][2 + d]; tpsA = R["t_ps"][2 + d]
            psU = R["ps"][4 + d]; tpsU = R["t_ps"][4 + d]
            P.op("dve", "tensor_scalar", [R["t_V"][ti], t_G], [tvp], out=vp[:, 0:129], in0=R["V"][:, ti, 0:129], scalar1=GU[:, d, ti:ti + 1], scalar2=None, op0=ALU.mult)
            P.op("pe", "matmul", [R["t_kT"][kci(ti)], R["t_qT"][kci(ti)]], [tpsS], psS[:, 0:128], lhsT=R["kT"][:, ti * 128:(ti + 1) * 128],
                 rhs=R["qT"][:, ti * 128:(ti + 1) * 128], start=True, stop=True)
            P.op("dve", "tensor_tensor", [tpsS, t_cst], [tsmb], out=smb[:], in0=psS[:, 0:128], in1=tri[:, d * 128:(d + 1) * 128], op=ALU.mult)
            P.op("pe", "matmul", [tsmb, tvp], [tpsA], psA[:, 0:129], lhsT=smb[:], rhs=vp[:, 0:129], start=True, stop=False)
            P.op("pe", "matmul", [R["t_qT"][kci(ti)], t_Cb[d]], [tpsA], psA[:, 0:129], lhsT=R["qT"][:, ti * 128:(ti + 1) * 128], rhs=Cb[d][:, 0:129],
                 start=False, stop=True)
            P.op("pe", "matmul", [t_Ktok[ti], tvp], [tpsU], psU[:, 0:129], lhsT=Ktok[:, ti, :], rhs=vp[:, 0:129], start=True, stop=True)
            c0 = 20 + 4 * d
            P.op("dve", "tensor_scalar", [tpsA, t_G], [tsm], out=sm[:, c0:c0 + 1], in0=psA[:, 128:129], scalar1=GA[:, d, ti:ti + 1], scalar2=1.0,
                 op0=ALU.mult, op1=ALU.max)
            P.op("dve", "tensor_scalar", [tpsA, t_G], [tsm], out=sm[:, c0 + 2:c0 + 3], in0=psA[:, 128:129], scalar1=GAn[:, d, ti:ti + 1], scalar2=1.0,
                 op0=ALU.mult, op1=ALU.max)
            P.op("dve", "tensor_tensor", [tsm], [tsm], out=sm[:, c0:c0 + 1], in0=sm[:, c0:c0 + 1], in1=sm[:, c0 + 2:c0 + 3], op=ALU.max)
            P.op("dve", "reciprocal", [tsm], [tsm], out=sm[:, c0 + 1:c0 + 2], in_=sm[:, c0:c0 + 1])
            P.op("dve", "tensor_tensor", [tsm, t_G], [tsm], out=sm[:, c0 + 1:c0 + 2], in0=sm[:, c0 + 1:c0 + 2], in1=GA[:, d, ti:ti + 1], op=ALU.mult)
            if ti not in inited:
                inited.add(ti)
                P.op("dve", "tensor_scalar", [tpsA, tsm], [t_hs[ti]], out=hs[:, ti, :], in0=psA[:, 0:128], scalar1=sm[:, c0 + 1:c0 + 2], scalar2=None, op0=ALU.mult)
            else:
                P.op("dve", "scalar_tensor_tensor", [tpsA, tsm, t_hs[ti]], [t_hs[ti]], out=hs[:, ti, :], in0=psA[:, 0:128], scalar=sm[:, c0 + 1:c0 + 2],
                     in1=hs[:, ti, :], op0=ALU.mult, op1=ALU.add)
            P.op("dve", "tensor_scalar", [t_Cst[d], t_G], [t_Cst[d]], out=Cst[d][:, 0:129], in0=Cst[d][:, 0:129], scalar1=GL[:, d, ti:ti + 1], scalar2=None, op0=ALU.mult)
            P.op("dve", "scalar_tensor_tensor", [tpsU, t_G, t_Cst[d]], [t_Cst[d]], out=Cst[d][:, 0:129], in0=psU[:, 0:129], scalar=GL[:, d, ti:ti + 1],
                 in1=Cst[d][:, 0:129], op0=ALU.mult, op1=ALU.add)
            P.op("act", "copy", [t_Cst[d]], [t_Cb[d]], out=Cb[d][:, 0:129], in_=Cst[d][:, 0:129])

    ysb = [C.sb([128, 128], BF16, f"ysbB{i}") for i in range(4)]; t_ysb = [TK() for _ in range(4)]
    junk = C.sb([128, 128], F32, "junkB"); t_junk = TK()
    for ci, (t0, n) in enumerate(CHUNKS):
        tiles = []
        for j in range(n // 128):
            ti = t0 // 128 + j
            P.op("act", "activation", [t_hs[ti]], [t_junk, tsm], out=junk[:], in_=hs[:, ti, :], func=AF.Square, accum_out=sm[:, 30:31])
            rms_scale(P, sm, tsm, 30, 31, 128)
            P.op("dve", "scalar_tensor_tensor", [t_hs[ti], tsm, R["t_zg"][ti]], [t_ysb[j]], out=ysb[j][:], in0=hs[:, ti, :], scalar=sm[:, 31:32],
                 in1=R["zg"][:, ti, :], op0=ALU.mult, op1=ALU.mult)
            tiles.append((ysb[j][:], t_ysb[j]))
        emit_yT(C, R, tiles, t0, n, yT_d, yrow)


def pass_C(C, R, l, hT_d, wC_d, biasg_d, maskc_d, yT_d, yrow=256):
    P = C.P
    sm = R["small"]; tsm = R["t_small"]
    load_weights(C, R, wC_d, 512)
    bias = C.sb([128, 6400], F32, "biasC"); t_bias = TK()
    mtmp = C.sb([128, 640], F32, "mtmpC"); t_mtmp = TK()
    P.dma("sp", [], [t_bias], bias[:, 0:3200], biasg_d[:, 0:3200])
    P.dma(POOLD, [], [t_bias], bias[:, 3200:6400], biasg_d[:, 3200:6400])
    for i in range(10):
        P.dma("sp", [], [t_mtmp], mtmp[:], maskc_d[:, i * 640:(i + 1) * 640])
        P.op("dve", "tensor_tensor", [t_mtmp, t_bias], [t_bias], out=bias[:, i * 640:(i + 1) * 640], in0=bias[:, i * 640:(i + 1) * 640], in1=mtmp[:], op=ALU.add)
    P.op(POOLC, "memset", [], R["t_V"], R["V"][:, :, 64:65], 1.0)
    P.op(POOLC, "memset", [], R["t_V"], R["V"][:, :, 129:130], 1.0)

    def fm_cb(j, ci, t0, n, pb, tpb):
        dst = R["qT"] if j == 0 else R["kT"]
        tds = R["t_qT"] if j == 0 else R["t_kT"]
        P.op("act", "copy", [tpb], [tds[ci]], out=dst[:, t0:t0 + n], in_=pb[:, 0:n])

    def tm_cb(ti, pb, tpb):
        P.op("dve", "tensor_copy", [tpb], [R["t_V"][ti]], out=R["V"][:, ti, 0:64], in_=pb[:, 0:64])
        P.op("dve", "tensor_copy", [tpb], [R["t_V"][ti]], out=R["V"][:, ti, 65:129], in_=pb[:, 64:128])
        P.op("act", "activation", [tpb], [R["t_zg"][ti]], out=R["zg"][:, ti, :], in_=pb[:, 128:256], func=AF.Silu)

    project(C, R, hT_d, [0, 128], (256, 256), fm_cb, tm_cb)

    tmpS = [C.sb([128, 640], F32, f"tmpS{i}") for i in range(2)]; t_tmpS = [TK(), TK()]
    Pl = [C.sb([128, 896], BF16, f"PlC{i}") for i in range(2)]; t_Pl = [TK(), TK()]
    ysb = [C.sb([128, 128], BF16, f"ysbC{i}") for i in range(4)]; t_ysb = [TK() for _ in range(4)]
    kci = lambda ti: 0 if ti < 2 else 1 + (ti - 2) // 4
    u = 0
    for ci, (t0, n) in enumerate(CHUNKS):
        tiles = []
        for jj in range(n // 128):
            ti = t0 // 128 + jj
            for hh in range(2):
                hs_ = slice(hh * 64, (hh + 1) * 64)
                bx = R["ps"][2 * (u % 2)]; tbx = R["t_ps"][2 * (u % 2)]
                by = R["ps"][2 * (u % 2) + 1]; tby = R["t_ps"][2 * (u % 2) + 1]
                pa = R["ps"][4 + u % 3]; tpa = R["t_ps"][4 + u % 3]
                pl = Pl[u % 2]; tpl = t_Pl[u % 2]
                ts_ = tmpS[u % 2]; tts = t_tmpS[u % 2]
                qsl = R["qT"][hs_, ti * 128:(ti + 1) * 128]
                if ti < 2:
                    keyt = []
                else:
                    j = ti - 2
                    kt0 = min(max(j - 2, 0), 59)
                    pat = 0 if j == 0 else 1 if j == 1 else 3 if j == 62 else 4 if j == 63 else 2
                    keyt = [2 + kt0 + a for a in range(5)]
                for a, kt in enumerate(keyt):
                    dst = bx[:, a * 128:(a + 1) * 128] if a < 4 else by[:, 0:128]
                    P.op("pe", "matmul", [R["t_kT"][kci(kt)], R["t_qT"][kci(ti)]], [tbx if a < 4 else tby], dst,
                         lhsT=R["kT"][hs_, kt * 128:(kt + 1) * 128], rhs=qsl, start=True, stop=True, skip_group_check=True)
                for a in range(2):
                    P.op("pe", "matmul", [R["t_kT"][0], R["t_qT"][kci(ti)]], [tby], by[:, (1 + a) * 128:(2 + a) * 128],
                         lhsT=R["kT"][hs_, a * 128:(a + 1) * 128], rhs=qsl, start=True, stop=True, skip_group_check=True)
                if keyt:
                    bo = (pat * 2 + hh) * 640
                    P.op("dve", "scalar_tensor_tensor", [tbx, t_bias], [tts], out=ts_[:, 0:512], in0=bx[:, 0:512], scalar=0.125, in1=bias[:, bo:bo + 512],
                         op0=ALU.mult, op1=ALU.add)
                    P.op("dve", "scalar_tensor_tensor", [tby, t_bias], [tts], out=ts_[:, 512:640], in0=by[:, 0:128], scalar=0.125, in1=bias[:, bo + 512:bo + 640],
                         op0=ALU.mult, op1=ALU.add)
                    P.op("act", "activation", [tts], [tpl], out=pl[:, 0:640], in_=ts_[:, 0:640], func=AF.Exp)
                P.op("act", "activation", [tby], [tpl], out=pl[:, 640:896], in_=by[:, 128:384], func=AF.Exp, scale=0.125)
                vs = slice(hh * 65, (hh + 1) * 65)
                allk = [(pl[:, a * 128:(a + 1) * 128], kt) for a, kt in enumerate(keyt)] + [(pl[:, 640 + a * 128:640 + (a + 1) * 128], a) for a in range(2)]
                for i, (pap, kt) in enumerate(allk):
                    P.op("pe", "matmul", [tpl, R["t_V"][kt]], [tpa], pa[:, 0:65], lhsT=pap, rhs=R["V"][:, kt, vs], start=(i == 0), stop=(i == len(allk) - 1))
                P.op("dve", "reciprocal", [tpa], [tsm], out=sm[:, 40:41], in_=pa[:, 64:65])
                P.op("dve", "scalar_tensor_tensor", [tpa, tsm, R["t_zg"][ti]], [t_ysb[jj]], out=ysb[jj][:, hs_], in0=pa[:, 0:64], scalar=sm[:, 40:41],
                     in1=R["zg"][:, ti, hs_], op0=ALU.mult, op1=ALU.mult)
                u += 1
            tiles.append((ysb[jj][:], t_ysb[jj]))
        emit_yT(C, R, tiles, t0, n, yT_d, yrow)


NTOK2 = 2304
CHUNKS2 = [(0, 256)] + [(256 + i * 512, 512) for i in range(4)]


def alloc_m2(C, G=None):
    M = dict(G) if G is not None else alloc_common(C)
    M["wst"] = [C.sb([128, 8, 128], F32, f"wst{i}") for i in range(2)]; M["t_wst"] = [TK(), TK()]
    M["xt"] = [C.sb([128, 1024], F32, f"xt{i}") for i in range(2)]; M["t_xt"] = [TK(), TK()]
    M["htmp"] = C.sb([128, 1024], F32, "htmp"); M["t_htmp"] = TK()
    M["hbf"] = C.sb([128, 1024], BF16, "hbf"); M["t_hbf"] = TK()
    M["hst"] = [C.sb([128, 8, 128], BF16, f"hst{i}") for i in range(2)]; M["t_hst"] = [TK(), TK()]
    M["junk"] = C.sb([128, 1024], F32, "junk"); M["t_junk"] = TK()
    M["mod"] = {}
    M["nrm"] = 0
    return M


def alloc_mod_out(C, M, key, parts):
    for part in parts:
        for v in range(2):
            M["mod"][(key, v, part)] = (C.sb([128, 1024], F32, f"mod_{key}_{v}_{part}"), TK())


def alloc_mod_tmp(C, M):
    M["wmst"] = C.sb([128, 8, 512], F32, "wmst"); M["t_wmst"] = TK()
    M["bst"] = C.sb([128, 512], F32, "bst"); M["t_bst"] = TK()
    M["screp"] = [C.sb([128, 8, 128], F32, f"screp{v}") for v in range(2)]; M["t_screp"] = TK()


def mod_vectors(C, M, key, crepL_d, crepC_d, wmod_d, bmodrep_d, parts, normgrep_d=None):
    P = C.P
    if "sc_done" not in M:
        M["sc_done"] = True
        for v, cd in enumerate((crepL_d, crepC_d)):
            P.dma("sp", [], [M["t_screp"]], M["screp"][v][:], cd[:, :, :])
            P.op("act", "activation", [M["t_screp"]], [M["t_screp"]], out=M["screp"][v][:], in_=M["screp"][v][:], func=AF.Silu)
    wv = wmod_d.rearrange("(k p) n -> p k n", p=128)
    pm = M["ps"][6]; tpm = M["t_ps"][6]
    if normgrep_d is not None:
        ng = C.sb([128, 1024], F32, "normg"); t_ng = TK()
        P.dma("sp", [], [t_ng], ng[:], normgrep_d[:, :])
    for part in parts:
        for half in range(2):
            c0 = part * 1024 + half * 512
            P.dma("sp", [], [M["t_wmst"]], M["wmst"][:, 0:4, :], wv[:, 0:4, c0:c0 + 512])
            P.dma(POOLD, [], [M["t_wmst"]], M["wmst"][:, 4:8, :], wv[:, 4:8, c0:c0 + 512])
            P.dma("sp", [], [M["t_bst"]], M["bst"][:], bmodrep_d[:, c0:c0 + 512])
            for v in range(2):
                dst, tdst = M["mod"][(key, v, part)]
                for k in range(8):
                    P.op("pe", "matmul", [M["t_screp"], M["t_wmst"]], [tpm], pm[:, :], lhsT=M["screp"][v][:, k, :], rhs=M["wmst"][:, k, :],
                         start=(k == 0), stop=(k == 7))
                P.op("dve", "tensor_tensor", [tpm, M["t_bst"]], [tdst], out=dst[:, half * 512:(half + 1) * 512], in0=pm[:, :], in1=M["bst"][:], op=ALU.add)
        if part == 1:
            for v in range(2):
                dst, tdst = M["mod"][(key, v, part)]
                P.op("dve", "scalar_tensor_tensor", [tdst, t_ng], [tdst], out=dst[:], in0=dst[:], scalar=1.0, in1=ng[:], op0=ALU.add, op1=ALU.mult)


def norm_tile(C, M, xt, t_xt, scale_bc, shift_bc, hT_view, i):
    P = C.P
    sm = M["small"]; tsm = M["t_small"]
    P.op("act", "activation", [t_xt], [M["t_junk"], tsm], out=M["junk"][:], in_=xt[:], func=AF.Square, accum_out=sm[:, 0:1])
    rms_scale(P, sm, tsm, 0, 1, 1024)
    P.op("dve", "scalar_tensor_tensor", [t_xt, tsm, scale_bc[1]], [M["t_htmp"]], out=M["htmp"][:], in0=xt[:], scalar=sm[:, 1:2], in1=scale_bc[0][:],
         op0=ALU.mult, op1=ALU.mult)
    P.op(POOLC, "tensor_tensor", [M["t_htmp"], shift_bc[1]], [M["t_hbf"]], out=M["hbf"][:], in0=M["htmp"][:], in1=shift_bc[0][:], op=ALU.add)
    M["nrm"] += 1
    hst = M["hst"][M["nrm"] % 2]; thst = M["t_hst"][M["nrm"] % 2]
    for k in range(8):
        P.op("pe", "transpose", [M["t_hbf"], M["t_ident"]], [M["t_pst"]], M["pst"][:, k * 128:(k + 1) * 128], M["hbf"][:, k * 128:(k + 1) * 128], M["ident"][:])
    P.op("act", "copy", [M["t_pst"]], [thst], out=hst[:], in_=M["pst"][:, :].rearrange("p (k t) -> p k t", t=128))
    P.dma("sp", [thst], [], hT_view[:, :, i * 128:(i + 1) * 128], hst[:])


def stage_N(C, M, key, x_d, hT_d, ntok=NTOK2):
    P = C.P
    hv = hT_d.rearrange("(k p) t -> p k t", p=128)
    for i in range(ntok // 128):
        v = 1 if i < 2 else 0
        xt = M["xt"][i % 2]; t_xt = M["t_xt"][i % 2]
        P.dma(POOLD, [], [t_xt], xt[:], x_d[i * 128:(i + 1) * 128, :])
        norm_tile(C, M, xt, t_xt, M["mod"][(key, v, 1)], M["mod"][(key, v, 0)], hv, i)


def load_w_bf16(C, M, w_d, dst, tdst, nk, ncols):
    P = C.P
    wv = w_d.rearrange("(k p) n -> p k n", p=128)
    i = 0
    for k0 in range(0, nk, 8):
        kn = min(8, nk - k0)
        for c0 in range(0, ncols, 128):
            st = M["wst"][i % 2]; tst = M["t_wst"][i % 2]
            P.dma(dma_rr(C), [], [tst], st[:, 0:kn, :], wv[:, k0:k0 + kn, c0:c0 + 128])
            P.op(POOLC, "tensor_copy", [tst], [tdst], out=dst[:, k0:k0 + kn, c0:c0 + 128], in_=st[:, 0:kn, :])
            i += 1


def stage_M2(C, M, key, x_d, hT_d, yT_d, wgm_d, wbr_d, wout_d, xout_d, next_key=None, hTn_d=None, finalg_d=None, out_d=None, chunks=None):
    P = C.P
    Wgm = C.sb([128, 8, 3072], BF16, "Wgm"); t_Wgm = TK()
    Wbr = C.sb([128, 12, 1024], BF16, "Wbr"); t_Wbr = TK()
    Wout = C.sb([128, 8, 1024], BF16, "Wout"); t_Wout = TK()
    load_w_bf16(C, M, wgm_d, Wgm, t_Wgm, 8, 3072)
    load_w_bf16(C, M, wbr_d, Wbr, t_Wbr, 12, 1024)
    load_w_bf16(C, M, wout_d, Wout, t_Wout, 8, 1024)
    hTc = C.sb([128, 8, 512], BF16, "hTc"); t_hTc = TK()
    yTc = C.sb([128, 12, 512], BF16, "yTc"); t_yTc = TK()
    uT = C.sb([128, 8, 512], BF16, "uT"); t_uT = TK()
    sg = C.sb([128, 512], F32, "sg"); t_sg = TK()
    uacc = C.sb([128, 512], F32, "uacc"); t_uacc = TK()
    ut = C.sb([128, 512], F32, "ut"); t_ut = TK()
    xn = [C.sb([128, 1024], F32, f"xn{i}") for i in range(2)]; t_xn = [TK(), TK()]
    if finalg_d is not None:
        fg = C.sb([128, 1024], F32, "fg"); t_fg = TK()
        P.dma("sp", [], [t_fg], fg[:], finalg_d[:, :])
        ost = [C.sb([128, 1024], F32, f"ost{i}") for i in range(2)]; t_ost = [TK(), TK()]
    hv = hT_d.rearrange("(k p) t -> p k t", p=128)
    yv = yT_d.rearrange("(k p) t -> p k t", p=128)
    hvn = hTn_d.rearrange("(k p) t -> p k t", p=128) if hTn_d is not None else None
    sm = M["small"]; tsm = M["t_small"]
    g = 0
    for ci, (t0, n) in enumerate(chunks or CHUNKS2):
        v = 1 if ci == 0 else 0
        gate, t_gate = M["mod"][(key, v, 2)]
        P.dma("sp", [], [t_hTc], hTc[:, :, 0:n], hv[:, :, t0:t0 + n])
        P.dma(POOLD, [], [t_yTc], yTc[:, :, 0:n], yv[:, :, t0:t0 + n])
        for oc in range(8):
            for nb in range(3):
                pg = M["ps"][g % 2]; tpg = M["t_ps"][g % 2]
                pp = M["ps"][2 + g % 2]; tpp = M["t_ps"][2 + g % 2]
                g += 1
                for k in range(8):
                    P.op("pe", "matmul", [t_Wgm, t_hTc], [tpg], pg[:, 0:n], lhsT=Wgm[:, k, nb * 1024 + oc * 128:nb * 1024 + (oc + 1) * 128], rhs=hTc[:, k, 0:n],
                         start=(k == 0), stop=(k == 7))
                for kk in range(4):
                    P.op("pe", "matmul", [t_Wbr, t_yTc], [tpp], pp[:, 0:n], lhsT=Wbr[:, nb * 4 + kk, oc * 128:(oc + 1) * 128], rhs=yTc[:, nb * 4 + kk, 0:n],
                         start=(kk == 0), stop=(kk == 3))
                P.op("act", "activation", [tpg], [t_sg], out=sg[:, 0:n], in_=pg[:, 0:n], func=AF.Sigmoid)
                if nb == 0:
                    P.op("dve", "tensor_tensor", [tpp, t_sg], [t_uacc], out=uacc[:, 0:n], in0=pp[:, 0:n], in1=sg[:, 0:n], op=ALU.mult)
                else:
                    P.op("dve", "tensor_tensor", [tpp, t_sg], [t_ut], out=ut[:, 0:n], in0=pp[:, 0:n], in1=sg[:, 0:n], op=ALU.mult)
                    if nb == 1:
                        P.op(POOLC, "tensor_tensor", [t_ut, t_uacc], [t_uacc], out=uacc[:, 0:n], in0=uacc[:, 0:n], in1=ut[:, 0:n], op=ALU.add)
                    else:
                        P.op(POOLC, "tensor_tensor", [t_ut, t_uacc], [t_uT], out=uT[:, oc, 0:n], in0=uacc[:, 0:n], in1=ut[:, 0:n], op=ALU.add)
        for tt in range(n // 128):
            i = t0 // 128 + tt
            xt = M["xt"][i % 2]; t_xt = M["t_xt"][i % 2]
            xo = xn[i % 2]; t_xo = t_xn[i % 2]
            P.dma(POOLD, [], [t_xt], xt[:], x_d[i * 128:(i + 1) * 128, :])
            for half in range(2):
                po = M["ps"][4 + half]; tpo = M["t_ps"][4 + half]
                cs = slice(half * 512, (half + 1) * 512)
                for oc in range(8):
                    P.op("pe", "matmul", [t_uT, t_Wout], [tpo], po[:, :], lhsT=uT[:, oc, tt * 128:(tt + 1) * 128], rhs=Wout[:, oc, cs], start=(oc == 0), stop=(oc == 7))
                P.op("dve", "tensor_tensor", [tpo, t_gate], [t_xo], out=xo[:, cs], in0=po[:, :], in1=gate[:, cs], op=ALU.mult)
                P.op(POOLC, "tensor_tensor", [t_xo, t_xt], [t_xo], out=xo[:, cs], in0=xo[:, cs], in1=xt[:, cs], op=ALU.add)
            if xout_d is not None:
                P.dma("sp", [t_xo], [], xout_d[i * 128:(i + 1) * 128, :], xo[:])
            if next_key is not None:
                norm_tile(C, M, xo, t_xo, M["mod"][(next_key, v, 1)], M["mod"][(next_key, v, 0)], hvn, i)
            if finalg_d is not None:
                os_ = ost[i % 2]; t_os = t_ost[i % 2]
                P.op("act", "activation", [t_xo], [M["t_junk"], tsm], out=M["junk"][:], in_=xo[:], func=AF.Square, accum_out=sm[:, 2:3])
                rms_scale(P, sm, tsm, 2, 3, 1024)
                P.op("dve", "scalar_tensor_tensor", [t_xo, tsm, t_fg], [t_os], out=os_[:], in0=xo[:], scalar=sm[:, 3:4], in1=fg[:], op0=ALU.mult, op1=ALU.mult)
                P.dma("sp", [t_os], [], out_d[i * 128:(i + 1) * 128, :], os_[:])


def rope_tables():
    t = np.arange(S)
    row = (t // GRID_W).astype(np.float32)
    col = (t % GRID_W).astype(np.float32)
    inv = (10000.0 ** (-np.arange(0, 32, 2, dtype=np.float32) / 32)).astype(np.float32)
    ang = np.concatenate([row[:, None] * inv, col[:, None] * inv], axis=-1).astype(np.float32)
    cos = np.cos(ang).astype(np.float32); sin = np.sin(ang).astype(np.float32)
    cosT = np.ones((128, T), np.float32); sinT = np.zeros((128, T), np.float32)
    for c in range(2):
        cosT[c * 64:c * 64 + 32, LC:] = cos.T
        cosT[c * 64 + 32:c * 64 + 64, LC:] = cos.T
        sinT[c * 64:c * 64 + 32, LC:] = -sin.T
        sinT[c * 64 + 32:c * 64 + 64, LC:] = sin.T
    return cosT, sinT


def wA_cols(g):
    cols = []
    ev = np.arange(0, 64, 2); od = np.arange(1, 64, 2)
    for base in (OFF["aq"], OFF["ak"]):
        main = []; swp = []
        for c in range(2):
            b0 = base + g * 128 + c * 64
            main += list(b0 + ev) + list(b0 + od)
            swp += list(b0 + od) + list(b0 + ev)
        cols += main + swp
    cols += list(OFF["av"] + g * 128 + np.arange(128)) + list(OFF["az"] + g * 128 + np.arange(128))
    return np.array(cols)


def rep128(v):
    return np.ascontiguousarray(np.broadcast_to(np.asarray(v, np.float32).reshape(1, -1), (128, v.size)))


def wB_cols(g):
    r = np.arange(128)
    cols = list(OFF["bq"] + g * 128 + r) + list(OFF["bk"] + g * 128 + r)
    cols += list(OFF["bv"] + g * 128 + r) + list(OFF["bo"] + g * 128 + r) + list(OFF["bz"] + g * 128 + r)
    cols += [OFF["bg"] + 0 + g, OFF["bg"] + 4 + g, OFF["bg"] + 8 + g, OFF["bg"] + 12 + g]
    return np.array(cols)


def convp_host(conv_w, conv_b, g):
    out = np.zeros((128, 8), np.float32)
    for j, base in enumerate((g * 128, 512 + g * 128)):
        out[:, 4 * j:4 * j + 3] = conv_w[:, base:base + 128].T
        out[:, 4 * j + 3] = conv_b[base:base + 128]
    return out


def tri_host():
    s = np.arange(128)
    triF = (s[:, None] <= s[None, :]).astype(np.float32)
    return np.ascontiguousarray(np.concatenate([triF, triF.T, np.ones((128, 128), np.float32)], axis=1))


def wC_cols(g):
    r = np.arange(128)
    return np.array(list(OFF["cq"] + g * 128 + r) + list(OFF["ck"] + g * 128 + r) + list(OFF["cv"] + g * 128 + r) + list(OFF["cz"] + g * 128 + r))


def na_geometry():
    pats = [(0, 0), (1, 0), (10, 8), (62, 59), (63, 59)]
    ridx = np.zeros((5, 5, 128, 128), np.int64); cidx = np.zeros_like(ridx); valid = np.zeros(ridx.shape, bool)
    ki = np.arange(128); qi = np.arange(128)
    for p, (j, kt0) in enumerate(pats):
        rq = 2 * j + qi // 64; cq = qi % 64
        rs = np.clip(rq - 4, 0, 120); cs = np.clip(cq - 8, 0, 48)
        for a in range(5):
            rk = 2 * (kt0 + a) + ki // 64; ck = ki % 64
            v = (rk[:, None] >= rs[None, :]) & (rk[:, None] < rs[None, :] + 8) & (ck[:, None] >= cs[None, :]) & (ck[:, None] < cs[None, :] + 16)
            valid[p, a] = v
            ridx[p, a] = np.clip(rk[:, None] - rq[None, :] + 7, 0, 14)
            cidx[p, a] = np.clip(ck[:, None] - cq[None, :] + 15, 0, 30)
    return ridx, cidx, valid


def na_bias_host(rpb, g):
    ridx, cidx, valid = na_geometry()
    out = np.zeros((128, 5, 2, 5, 128), np.float32)
    for hh in range(2):
        gathered = rpb[2 * g + hh][ridx, cidx]
        out[:, :, hh, :, :] = np.transpose(gathered, (2, 0, 1, 3))
    return np.ascontiguousarray(out.reshape(128, 6400))


def na_mask_host():
    ridx, cidx, valid = na_geometry()
    m = np.where(valid, 0.0, NEG).astype(np.float32)
    out = np.zeros((128, 5, 2, 5, 128), np.float32)
    for hh in range(2):
        out[:, :, hh, :, :] = np.transpose(m, (2, 0, 1, 3))
    return np.ascontiguousarray(out.reshape(128, 6400))


def build_N0():
    nc = bass.Bass("TRN2", target_bir_lowering=False)
    with contextlib.ExitStack() as es:
        C = Ctx(nc, es)
        x_d = C.dram("x_in", [NTOK2, D], F32, "ExternalInput")
        crepL = C.dram("crepL", [128, 8, 128], F32, "ExternalInput")
        crepC = C.dram("crepC", [128, 8, 128], F32, "ExternalInput")
        wmod = C.dram("wmod", [D, 3 * D], F32, "ExternalInput")
        bmod = C.dram("bmodrep", [128, 3 * D], F32, "ExternalInput")
        normg = C.dram("normgrep", [128, D], F32, "ExternalInput")
        ident_d = C.dram("ident", [128, 128], F32, "ExternalInput")
        hT_d = C.dram("hT_out", [D, NTOK2], BF16, "ExternalOutput")
        M = alloc_m2(C)
        load_ident(C, M, ident_d)
        alloc_mod_out(C, M, "n", [0, 1])
        outer = C.es
        with contextlib.ExitStack() as es2:
            C.es = es2
            alloc_mod_tmp(C, M)
            mod_vectors(C, M, "n", crepL, crepC, wmod, bmod, [0, 1], normg)
        C.es = outer
        C.P.barrier()
        stage_N(C, M, "n", x_d, hT_d)
        C.P.emit()
    return nc


def build_M1(l):
    nc = bass.Bass("TRN2", target_bir_lowering=False)
    with contextlib.ExitStack() as es:
        C = Ctx(nc, es)
        hT_d = C.dram("hT", [D, T], BF16, "ExternalInput")
        ident_d = C.dram("ident", [128, 128], F32, "ExternalInput")
        yT_d = C.dram("yT", [384, T], BF16, "ExternalOutput")
        wA_d = C.dram("wA", [D, 768], F32, "ExternalInput")
        cos_d = C.dram("cosT", [128, T], F32, "ExternalInput")
        sin_d = C.dram("sinT", [128, T], F32, "ExternalInput")
        lamq_d = C.dram("lamq", [128, 128], F32, "ExternalInput")
        lamk_d = C.dram("lamk", [128, 128], F32, "ExternalInput")
        dag_d = C.dram("dag", [128, 128], F32, "ExternalInput")
        wB_d = C.dram("wB", [D, 644], F32, "ExternalInput")
        convp_d = C.dram("convp", [128, 8], F32, "ExternalInput")
        gateb_d = C.dram("gateb", [128, 4], F32, "ExternalInput")
        mlg_d = C.dram("mlg", [128, 128], F32, "ExternalInput")
        tri_d = C.dram("tri", [128, 384], F32, "ExternalInput")
        wC_d = C.dram("wC", [D, 512], F32, "ExternalInput")
        biasg_d = C.dram("biasg", [128, 6400], F32, "ExternalInput")
        maskc_d = C.dram("maskc", [128, 6400], F32, "ExternalInput")
        R = alloc_m1(C)
        load_ident(C, R, ident_d)
        outer = C.es
        with contextlib.ExitStack() as es2:
            C.es = es2
            pass_A(C, R, l, hT_d, wA_d, cos_d, sin_d, lamq_d, lamk_d, dag_d, yT_d)
        C.P.barrier()
        with contextlib.ExitStack() as es2:
            C.es = es2
            pass_B(C, R, l, hT_d, wB_d, convp_d, gateb_d, mlg_d, tri_d, yT_d)
        C.P.barrier()
        with contextlib.ExitStack() as es2:
            C.es = es2
            pass_C(C, R, l, hT_d, wC_d, biasg_d, maskc_d, yT_d)
        C.es = outer
        C.P.emit()
    return nc


def build_M2(last):
    nc = bass.Bass("TRN2", target_bir_lowering=False)
    with contextlib.ExitStack() as es:
        C = Ctx(nc, es)
        x_d = C.dram("x_in", [NTOK2, D], F32, "ExternalInput")
        hT_d = C.dram("hT", [D, NTOK2], BF16, "ExternalInput")
        yT_d = C.dram("yT", [1536, NTOK2], BF16, "ExternalInput")
        wgm = C.dram("wgm", [D, 3 * D], F32, "ExternalInput")
        wbr = C.dram("wbr", [1536, D], F32, "ExternalInput")
        wout = C.dram("wout", [D, D], F32, "ExternalInput")
        crepL = C.dram("crepL", [128, 8, 128], F32, "ExternalInput")
        crepC = C.dram("crepC", [128, 8, 128], F32, "ExternalInput")
        wmod = C.dram("wmod", [D, 3 * D], F32, "ExternalInput")
        bmod = C.dram("bmodrep", [128, 3 * D], F32, "ExternalInput")
        ident_d = C.dram("ident", [128, 128], F32, "ExternalInput")
        M = alloc_m2(C)
        load_ident(C, M, ident_d)
        alloc_mod_out(C, M, "g", [2])
        if not last:
            alloc_mod_out(C, M, "n", [0, 1])
            wmod2 = C.dram("wmod2", [D, 3 * D], F32, "ExternalInput")
            bmod2 = C.dram("bmodrep2", [128, 3 * D], F32, "ExternalInput")
            normg2 = C.dram("normgrep2", [128, D], F32, "ExternalInput")
            xout_d = C.dram("x_out", [NTOK2, D], F32, "ExternalOutput")
            hTn_d = C.dram("hT_out", [D, NTOK2], BF16, "ExternalOutput")
        outer = C.es
        with contextlib.ExitStack() as es2:
            C.es = es2
            alloc_mod_tmp(C, M)
            mod_vectors(C, M, "g", crepL, crepC, wmod, bmod, [2])
            if not last:
                mod_vectors(C, M, "n", crepL, crepC, wmod2, bmod2, [0, 1], normg2)
        C.es = outer
        C.P.barrier()
        if not last:
            stage_M2(C, M, "g", x_d, hT_d, yT_d, wgm, wbr, wout, xout_d, next_key="n", hTn_d=hTn_d)
        else:
            fg = C.dram("finalg", [128, D], F32, "ExternalInput")
            out_d = C.dram("out", [NTOK2, D], F32, "ExternalOutput")
            stage_M2(C, M, "g", x_d, hT_d, yT_d, wgm, wbr, wout, None, finalg_d=fg, out_d=out_d)
        C.P.emit()
    return nc


def crep_host(c):
    a = np.asarray(c, np.float32).reshape(8, 128).T
    return np.ascontiguousarray(np.broadcast_to(a[:, :, None], (128, 8, 128)))


def assemble_hT(parts, b):
    return np.ascontiguousarray(np.concatenate([parts[(b, 0)][:, 0:256]] + [parts[(b, s)][:, 256:] for s in range(4)], axis=1))


def tok_slice(a, s):
    return np.concatenate([a[..., 0:256], a[..., 256 + s * 2048:256 + (s + 1) * 2048]], axis=-1)


CORES = [(b, i) for b in range(B) for i in range(4)]


def kernel_unfused(x, c, ctx, c_ctx, w_mod, b_mod, norm_g, w_in, da_lam_q, da_lam_k, da_norm_g,
           ml_conv_w, ml_conv_b, ml_gate_b, ml_norm_g, na_rpb, w_br, w_out, final_g):
    f32 = lambda a: np.ascontiguousarray(np.asarray(a, np.float32))
    x = f32(x); c = f32(c); ctx = f32(ctx); c_ctx = f32(c_ctx); w_mod = f32(w_mod); b_mod = f32(b_mod); norm_g = f32(norm_g)
    w_in = f32(w_in); w_br = f32(w_br); w_out = f32(w_out); final_g = f32(final_g)
    ident = np.eye(128, dtype=np.float32)
    cosT, sinT = rope_tables()
    tri = tri_host()
    maskc = na_mask_host()
    ids = list(range(8))
    crepC = crep_host(c_ctx)
    crepL = {b: crep_host(c[b]) for b in range(B)}

    x_cur = {(b, s): np.ascontiguousarray(np.concatenate([ctx[b], x[b, s * 2048:(s + 1) * 2048]], axis=0)) for (b, s) in CORES}
    ncA = build_N0()
    im = [dict(x_in=x_cur[(b, s)], crepL=crepL[b], crepC=crepC, wmod=w_mod[0], bmodrep=rep128(b_mod[0]), normgrep=rep128(norm_g[0]), ident=ident)
          for (b, s) in CORES]
    res = run_bass_kernel_spmd(ncA, im, core_ids=ids).results
    hparts = {CORES[i]: np.asarray(res[i]["hT_out"]) for i in range(8)}
    out = np.zeros((B, S, D), np.float32)
    for l in range(DEPTH):
        hT = {b: assemble_hT(hparts, b) for b in range(B)}
        nc1 = build_M1(l)
        im = []
        for (b, g) in CORES:
            gb = np.asarray(ml_gate_b[l], np.float32)
            im.append(dict(
                hT=hT[b], ident=ident,
                wA=np.ascontiguousarray(w_in[l][:, wA_cols(g)]), cosT=cosT, sinT=sinT,
                lamq=rep128(np.asarray(da_lam_q[l], np.float32).reshape(-1)), lamk=rep128(np.asarray(da_lam_k[l], np.float32).reshape(-1)),
                dag=rep128(np.asarray(da_norm_g[l], np.float32)),
                wB=np.ascontiguousarray(w_in[l][:, wB_cols(g)]),
                convp=convp_host(np.asarray(ml_conv_w[l], np.float32), np.asarray(ml_conv_b[l], np.float32), g),
                gateb=rep128(np.array([gb[0, 0, g], gb[0, 1, g], gb[1, 0, g], gb[1, 1, g]], np.float32)),
                mlg=rep128(np.asarray(ml_norm_g[l], np.float32)), tri=tri,
                wC=np.ascontiguousarray(w_in[l][:, wC_cols(g)]), biasg=na_bias_host(np.asarray(na_rpb[l], np.float32), g), maskc=maskc))
        res = run_bass_kernel_spmd(nc1, im, core_ids=ids).results
        yT = {CORES[i]: np.asarray(res[i]["yT"]) for i in range(8)}
        last = (l == DEPTH - 1)
        nc2 = build_M2(last)
        im = []
        for (b, s) in CORES:
            yall = np.concatenate([yT[(b, g)][n * 128:(n + 1) * 128] for n in range(3) for g in range(4)], axis=0)
            d = dict(x_in=x_cur[(b, s)], hT=np.ascontiguousarray(tok_slice(hT[b], s)), yT=np.ascontiguousarray(tok_slice(yall, s)),
                     wgm=np.ascontiguousarray(w_in[l][:, OFF["gm"]:]), wbr=np.ascontiguousarray(w_br[l].reshape(1536, D)), wout=w_out[l],
                     crepL=crepL[b], crepC=crepC, wmod=w_mod[l], bmodrep=rep128(b_mod[l]), ident=ident)
            if not last:
                d.update(wmod2=w_mod[l + 1], bmodrep2=rep128(b_mod[l + 1]), normgrep2=rep128(norm_g[l + 1]))
            else:
                d.update(finalg=rep128(final_g))
            im.append(d)
        res = run_bass_kernel_spmd(nc2, im, core_ids=ids).results
        if not last:
            x_cur = {CORES[i]: np.asarray(res[i]["x_out"]) for i in range(8)}
            hparts = {CORES[i]: np.asarray(res[i]["hT_out"]) for i in range(8)}
        else:
            for i, (b, s) in enumerate(CORES):
                out[b, s * 2048:(s + 1) * 2048] = np.asarray(res[i]["out"])[256:]
    return out


@contextlib.contextmanager
def scope(C):
    outer = C.es
    with contextlib.ExitStack() as es2:
        C.es = es2
        try:
            yield
        finally:
            C.es = outer


def build_fused():
    nc = bass.Bass("TRN2", target_bir_lowering=False)
    with contextlib.ExitStack() as es:
        C = Ctx(nc, es)
        P = C.P
        ext = lambda name, shape, dt=F32: C.dram(name, shape, dt, "ExternalInput")
        x_in = ext("x_in", [T, D])
        crepL = ext("crepL", [128, 8, 128]); crepC = ext("crepC", [128, 8, 128]); ident_d = ext("ident", [128, 128])
        cos_d = ext("cosT", [128, T]); sin_d = ext("sinT", [128, T]); tri_d = ext("tri", [128, 384]); maskc_d = ext("maskc", [128, 6400])
        fg_d = ext("finalg", [128, D])
        L = []
        for l in range(DEPTH):
            L.append(dict(
                wmod=ext(f"wmod{l}", [D, 3 * D]), bmod=ext(f"bmodrep{l}", [128, 3 * D]), normg=ext(f"normgrep{l}", [128, D]),
                wA=ext(f"wA{l}", [4, D, 768]), wB=ext(f"wB{l}", [4, D, 644]), wC=ext(f"wC{l}", [4, D, 512]),
                lamq=ext(f"lamq{l}", [128, 128]), lamk=ext(f"lamk{l}", [128, 128]), dag=ext(f"dag{l}", [128, 128]),
                convp=ext(f"convp{l}", [4, 128, 8]), gateb=ext(f"gateb{l}", [4, 128, 4]), mlg=ext(f"mlg{l}", [128, 128]),
                biasg=ext(f"biasg{l}", [4, 128, 6400]),
                wgm=ext(f"wgm{l}", [D, 3 * D]), wbr=ext(f"wbr{l}", [1536, D]), wout=ext(f"wout{l}", [D, D])))
        out_d = C.dram("out", [T, D], F32, "ExternalOutput")
        hTs = [nc.dram_tensor(f"hT_scr{i}", [D, T], BF16).ap() for i in range(2)]
        yT = nc.dram_tensor("yT_scr", [1536, T], BF16).ap()
        xbuf = nc.dram_tensor("x_scr", [T, D], F32).ap()

        G = alloc_common(C)
        load_ident(C, G, ident_d)
        with scope(C):
            M = alloc_m2(C, G)
            alloc_mod_out(C, M, "n", [0, 1])
            with scope(C):
                alloc_mod_tmp(C, M)
                mod_vectors(C, M, "n", crepL, crepC, L[0]["wmod"], L[0]["bmod"], [0, 1], L[0]["normg"])
            P.barrier()
            stage_N(C, M, "n", x_in, hTs[0], ntok=T)
        P.barrier()
        for l in range(DEPTH):
            last = (l == DEPTH - 1)
            W = L[l]
            hT = hTs[l % 2]
            with scope(C):
                R = alloc_m1(C, G)
                for g in range(4):
                    with scope(C):
                        pass_A(C, R, l, hT, W["wA"][g], cos_d, sin_d, W["lamq"], W["lamk"], W["dag"], yT, yrow=0 * 512 + g * 128)
                    P.barrier()
                    with scope(C):
                        pass_B(C, R, l, hT, W["wB"][g], W["convp"][g], W["gateb"][g], W["mlg"], tri_d, yT, yrow=1 * 512 + g * 128)
                    P.barrier()
                    with scope(C):
                        pass_C(C, R, l, hT, W["wC"][g], W["biasg"][g], maskc_d, yT, yrow=2 * 512 + g * 128)
                    P.barrier()
            with scope(C):
                M = alloc_m2(C, G)
                alloc_mod_out(C, M, "g", [2])
                if not last:
                    alloc_mod_out(C, M, "n", [0, 1])
                with scope(C):
                    alloc_mod_tmp(C, M)
                    mod_vectors(C, M, "g", crepL, crepC, W["wmod"], W["bmod"], [2])
                    if not last:
                        mod_vectors(C, M, "n", crepL, crepC, L[l + 1]["wmod"], L[l + 1]["bmod"], [0, 1], L[l + 1]["normg"])
                P.barrier()
                x_src = x_in if l == 0 else xbuf
                if not last:
                    stage_M2(C, M, "g", x_src, hT, yT, W["wgm"], W["wbr"], W["wout"], xbuf, next_key="n", hTn_d=hTs[(l + 1) % 2], chunks=CHUNKS)
                else:
                    stage_M2(C, M, "g", x_src, hT, yT, W["wgm"], W["wbr"], W["wout"], None, finalg_d=fg_d, out_d=out_d, chunks=CHUNKS)
            P.barrier()
        print("[kernel] instruction counts", {k: len(v) for k, v in P.q.items()}, flush=True)
        P.emit()
    return nc


def kernel(x, c, ctx, c_ctx, w_mod, b_mod, norm_g, w_in, da_lam_q, da_lam_k, da_norm_g,
           ml_conv_w, ml_conv_b, ml_gate_b, ml_norm_g, na_rpb, w_br, w_out, final_g):
    f32 = lambda a: np.ascontiguousarray(np.asarray(a, np.float32))
    x = f32(x); c = f32(c); ctx = f32(ctx); c_ctx = f32(c_ctx); w_mod = f32(w_mod); b_mod = f32(b_mod); norm_g = f32(norm_g)
    w_in = f32(w_in); w_br = f32(w_br); w_out = f32(w_out); final_g = f32(final_g)
    cosT, sinT = rope_tables()
    shared = dict(ident=np.eye(128, dtype=np.float32), cosT=cosT, sinT=sinT, tri=tri_host(), maskc=na_mask_host(),
                  finalg=rep128(final_g), crepC=crep_host(c_ctx))
    for l in range(DEPTH):
        gb = np.asarray(ml_gate_b[l], np.float32)
        shared.update({
            f"wmod{l}": w_mod[l], f"bmodrep{l}": rep128(b_mod[l]), f"normgrep{l}": rep128(norm_g[l]),
            f"wA{l}": np.ascontiguousarray(np.stack([w_in[l][:, wA_cols(g)] for g in range(4)])),
            f"wB{l}": np.ascontiguousarray(np.stack([w_in[l][:, wB_cols(g)] for g in range(4)])),
            f"wC{l}": np.ascontiguousarray(np.stack([w_in[l][:, wC_cols(g)] for g in range(4)])),
            f"lamq{l}": rep128(np.asarray(da_lam_q[l], np.float32).reshape(-1)), f"lamk{l}": rep128(np.asarray(da_lam_k[l], np.float32).reshape(-1)),
            f"dag{l}": rep128(np.asarray(da_norm_g[l], np.float32)),
            f"convp{l}": np.stack([convp_host(np.asarray(ml_conv_w[l], np.float32), np.asarray(ml_conv_b[l], np.float32), g) for g in range(4)]),
            f"gateb{l}": np.stack([rep128(np.array([gb[0, 0, g], gb[0, 1, g], gb[1, 0, g], gb[1, 1, g]], np.float32)) for g in range(4)]),
            f"mlg{l}": rep128(np.asarray(ml_norm_g[l], np.float32)),
            f"biasg{l}": np.stack([na_bias_host(np.asarray(na_rpb[l], np.float32), g) for g in range(4)]),
            f"wgm{l}": np.ascontiguousarray(w_in[l][:, OFF["gm"]:]), f"wbr{l}": np.ascontiguousarray(w_br[l].reshape(1536, D)), f"wout{l}": w_out[l]})
    im = []
    for b in range(B):
        d = dict(shared)
        d["x_in"] = np.ascontiguousarray(np.concatenate([ctx[b], x[b]], axis=0))
        d["crepL"] = crep_host(c[b])
        im.append(d)
    nc = build_fused()
    res = run_bass_kernel_spmd(nc, im, core_ids=list(range(B))).results
    return np.stack([np.asarray(res[b]["out"])[LC:] for b in range(B)]).astype(np.float32)
```

```python
import contextlib
import math
import numpy as np
import ml_dtypes
import concourse.bass as bass
import concourse.mybir as mybir
from concourse.bass_utils import run_bass_kernel_spmd

F32 = mybir.dt.float32
BF16 = mybir.dt.bfloat16
AF = mybir.ActivationFunctionType
ALU = mybir.AluOpType
AX = mybir.AxisListType

D = 1024
B = 2
S = 8192
LC = 256
T = S + LC
NT = T // 128
DEPTH = 2
EPS = 1e-6
NDSEM = 8
GRID_W = 64
NEG = -30000.0
import os
POOLC = os.environ.get("K_POOLC", "pool")
POOLD = os.environ.get("K_POOLD", "pool")

OFF = dict(aq=0, ak=512, av=1024, az=1536, bq=2048, bk=2560, bv=3072, bo=3584, bz=4096, bg=4608,
           cq=4624, ck=5136, cv=5648, cz=6160, gm=6672)
D_IN = 9744

CHUNKS = [(0, 256)] + [(256 + i * 512, 512) for i in range(16)]


class TK:
    __slots__ = ("w", "r", "excl")

    def __init__(self, excl=False):
        self.w = None
        self.r = []
        self.excl = excl


class Prog:
    def __init__(self, nc):
        self.nc = nc
        self.q = {e: [] for e in ("pe", "act", "dve", "pool", "sp")}
        self.cnt = {e: 0 for e in self.q}
        self.dcnt = {e: 0 for e in ("sp", "act", "pool")}
        self.seen = {e: {} for e in self.q}
        self.dma_tokens = []
        self.pending = {e: [] for e in self.q}

    def barrier(self):
        toks = [(("e", e), c) for e, c in self.cnt.items() if c > 0]
        last = {}
        for k, v in self.dma_tokens:
            last[k] = max(last.get(k, 0), v)
        toks += list(last.items())
        for e in self.q:
            self.pending[e] = list(toks)

    def _waits(self, eng, reads, writes):
        w = {}
        seen = self.seen[eng]

        def need(tok):
            if tok is None:
                return
            k, v = tok
            if eng == "pe" and k == ("e", "pe"):
                return
            if seen.get(k, 0) >= v:
                return
            if w.get(k, 0) < v:
                w[k] = v
        me = ("e", eng)
        if self.pending[eng]:
            for tok in self.pending[eng]:
                if tok[0] != me:
                    need(tok)
            self.pending[eng] = []
        for t in reads:
            need(t.w)
            if t.excl:
                for r in t.r:
                    if r[0] != me:
                        need(r)
        for t in writes:
            need(t.w)
            for r in t.r:
                need(r)
        for k, v in w.items():
            seen[k] = v
        return w

    def _update(self, tok, reads, writes):
        for t in reads:
            if len(t.r) > 24:
                m = {}
                for k, v in t.r:
                    if m.get(k, 0) < v:
                        m[k] = v
                t.r = list(m.items())
            t.r.append(tok)
        for t in writes:
            t.w = tok
            t.r = []

    def op(self, eng, meth, reads, writes, *a, **kw):
        w = self._waits(eng, reads, writes)
        self.cnt[eng] += 1
        tok = (("e", eng), self.cnt[eng])
        self.q[eng].append((w, (meth, a, kw), None))
        self._update(tok, reads, writes)
        return tok

    def dma(self, queue, reads, writes, out=None, in_=None, fn=None):
        if fn is None:
            fn = ("dma_start", (), dict(out=out, in_=in_))
        i = self.dcnt[queue]
        self.dcnt[queue] += 1
        s = i % NDSEM
        prev = 16 * (i // NDSEM)
        w = self._waits(queue, reads, writes)
        key = ("d", queue, s)
        if prev > 0 and self.seen[queue].get(key, 0) < prev:
            w[key] = prev
            self.seen[queue][key] = prev
        tok = (key, prev + 16)
        self.q[queue].append((w, fn, key))
        self._update(tok, reads, writes)
        self.dma_tokens.append(tok)
        return tok

    def emit(self):
        nc = self.nc
        with contextlib.ExitStack() as es:
            sems = {}
            for e in self.q:
                sems[("e", e)] = es.enter_context(nc.semaphore("s_" + e))
            for qn in self.dcnt:
                for s in range(NDSEM):
                    sems[("d", qn, s)] = es.enter_context(nc.semaphore(f"d_{qn}_{s}"))
            fin = {}
            for k, v in self.dma_tokens:
                fin[k] = max(fin.get(k, 0), v)
            block = es.enter_context(nc.Block())

            def run(name, e):
                for w, fn, dkey in self.q[name]:
                    for k, v in w.items():
                        e.wait_ge(sems[k], v)
                    inst = getattr(e, fn[0])(*fn[1], **fn[2])
                    if dkey is None:
                        inst.then_inc(sems[("e", name)], 1)
                    else:
                        inst.then_inc(sems[dkey], 16)
                if name == "sp":
                    for k, v in fin.items():
                        e.wait_ge(sems[k], v)

            @block.tensor
            def _(e):
                run("pe", e)

            @block.scalar
            def _(e):
                run("act", e)

            @block.vector
            def _(e):
                run("dve", e)

            @block.gpsimd
            def _(e):
                run("pool", e)

            @block.sync
            def _(e):
                run("sp", e)


class Ctx:
    def __init__(self, nc, es):
        self.nc = nc
        self.es = es
        self.P = Prog(nc)
        self.n = 0

    def sb(self, shape, dt, name=None):
        self.n += 1
        return self.es.enter_context(self.nc.sbuf_tensor(f"sb{self.n}_{name or 'x'}", list(shape), dt))

    def ps(self, shape, dt, name=None):
        self.n += 1
        return self.es.enter_context(self.nc.psum_tensor(f"ps{self.n}_{name or 'x'}", list(shape), dt))

    def dram(self, name, shape, dt, kind):
        return self.nc.dram_tensor(name, list(shape), dt, kind=kind).ap()


def dma_rr(C):
    C._rr = getattr(C, "_rr", 0) + 1
    return "sp" if C._rr % 2 else POOLD


def alloc_common(C):
    G = {}
    G["ident"] = C.sb([128, 128], BF16, "ident_sb"); G["t_ident"] = TK()
    G["identf"] = C.sb([128, 128], F32, "identf")
    G["ps"] = [C.ps([128, 512], F32, f"psb{i}") for i in range(7)]; G["t_ps"] = [TK(True) for _ in range(7)]
    G["pst"] = C.ps([128, 1024], BF16, "pstr"); G["t_pst"] = TK(True)
    G["small"] = C.sb([128, 64], F32, "small"); G["t_small"] = TK()
    return G


def alloc_m1(C, G=None):
    R = dict(G) if G is not None else alloc_common(C)
    R["qT"] = C.sb([128, T], BF16, "qT"); R["t_qT"] = [TK() for _ in range(17)]
    R["kT"] = C.sb([128, T], BF16, "kT"); R["t_kT"] = [TK() for _ in range(17)]
    R["V"] = C.sb([128, NT, 132], BF16, "Vaug"); R["t_V"] = [TK() for _ in range(NT)]
    R["zg"] = C.sb([128, NT, 128], BF16, "zg"); R["t_zg"] = [TK() for _ in range(NT)]
    R["hbuf"] = [C.sb([128, 8, 512], BF16, f"hbuf{i}") for i in range(2)]; R["t_hbuf"] = [TK(), TK()]
    R["wst"] = [C.sb([128, 8, 128], F32, f"wst{i}") for i in range(2)]; R["t_wst"] = [TK(), TK()]
    R["W"] = C.sb([128, 8, 768], BF16, "Wbf"); R["t_W"] = TK()
    R["ystage"] = [C.sb([128, 512], BF16, f"ystage{i}") for i in range(2)]; R["t_ystage"] = [TK(), TK()]
    return R


def load_ident(C, R, ident_d):
    P = C.P
    P.dma("sp", [], [R["t_ident"]], R["identf"][:], ident_d[:, :])
    P.op("dve", "tensor_copy", [R["t_ident"]], [R["t_ident"]], out=R["ident"][:], in_=R["identf"][:])


def load_weights(C, R, w_d, ncols):
    P = C.P
    wv = w_d.rearrange("(k p) n -> p k n", p=128)
    i = 0
    for c0 in range(0, ncols, 128):
        cn = min(128, ncols - c0)
        st = R["wst"][i % 2]; tst = R["t_wst"][i % 2]
        P.dma(dma_rr(C), [], [tst], st[:, :, 0:cn], wv[:, :, c0:c0 + cn])
        P.op(POOLC, "tensor_copy", [tst], [R["t_W"]], out=R["W"][:, :, c0:c0 + cn], in_=st[:, :, 0:cn])
        i += 1


def project(C, R, hT_d, fm_groups, tm_cols, fm_cb, tm_cb, pre_chunk=None):
    P = C.P
    hv = hT_d.rearrange("(k p) t -> p k t", p=128)
    W = R["W"]
    for ci, (t0, n) in enumerate(CHUNKS):
        hb = R["hbuf"][ci % 2]; thb = R["t_hbuf"][ci % 2]
        P.dma("sp", [], [thb], hb[:, 0:4, 0:n], hv[:, 0:4, t0:t0 + n])
        P.dma(POOLD, [], [thb], hb[:, 4:8, 0:n], hv[:, 4:8, t0:t0 + n])
        if pre_chunk is not None:
            pre_chunk(ci, t0, n)
        for j, c0 in enumerate(fm_groups):
            pb = R["ps"][j]; tpb = R["t_ps"][j]
            for k in range(8):
                P.op("pe", "matmul", [R["t_W"], thb], [tpb], pb[:, 0:n], lhsT=W[:, k, c0:c0 + 128], rhs=hb[:, k, 0:n],
                     start=(k == 0), stop=(k == 7))
            fm_cb(j, ci, t0, n, pb, tpb)
        if tm_cols is not None:
            c0, nc_ = tm_cols
            for tt in range(n // 128):
                ti = t0 // 128 + tt
                bi = max(4, len(fm_groups)) + (ti % 2)
                pb = R["ps"][bi]; tpb = R["t_ps"][bi]
                for k in range(8):
                    P.op("pe", "matmul", [R["t_W"], thb], [tpb], pb[:, 0:nc_], lhsT=hb[:, k, tt * 128:(tt + 1) * 128],
                         rhs=W[:, k, c0:c0 + nc_], start=(k == 0), stop=(k == 7))
                tm_cb(ti, pb, tpb)


def emit_yT(C, R, src_tiles, t0, n, yT_d, row0):
    P = C.P
    C._ys = getattr(C, "_ys", 0) + 1
    ys = R["ystage"][C._ys % 2]; tys = R["t_ystage"][C._ys % 2]
    for j, (ap, tk) in enumerate(src_tiles):
        P.op("pe", "transpose", [tk, R["t_ident"]], [R["t_pst"]], R["pst"][:, j * 128:(j + 1) * 128], ap, R["ident"][:])
    P.op("dve", "tensor_copy", [R["t_pst"]], [tys], out=ys[:, 0:n], in_=R["pst"][:, 0:n])
    P.dma("sp", [tys], [], yT_d[row0:row0 + 128, t0:t0 + n], ys[:, 0:n])


def pass_A(C, R, l, hT_d, wA_d, cos_d, sin_d, lamq_d, lamk_d, dag_d, yT_d, yrow=0):
    P = C.P
    lam_init = 0.8 - 0.6 * math.exp(-0.3 * l)
    load_weights(C, R, wA_d, 768)
    sm = R["small"]; tsm = R["t_small"]
    lq = C.sb([128, 128], F32, "lq"); lk = C.sb([128, 128], F32, "lk"); dag = C.sb([128, 128], F32, "dag")
    ones = C.sb([128, 128], F32, "onesA"); t_ones = TK()
    t_l = TK(); t_dag = TK()
    P.dma("sp", [], [t_l], lq[:], lamq_d[:, :])
    P.dma("sp", [], [t_l], lk[:], lamk_d[:, :])
    P.dma("sp", [], [t_dag], dag[:], dag_d[:, :])
    P.op(POOLC, "memset", [], [t_ones], ones[:], 1.0)
    P.op("dve", "tensor_tensor", [t_l], [t_l], out=lq[:], in0=lq[:], in1=lk[:], op=ALU.mult)
    P.op("dve", "reduce_sum", [t_l], [tsm], out=sm[:, 0:2], in_=lq[:].rearrange("p (c d) -> p c d", c=2), axis=AX.X)
    P.op("act", "activation", [tsm], [tsm], out=sm[:, 2:4], in_=sm[:, 0:2], func=AF.Exp)
    P.op("dve", "tensor_tensor", [tsm], [tsm], out=sm[:, 4:5], in0=sm[:, 3:4], in1=sm[:, 2:3], op=ALU.subtract)
    P.op("dve", "tensor_scalar_add", [tsm], [tsm], out=sm[:, 4:5], in0=sm[:, 4:5], scalar1=-lam_init)
    P.op("dve", "tensor_scalar_mul", [t_dag], [t_dag], out=dag[:], in0=dag[:], scalar1=(1.0 - lam_init))

    q0z = R["qT"]
    q1z = C.sb([128, T], BF16, "q1z")
    P.op(POOLC, "memset", [], R["t_qT"], q0z[64:128, :], 0.0)
    P.op(POOLC, "memset", [], R["t_qT"], q1z[0:64, :], 0.0)
    zgT = R["zg"][:].rearrange("p a b -> p (a b)")
    t_zgT = R["t_zg"][:17]

    cosb = [C.sb([128, 512], F32, f"cosb{i}") for i in range(2)]
    sinb = [C.sb([128, 512], F32, f"sinb{i}") for i in range(2)]
    t_cs = [TK(), TK()]
    tmp1 = C.sb([128, 512], F32, "ropet1"); tmp2 = C.sb([128, 512], F32, "ropet2"); t_tmp = TK()
    ztmp = C.sb([128, 512], F32, "ztmp"); t_ztmp = TK()

    def pre_chunk(ci, t0, n):
        P.dma("sp", [], [t_cs[ci % 2]], cosb[ci % 2][:, 0:n], cos_d[:, t0:t0 + n])
        P.dma(POOLD, [], [t_cs[ci % 2]], sinb[ci % 2][:, 0:n], sin_d[:, t0:t0 + n])

    def fm_cb(j, ci, t0, n, pb, tpb):
        if j == 4:
            P.op("act", "activation", [tpb], [t_ztmp], out=ztmp[:, 0:n], in_=pb[:, 0:n], func=AF.Silu)
            P.op("dve", "tensor_scalar", [t_ztmp, t_dag], [t_zgT[ci]], out=zgT[:, t0:t0 + n], in0=ztmp[:, 0:n], scalar1=dag[:, 0:1], scalar2=None, op0=ALU.mult)
        elif j % 2 == 0:
            P.op("dve", "tensor_tensor", [tpb, t_cs[ci % 2]], [t_tmp], out=tmp1[:, 0:n], in0=pb[:, 0:n], in1=cosb[ci % 2][:, 0:n], op=ALU.mult)
        else:
            P.op("dve", "tensor_tensor", [tpb, t_cs[ci % 2]], [t_tmp], out=tmp2[:, 0:n], in0=pb[:, 0:n], in1=sinb[ci % 2][:, 0:n], op=ALU.mult)
            if j == 1:
                P.op("dve", "tensor_tensor", [t_tmp], [R["t_qT"][ci]], out=q0z[0:64, t0:t0 + n], in0=tmp1[0:64, 0:n], in1=tmp2[0:64, 0:n], op=ALU.add)
                P.op("dve", "tensor_tensor", [t_tmp], [R["t_qT"][ci]], out=q1z[64:128, t0:t0 + n], in0=tmp1[64:128, 0:n], in1=tmp2[64:128, 0:n], op=ALU.add)
            else:
                P.op("dve", "tensor_tensor", [t_tmp], [R["t_kT"][ci]], out=R["kT"][:, t0:t0 + n], in0=tmp1[:, 0:n], in1=tmp2[:, 0:n], op=ALU.add)

    def tm_cb(ti, pb, tpb):
        P.op("act", "copy", [tpb], [R["t_V"][ti]], out=R["V"][:, ti, 0:128], in_=pb[:, 0:128])

    project(C, R, hT_d, [0, 128, 256, 384, 512], (640, 128), fm_cb, tm_cb, pre_chunk)

    qz = [q0z, q1z]
    pbuf = [C.sb([128, 512], BF16, f"pbuf{i}") for i in range(4)]; t_pbuf = [TK() for _ in range(4)]
    sacc = [C.sb([128, 512], F32, f"sacc{c}") for c in range(2)]; t_sacc = [TK(), TK()]
    rr = [C.sb([128, 512], F32, f"rr{c}") for c in range(2)]; t_rr = [TK(), TK()]
    osb = C.sb([128, 512], F32, "osb"); t_osb = TK()
    osq = C.sb([128, 512], F32, "osq"); t_osq = TK()
    kci = lambda kt: 0 if kt < 2 else 1 + (kt - 2) // 4
    ps = R["ps"]; tps = R["t_ps"]
    for ci, (t0, n) in enumerate(CHUNKS):
        kts = [0, 1] if ci == 0 else list(range(NT))

        def s_mm(i):
            kt = kts[i]
            for c in range(2):
                bk = (i % 2) * 2 + c
                P.op("pe", "matmul", [R["t_kT"][kci(kt)], R["t_qT"][ci]], [tps[bk]], ps[bk][:, 0:n],
                     lhsT=R["kT"][:, kt * 128:(kt + 1) * 128], rhs=qz[c][:, t0:t0 + n], start=True, stop=True)

        s_mm(0)
        for i, kt in enumerate(kts):
            if i + 1 < len(kts):
                s_mm(i + 1)
            for c in range(2):
                bk = (i % 2) * 2 + c
                P.op("act", "activation", [tps[bk]], [t_pbuf[bk]], out=pbuf[bk][:, 0:n], in_=ps[bk][:, 0:n], func=AF.Exp, scale=0.125)
            for c in range(2):
                bk = (i % 2) * 2 + c
                P.op("pe", "matmul", [t_pbuf[bk], R["t_V"][kt]], [tps[4 + c]], ps[4 + c][:, 0:n], lhsT=R["V"][:, kt, 0:128], rhs=pbuf[bk][:, 0:n],
                     start=(i == 0), stop=(i == len(kts) - 1))
                if i == 0:
                    P.op("dve", "tensor_copy", [t_pbuf[bk]], [t_sacc[c]], out=sacc[c][:, 0:n], in_=pbuf[bk][:, 0:n])
                else:
                    P.op("dve", "tensor_tensor", [t_pbuf[bk], t_sacc[c]], [t_sacc[c]], out=sacc[c][:, 0:n], in0=sacc[c][:, 0:n], in1=pbuf[bk][:, 0:n], op=ALU.add)
        for c in range(2):
            P.op("pe", "matmul", [t_sacc[c], t_ones], [tps[6]], ps[6][:, 0:n], lhsT=ones[:], rhs=sacc[c][:, 0:n], start=True, stop=True)
            P.op("dve", "reciprocal", [tps[6]], [t_rr[c]], out=rr[c][:, 0:n], in_=ps[6][:, 0:n])
        P.op("dve", "tensor_scalar", [t_rr[1], tsm], [t_rr[1]], out=rr[1][:, 0:n], in0=rr[1][:, 0:n], scalar1=sm[:, 4:5], scalar2=None, op0=ALU.mult)
        P.op("dve", "tensor_tensor", [tps[4], t_rr[0]], [t_osb], out=osb[:, 0:n], in0=ps[4][:, 0:n], in1=rr[0][:, 0:n], op=ALU.mult)
        P.op("dve", "tensor_tensor", [tps[5], t_rr[1]], [t_rr[1]], out=rr[1][:, 0:n], in0=ps[5][:, 0:n], in1=rr[1][:, 0:n], op=ALU.mult)
        P.op("dve", "tensor_tensor", [t_osb, t_rr[1]], [t_osb], out=osb[:, 0:n], in0=osb[:, 0:n], in1=rr[1][:, 0:n], op=ALU.add)
        P.op("act", "activation", [t_osb], [t_osq], out=osq[:, 0:n], in_=osb[:, 0:n], func=AF.Square)
        P.op("pe", "matmul", [t_osq, t_ones], [tps[6]], ps[6][:, 0:n], lhsT=ones[:], rhs=osq[:, 0:n], start=True, stop=True)
        P.op("act", "activation", [tps[6]], [t_osq], out=osq[:, 0:n], in_=ps[6][:, 0:n], func=AF.Ln, scale=1.0 / 128, bias=EPS)
        P.op("act", "activation", [t_osq], [t_osq], out=osq[:, 0:n], in_=osq[:, 0:n], func=AF.Exp, scale=-0.5)
        P.op("dve", "tensor_tensor", [t_osb, t_osq], [t_osb], out=osb[:, 0:n], in0=osb[:, 0:n], in1=osq[:, 0:n], op=ALU.mult)
        C._ys = getattr(C, "_ys", 0) + 1
        ys = R["ystage"][C._ys % 2]; tys = R["t_ystage"][C._ys % 2]
        P.op("dve", "tensor_tensor", [t_osb, t_zgT[ci]], [tys], out=ys[:, 0:n], in0=osb[:, 0:n], in1=zgT[:, t0:t0 + n], op=ALU.mult)
        P.dma("sp", [tys], [], yT_d[yrow:yrow + 128, t0:t0 + n], ys[:, 0:n])


def rms_scale(P, sm, tsm, ssq_col, out_col, n_feat):
    P.op("act", "activation", [tsm], [tsm], out=sm[:, out_col:out_col + 1], in_=sm[:, ssq_col:ssq_col + 1], func=AF.Ln, scale=1.0 / n_feat, bias=EPS)
    P.op("act", "activation", [tsm], [tsm], out=sm[:, out_col:out_col + 1], in_=sm[:, out_col:out_col + 1], func=AF.Exp, scale=-0.5)


def pass_B(C, R, l, hT_d, wB_d, convp_d, gateb_d, mlg_d, tri_d, yT_d, yrow=128):
    P = C.P
    sm = R["small"]; tsm = R["t_small"]
    load_weights(C, R, wB_d, 644)
    convp = C.sb([128, 8], F32, "convp"); gateb = C.sb([128, 4], F32, "gateb"); mlg = C.sb([128, 128], F32, "mlg")
    tri = C.sb([128, 384], F32, "tri")
    t_cst = TK()
    P.dma("sp", [], [t_cst], convp[:], convp_d[:, :])
    P.dma("sp", [], [t_cst], gateb[:], gateb_d[:, :])
    P.dma("sp", [], [t_cst], mlg[:], mlg_d[:, :])
    P.dma("sp", [], [t_cst], tri[:], tri_d[:, :])
    P.op(POOLC, "memset", [], R["t_V"], R["V"][:, :, 128:129], 1.0)
    Ktok = C.sb([128, NT, 128], BF16, "Ktok"); t_Ktok = [TK() for _ in range(NT)]
    hs = C.sb([128, NT, 128], F32, "hs"); t_hs = [TK() for _ in range(NT)]
    g4 = C.sb([128, 4, NT], F32, "g4"); t_g4 = TK()
    Rb = [C.sb([128, 520], F32, f"Rb{i}") for i in range(2)]; t_Rb = [TK(), TK()]
    ctmp = C.sb([128, 516], F32, "ctmp"); t_ctmp = TK()
    ctmp2 = C.sb([128, 516], F32, "ctmp2"); t_ctmp2 = TK()
    zt1 = C.sb([128, 128], F32, "zt1"); zt2 = C.sb([128, 128], F32, "zt2"); t_zt = TK()
    for i in range(2):
        P.op(POOLC, "memset", [], [t_Rb[i]], Rb[i][:], 0.0)

    def fm_cb(j, ci, t0, n, pb, tpb):
        rb = Rb[j]; trb = t_Rb[j]
        first = ci in (0, 1)
        last = ci in (0, 16)
        if first:
            P.op(POOLC, "memset", [], [trb], rb[:, 0:2], 0.0)
        if last:
            P.op(POOLC, "memset", [], [trb], rb[:, n + 2:n + 3], 0.0)
        P.op("act", "copy", [tpb], [trb], out=rb[:, 2:2 + n], in_=pb[:, 0:n])
        ta = t0 - 1 + (1 if first else 0)
        tb = t0 + n - 1 + (1 if last else 0)
        m = tb - ta
        o = ta - t0
        wc = 4 * j
        P.op("dve", "tensor_scalar", [trb, t_cst], [t_ctmp], out=ctmp[:, 0:m], in0=rb[:, o + 1:o + 1 + m], scalar1=convp[:, wc:wc + 1], scalar2=None, op0=ALU.mult)
        P.op("dve", "scalar_tensor_tensor", [trb, t_cst, t_ctmp], [t_ctmp], out=ctmp[:, 0:m], in0=rb[:, o + 2:o + 2 + m], scalar=convp[:, wc + 1:wc + 2],
             in1=ctmp[:, 0:m], op0=ALU.mult, op1=ALU.add)
        P.op("dve", "scalar_tensor_tensor", [trb, t_cst, t_ctmp], [t_ctmp], out=ctmp[:, 0:m], in0=rb[:, o + 3:o + 3 + m], scalar=convp[:, wc + 2:wc + 3],
             in1=ctmp[:, 0:m], op0=ALU.mult, op1=ALU.add)
        dst = R["qT"] if j == 0 else R["kT"]
        tds = R["t_qT"] if j == 0 else R["t_kT"]
        wr = [tds[ci]] + ([tds[ci - 1]] if not first else [])
        if j == 0:
            P.op("act", "activation", [t_ctmp, t_cst], [t_ctmp2], out=ctmp2[:, 0:m], in_=ctmp[:, 0:m], func=AF.Silu, bias=convp[:, 3:4])
            P.op("dve", "tensor_scalar_mul", [t_ctmp2], wr, out=dst[:, ta:tb], in0=ctmp2[:, 0:m], scalar1=128.0 ** -0.5)
        else:
            P.op("act", "activation", [t_ctmp, t_cst], wr, out=dst[:, ta:tb], in_=ctmp[:, 0:m], func=AF.Silu, bias=convp[:, 7:8])
        if not last:
            P.op(POOLC, "tensor_copy", [trb], [trb], out=rb[:, 0:2], in_=rb[:, n:n + 2])

    def tm_cb(ti, pb, tpb):
        P.op("act", "copy", [tpb], [R["t_V"][ti]], out=R["V"][:, ti, 0:128], in_=pb[:, 0:128])
        P.op("act", "activation", [tpb], [t_zt], out=zt1[:], in_=pb[:, 128:256], func=AF.Sigmoid)
        P.op("act", "activation", [tpb], [t_zt], out=zt2[:], in_=pb[:, 256:384], func=AF.Silu)
        P.op("dve", "tensor_tensor", [tpb, t_cst], [t_g4], out=g4[:, :, ti], in0=pb[:, 384:388], in1=gateb[:], op=ALU.add)
        P.op("dve", "tensor_tensor", [t_zt], [t_zt], out=zt1[:], in0=zt1[:], in1=zt2[:], op=ALU.mult)
        P.op("dve", "tensor_tensor", [t_zt, t_cst], [R["t_zg"][ti]], out=R["zg"][:, ti, :], in0=zt1[:], in1=mlg[:], op=ALU.mult)

    project(C, R, hT_d, [0, 128], (256, 388), fm_cb, tm_cb)

    for t0 in range(0, NT, 8):
        nn = min(8, NT - t0)
        kci = lambda ti: 0 if ti < 2 else 1 + (ti - 2) // 4
        for j in range(nn):
            ti = t0 + j
            P.op("pe", "transpose", [R["t_kT"][kci(ti)], R["t_ident"]], [R["t_pst"]], R["pst"][:, j * 128:(j + 1) * 128],
                 R["kT"][:, ti * 128:(ti + 1) * 128], R["ident"][:])
        P.op("dve", "tensor_copy", [R["t_pst"]], [t_Ktok[t0 + j] for j in range(nn)], out=Ktok[:, t0:t0 + nn, :],
             in_=R["pst"][:, 0:nn * 128].rearrange("p (a b) -> p a b", b=128))

    LFP = C.sb([128, 2, NT], F32, "LFP"); t_LFP = TK()
    GA = C.sb([128, 2, NT], F32, "GA"); GU = C.sb([128, 2, NT], F32, "GU"); GL = C.sb([128, 2, NT], F32, "GL"); t_G = TK()
    GAn = C.sb([128, 2, NT], F32, "GAn")
    for d in range(2):
        P.op("act", "activation", [t_g4], [t_LFP], out=LFP[:, d, :], in_=g4[:, 2 * d + 1, :], func=AF.Exp, scale=-1.0)
    for d in range(2):
        P.op("act", "activation", [t_LFP], [t_LFP], out=LFP[:, d, :], in_=LFP[:, d, :], func=AF.Ln, bias=1.0)
    pg = R["ps"][6]; tpg = R["t_ps"][6]
    for d in range(2):
        P.op("pe", "matmul", [t_LFP, t_cst], [tpg], pg[:, d * NT:(d + 1) * NT], lhsT=tri[:, d * 128:(d + 1) * 128], rhs=LFP[:, d, :], start=True, stop=True,
             skip_group_check=True)
        P.op("pe", "matmul", [t_LFP, t_cst], [tpg], pg[:, (2 + d) * NT:(3 + d) * NT], lhsT=tri[:, 256:384], rhs=LFP[:, d, :], start=True, stop=True,
             skip_group_check=True)
    for d in range(2):
        P.op("act", "activation", [tpg], [t_G], out=GA[:, d, :], in_=pg[:, d * NT:(d + 1) * NT], func=AF.Exp, scale=-1.0)
        P.op("dve", "tensor_tensor", [tpg, t_g4], [t_G], out=GU[:, d, :], in0=pg[:, d * NT:(d + 1) * NT], in1=g4[:, 2 * d, :], op=ALU.add)
        P.op("act", "activation", [t_G], [t_G], out=GU[:, d, :], in_=GU[:, d, :], func=AF.Exp)
        P.op("act", "activation", [tpg], [t_G], out=GL[:, d, :], in_=pg[:, (2 + d) * NT:(3 + d) * NT], func=AF.Exp, scale=-1.0)
        P.op("dve", "tensor_scalar_mul", [t_G], [t_G], out=GAn[:, d, :], in0=GA[:, d, :], scalar1=-1.0)

    Cst = [C.sb([128, 132], F32, f"Cst{d}") for d in range(2)]; t_Cst = [TK(), TK()]
    Cb = [C.sb([128, 132], BF16, f"Cb{d}") for d in range(2)]; t_Cb = [TK(), TK()]
    Vp = [[C.sb([128, 132], BF16, f"Vp{d}{i}") for i in range(2)] for d in range(2)]; t_Vp = [[TK(), TK()], [TK(), TK()]]
    Sm = [[C.sb([128, 128], BF16, f"Sm{d}{i}") for i in range(2)] for d in range(2)]; t_Sm = [[TK(), TK()], [TK(), TK()]]
    Up = [[C.sb([128, 132], F32, f"Up{d}{i}") for i in range(2)] for d in range(2)]; t_Up = [[TK(), TK()], [TK(), TK()]]
    for d in range(2):
        P.op(POOLC, "memset", [], [t_Cst[d]], Cst[d][:], 0.0)
        P.op(POOLC, "memset", [], [t_Cb[d]], Cb[d][:], 0.0)
    order = [list(range(NT)), [1, 0] + list(range(NT - 1, 1, -1))]
    inited = set()
    kci = lambda ti: 0 if ti < 2 else 1 + (ti - 2) // 4
    for step in range(NT):
        for d in range(2):
            ti = order[d][step]
            rr = step % 2
            vp = Vp[d][rr]; tvp = t_Vp[d][rr]; smb = Sm[d][rr]; tsmb = t_Sm[d][rr]
            psS = R["ps"][0 + d]; tpsS = R["t_ps"][0 + d]
            psA = R["ps"][2 + d]; tpsA = R["t_ps"][2 + d]
            psU = R["ps"][4 + d]; tpsU = R["t_ps"][4 + d]
            P.op("dve", "tensor_scalar", [R["t_V"][ti], t_G], [tvp], out=vp[:, 0:129], in0=R["V"][:, ti, 0:129], scalar1=GU[:, d, ti:ti + 1], scalar2=None, op0=ALU.mult)
            P.op("pe", "matmul", [R["t_kT"][kci(ti)], R["t_qT"][kci(ti)]], [tpsS], psS[:, 0:128], lhsT=R["kT"][:, ti * 128:(ti + 1) * 128],
                 rhs=R["qT"][:, ti * 128:(ti + 1) * 128], start=True, stop=True)
            P.op("dve", "tensor_tensor", [tpsS, t_cst], [tsmb], out=smb[:], in0=psS[:, 0:128], in1=tri[:, d * 128:(d + 1) * 128], op=ALU.mult)
            P.op("pe", "matmul", [tsmb, tvp], [tpsA], psA[:, 0:129], lhsT=smb[:], rhs=vp[:, 0:129], start=True, stop=False)
            P.op("pe", "matmul", [R["t_qT"][kci(ti)], t_Cb[d]], [tpsA], psA[:, 0:129], lhsT=R["qT"][:, ti * 128:(ti + 1) * 128], rhs=Cb[d][:, 0:129],
                 start=False, stop=True)
            P.op("pe", "matmul", [t_Ktok[ti], tvp], [tpsU], psU[:, 0:129], lhsT=Ktok[:, ti, :], rhs=vp[:, 0:129], start=True, stop=True)
            c0 = 20 + 4 * d
            P.op("dve", "tensor_scalar", [tpsA, t_G], [tsm], out=sm[:, c0:c0 + 1], in0=psA[:, 128:129], scalar1=GA[:, d, ti:ti + 1], scalar2=1.0,
                 op0=ALU.mult, op1=ALU.max)
            P.op("dve", "tensor_scalar", [tpsA, t_G], [tsm], out=sm[:, c0 + 2:c0 + 3], in0=psA[:, 128:129], scalar1=GAn[:, d, ti:ti + 1], scalar2=1.0,
                 op0=ALU.mult, op1=ALU.max)
            P.op("dve", "tensor_tensor", [tsm], [tsm], out=sm[:, c0:c0 + 1], in0=sm[:, c0:c0 + 1], in1=sm[:, c0 + 2:c0 + 3], op=ALU.max)
            P.op("dve", "reciprocal", [tsm], [tsm], out=sm[:, c0 + 1:c0 + 2], in_=sm[:, c0:c0 + 1])
            P.op("dve", "tensor_tensor", [tsm, t_G], [tsm], out=sm[:, c0 + 1:c0 + 2], in0=sm[:, c0 + 1:c0 + 2], in1=GA[:, d, ti:ti + 1], op=ALU.mult)
            if ti not in inited:
                inited.add(ti)
                P.op("dve", "tensor_scalar", [tpsA, tsm], [t_hs[ti]], out=hs[:, ti, :], in0=psA[:, 0:128], scalar1=sm[:, c0 + 1:c0 + 2], scalar2=None, op0=ALU.mult)
            else:
                P.op("dve", "scalar_tensor_tensor", [tpsA, tsm, t_hs[ti]], [t_hs[ti]], out=hs[:, ti, :], in0=psA[:, 0:128], scalar=sm[:, c0 + 1:c0 + 2],
                     in1=hs[:, ti, :], op0=ALU.mult, op1=ALU.add)
            up = Up[d][rr]; tup = t_Up[d][rr]
            P.op("act", "activation", [tpsU, t_G], [tup], out=up[:, 0:129], in_=psU[:, 0:129], func=AF.Copy, scale=GL[:, d, ti:ti + 1])
            P.op("dve", "scalar_tensor_tensor", [tup, t_G, t_Cst[d]], [t_Cst[d]], out=Cst[d][:, 0:129], in0=Cst[d][:, 0:129], scalar=GL[:, d, ti:ti + 1],
                 in1=up[:, 0:129], op0=ALU.mult, op1=ALU.add)
            P.op("act", "copy", [t_Cst[d]], [t_Cb[d]], out=Cb[d][:, 0:129], in_=Cst[d][:, 0:129])

    ysb = [C.sb([128, 128], BF16, f"ysbB{i}") for i in range(4)]; t_ysb = [TK() for _ in range(4)]
    junk = C.sb([128, 128], F32, "junkB"); t_junk = TK()
    for ci, (t0, n) in enumerate(CHUNKS):
        tiles = []
        for j in range(n // 128):
            ti = t0 // 128 + j
            P.op("act", "activation", [t_hs[ti]], [t_junk, tsm], out=junk[:], in_=hs[:, ti, :], func=AF.Square, accum_out=sm[:, 30:31])
            rms_scale(P, sm, tsm, 30, 31, 128)
            P.op("dve", "scalar_tensor_tensor", [t_hs[ti], tsm, R["t_zg"][ti]], [t_ysb[j]], out=ysb[j][:], in0=hs[:, ti, :], scalar=sm[:, 31:32],
                 in1=R["zg"][:, ti, :], op0=ALU.mult, op1=ALU.mult)
            tiles.append((ysb[j][:], t_ysb[j]))
        emit_yT(C, R, tiles, t0, n, yT_d, yrow)


def pass_C(C, R, l, hT_d, wC_d, biasg_d, maskc_d, yT_d, yrow=256):
    P = C.P
    sm = R["small"]; tsm = R["t_small"]
    load_weights(C, R, wC_d, 512)
    bias = C.sb([128, 6400], F32, "biasC"); t_bias = TK()
    mtmp = C.sb([128, 640], F32, "mtmpC"); t_mtmp = TK()
    P.dma("sp", [], [t_bias], bias[:, 0:3200], biasg_d[:, 0:3200])
    P.dma(POOLD, [], [t_bias], bias[:, 3200:6400], biasg_d[:, 3200:6400])
    for i in range(10):
        P.dma("sp", [], [t_mtmp], mtmp[:], maskc_d[:, i * 640:(i + 1) * 640])
        P.op("dve", "tensor_tensor", [t_mtmp, t_bias], [t_bias], out=bias[:, i * 640:(i + 1) * 640], in0=bias[:, i * 640:(i + 1) * 640], in1=mtmp[:], op=ALU.add)
    P.op(POOLC, "memset", [], R["t_V"], R["V"][:, :, 64:65], 1.0)
    P.op(POOLC, "memset", [], R["t_V"], R["V"][:, :, 129:130], 1.0)

    def fm_cb(j, ci, t0, n, pb, tpb):
        dst = R["qT"] if j == 0 else R["kT"]
        tds = R["t_qT"] if j == 0 else R["t_kT"]
        P.op("act", "copy", [tpb], [tds[ci]], out=dst[:, t0:t0 + n], in_=pb[:, 0:n])

    def tm_cb(ti, pb, tpb):
        P.op("dve", "tensor_copy", [tpb], [R["t_V"][ti]], out=R["V"][:, ti, 0:64], in_=pb[:, 0:64])
        P.op("dve", "tensor_copy", [tpb], [R["t_V"][ti]], out=R["V"][:, ti, 65:129], in_=pb[:, 64:128])
        P.op("act", "activation", [tpb], [R["t_zg"][ti]], out=R["zg"][:, ti, :], in_=pb[:, 128:256], func=AF.Silu)

    project(C, R, hT_d, [0, 128], (256, 256), fm_cb, tm_cb)

    tmpS = [C.sb([128, 640], F32, f"tmpS{i}") for i in range(2)]; t_tmpS = [TK(), TK()]
    Pl = [C.sb([128, 896], BF16, f"PlC{i}") for i in range(2)]; t_Pl = [TK(), TK()]
    ysb = [C.sb([128, 128], BF16, f"ysbC{i}") for i in range(4)]; t_ysb = [TK() for _ in range(4)]
    kci = lambda ti: 0 if ti < 2 else 1 + (ti - 2) // 4
    u = 0
    for ci, (t0, n) in enumerate(CHUNKS):
        tiles = []
        for jj in range(n // 128):
            ti = t0 // 128 + jj
            for hh in range(2):
                hs_ = slice(hh * 64, (hh + 1) * 64)
                bx = R["ps"][2 * (u % 2)]; tbx = R["t_ps"][2 * (u % 2)]
                by = R["ps"][2 * (u % 2) + 1]; tby = R["t_ps"][2 * (u % 2) + 1]
                pa = R["ps"][4 + u % 3]; tpa = R["t_ps"][4 + u % 3]
                pl = Pl[u % 2]; tpl = t_Pl[u % 2]
                ts_ = tmpS[u % 2]; tts = t_tmpS[u % 2]
                qsl = R["qT"][hs_, ti * 128:(ti + 1) * 128]
                if ti < 2:
                    keyt = []
                else:
                    j = ti - 2
                    kt0 = min(max(j - 2, 0), 59)
                    pat = 0 if j == 0 else 1 if j == 1 else 3 if j == 62 else 4 if j == 63 else 2
                    keyt = [2 + kt0 + a for a in range(5)]
                for a, kt in enumerate(keyt):
                    dst = bx[:, a * 128:(a + 1) * 128] if a < 4 else by[:, 0:128]
                    P.op("pe", "matmul", [R["t_kT"][kci(kt)], R["t_qT"][kci(ti)]], [tbx if a < 4 else tby], dst,
                         lhsT=R["kT"][hs_, kt * 128:(kt + 1) * 128], rhs=qsl, start=True, stop=True, skip_group_check=True)
                for a in range(2):
                    P.op("pe", "matmul", [R["t_kT"][0], R["t_qT"][kci(ti)]], [tby], by[:, (1 + a) * 128:(2 + a) * 128],
                         lhsT=R["kT"][hs_, a * 128:(a + 1) * 128], rhs=qsl, start=True, stop=True, skip_group_check=True)
                if keyt:
                    bo = (pat * 2 + hh) * 640
                    P.op("dve", "scalar_tensor_tensor", [tbx, t_bias], [tts], out=ts_[:, 0:512], in0=bx[:, 0:512], scalar=0.125, in1=bias[:, bo:bo + 512],
                         op0=ALU.mult, op1=ALU.add)
                    P.op("dve", "scalar_tensor_tensor", [tby, t_bias], [tts], out=ts_[:, 512:640], in0=by[:, 0:128], scalar=0.125, in1=bias[:, bo + 512:bo + 640],
                         op0=ALU.mult, op1=ALU.add)
                    P.op("act", "activation", [tts], [tpl], out=pl[:, 0:640], in_=ts_[:, 0:640], func=AF.Exp)
                P.op("act", "activation", [tby], [tpl], out=pl[:, 640:896], in_=by[:, 128:384], func=AF.Exp, scale=0.125)
                vs = slice(hh * 65, (hh + 1) * 65)
                allk = [(pl[:, a * 128:(a + 1) * 128], kt) for a, kt in enumerate(keyt)] + [(pl[:, 640 + a * 128:640 + (a + 1) * 128], a) for a in range(2)]
                for i, (pap, kt) in enumerate(allk):
                    P.op("pe", "matmul", [tpl, R["t_V"][kt]], [tpa], pa[:, 0:65], lhsT=pap, rhs=R["V"][:, kt, vs], start=(i == 0), stop=(i == len(allk) - 1))
                P.op("dve", "reciprocal", [tpa], [tsm], out=sm[:, 40:41], in_=pa[:, 64:65])
                P.op("dve", "scalar_tensor_tensor", [tpa, tsm, R["t_zg"][ti]], [t_ysb[jj]], out=ysb[jj][:, hs_], in0=pa[:, 0:64], scalar=sm[:, 40:41],
                     in1=R["zg"][:, ti, hs_], op0=ALU.mult, op1=ALU.mult)
                u += 1
            tiles.append((ysb[jj][:], t_ysb[jj]))
        emit_yT(C, R, tiles, t0, n, yT_d, yrow)


NTOK2 = 2304
CHUNKS2 = [(0, 256)] + [(256 + i * 512, 512) for i in range(4)]


def alloc_m2(C, G=None):
    M = dict(G) if G is not None else alloc_common(C)
    M["wst"] = [C.sb([128, 8, 128], F32, f"wst{i}") for i in range(2)]; M["t_wst"] = [TK(), TK()]
    M["xt"] = [C.sb([128, 1024], F32, f"xt{i}") for i in range(2)]; M["t_xt"] = [TK(), TK()]
    M["htmp"] = C.sb([128, 1024], F32, "htmp"); M["t_htmp"] = TK()
    M["hbf"] = C.sb([128, 1024], BF16, "hbf"); M["t_hbf"] = TK()
    M["hst"] = [C.sb([128, 8, 128], BF16, f"hst{i}") for i in range(2)]; M["t_hst"] = [TK(), TK()]
    M["junk"] = C.sb([128, 1024], F32, "junk"); M["t_junk"] = TK()
    M["mod"] = {}
    M["nrm"] = 0
    return M


def alloc_mod_out(C, M, key, parts):
    for part in parts:
        for v in range(2):
            M["mod"][(key, v, part)] = (C.sb([128, 1024], F32, f"mod_{key}_{v}_{part}"), TK())


def alloc_mod_tmp(C, M):
    M["wmst"] = C.sb([128, 8, 512], F32, "wmst"); M["t_wmst"] = TK()
    M["bst"] = C.sb([128, 512], F32, "bst"); M["t_bst"] = TK()
    M["screp"] = [C.sb([128, 8, 128], F32, f"screp{v}") for v in range(2)]; M["t_screp"] = TK()


def mod_vectors(C, M, key, crepL_d, crepC_d, wmod_d, bmodrep_d, parts, normgrep_d=None):
    P = C.P
    if "sc_done" not in M:
        M["sc_done"] = True
        for v, cd in enumerate((crepL_d, crepC_d)):
            P.dma("sp", [], [M["t_screp"]], M["screp"][v][:], cd[:, :, :])
            P.op("act", "activation", [M["t_screp"]], [M["t_screp"]], out=M["screp"][v][:], in_=M["screp"][v][:], func=AF.Silu)
    wv = wmod_d.rearrange("(k p) n -> p k n", p=128)
    pm = M["ps"][6]; tpm = M["t_ps"][6]
    if normgrep_d is not None:
        ng = C.sb([128, 1024], F32, "normg"); t_ng = TK()
        P.dma("sp", [], [t_ng], ng[:], normgrep_d[:, :])
    for part in parts:
        for half in range(2):
            c0 = part * 1024 + half * 512
            P.dma("sp", [], [M["t_wmst"]], M["wmst"][:, 0:4, :], wv[:, 0:4, c0:c0 + 512])
            P.dma(POOLD, [], [M["t_wmst"]], M["wmst"][:, 4:8, :], wv[:, 4:8, c0:c0 + 512])
            P.dma("sp", [], [M["t_bst"]], M["bst"][:], bmodrep_d[:, c0:c0 + 512])
            for v in range(2):
                dst, tdst = M["mod"][(key, v, part)]
                for k in range(8):
                    P.op("pe", "matmul", [M["t_screp"], M["t_wmst"]], [tpm], pm[:, :], lhsT=M["screp"][v][:, k, :], rhs=M["wmst"][:, k, :],
                         start=(k == 0), stop=(k == 7))
                P.op("dve", "tensor_tensor", [tpm, M["t_bst"]], [tdst], out=dst[:, half * 512:(half + 1) * 512], in0=pm[:, :], in1=M["bst"][:], op=ALU.add)
        if part == 1:
            for v in range(2):
                dst, tdst = M["mod"][(key, v, part)]
                P.op("dve", "scalar_tensor_tensor", [tdst, t_ng], [tdst], out=dst[:], in0=dst[:], scalar=1.0, in1=ng[:], op0=ALU.add, op1=ALU.mult)


def norm_tile(C, M, xt, t_xt, scale_bc, shift_bc, hT_view, i):
    P = C.P
    sm = M["small"]; tsm = M["t_small"]
    P.op("act", "activation", [t_xt], [M["t_junk"], tsm], out=M["junk"][:], in_=xt[:], func=AF.Square, accum_out=sm[:, 0:1])
    rms_scale(P, sm, tsm, 0, 1, 1024)
    P.op("dve", "scalar_tensor_tensor", [t_xt, tsm, scale_bc[1]], [M["t_htmp"]], out=M["htmp"][:], in0=xt[:], scalar=sm[:, 1:2], in1=scale_bc[0][:],
         op0=ALU.mult, op1=ALU.mult)
    P.op(POOLC, "tensor_tensor", [M["t_htmp"], shift_bc[1]], [M["t_hbf"]], out=M["hbf"][:], in0=M["htmp"][:], in1=shift_bc[0][:], op=ALU.add)
    M["nrm"] += 1
    hst = M["hst"][M["nrm"] % 2]; thst = M["t_hst"][M["nrm"] % 2]
    for k in range(8):
        P.op("pe", "transpose", [M["t_hbf"], M["t_ident"]], [M["t_pst"]], M["pst"][:, k * 128:(k + 1) * 128], M["hbf"][:, k * 128:(k + 1) * 128], M["ident"][:])
    P.op("act", "copy", [M["t_pst"]], [thst], out=hst[:], in_=M["pst"][:, :].rearrange("p (k t) -> p k t", t=128))
    P.dma("sp", [thst], [], hT_view[:, :, i * 128:(i + 1) * 128], hst[:])


def stage_N(C, M, key, x_d, hT_d, ntok=NTOK2):
    P = C.P
    hv = hT_d.rearrange("(k p) t -> p k t", p=128)
    for i in range(ntok // 128):
        v = 1 if i < 2 else 0
        xt = M["xt"][i % 2]; t_xt = M["t_xt"][i % 2]
        P.dma(POOLD, [], [t_xt], xt[:], x_d[i * 128:(i + 1) * 128, :])
        norm_tile(C, M, xt, t_xt, M["mod"][(key, v, 1)], M["mod"][(key, v, 0)], hv, i)


def load_w_bf16(C, M, w_d, dst, tdst, nk, ncols):
    P = C.P
    wv = w_d.rearrange("(k p) n -> p k n", p=128)
    i = 0
    for k0 in range(0, nk, 8):
        kn = min(8, nk - k0)
        for c0 in range(0, ncols, 128):
            st = M["wst"][i % 2]; tst = M["t_wst"][i % 2]
            P.dma(dma_rr(C), [], [tst], st[:, 0:kn, :], wv[:, k0:k0 + kn, c0:c0 + 128])
            P.op(POOLC, "tensor_copy", [tst], [tdst], out=dst[:, k0:k0 + kn, c0:c0 + 128], in_=st[:, 0:kn, :])
            i += 1


def stage_M2(C, M, key, x_d, hT_d, yT_d, wgm_d, wbr_d, wout_d, xout_d, next_key=None, hTn_d=None, finalg_d=None, out_d=None, chunks=None):
    P = C.P
    Wgm = C.sb([128, 8, 3072], BF16, "Wgm"); t_Wgm = TK()
    Wbr = C.sb([128, 12, 1024], BF16, "Wbr"); t_Wbr = TK()
    Wout = C.sb([128, 8, 1024], BF16, "Wout"); t_Wout = TK()
    load_w_bf16(C, M, wgm_d, Wgm, t_Wgm, 8, 3072)
    load_w_bf16(C, M, wbr_d, Wbr, t_Wbr, 12, 1024)
    load_w_bf16(C, M, wout_d, Wout, t_Wout, 8, 1024)
    hTc = C.sb([128, 8, 512], BF16, "hTc"); t_hTc = TK()
    yTc = C.sb([128, 12, 512], BF16, "yTc"); t_yTc = TK()
    uT = C.sb([128, 8, 512], BF16, "uT"); t_uT = TK()
    sg = C.sb([128, 512], F32, "sg"); t_sg = TK()
    uacc = C.sb([128, 512], F32, "uacc"); t_uacc = TK()
    ut = C.sb([128, 512], F32, "ut"); t_ut = TK()
    xn = [C.sb([128, 1024], F32, f"xn{i}") for i in range(2)]; t_xn = [TK(), TK()]
    if finalg_d is not None:
        fg = C.sb([128, 1024], F32, "fg"); t_fg = TK()
        P.dma("sp", [], [t_fg], fg[:], finalg_d[:, :])
        ost = [C.sb([128, 1024], F32, f"ost{i}") for i in range(2)]; t_ost = [TK(), TK()]
    hv = hT_d.rearrange("(k p) t -> p k t", p=128)
    yv = yT_d.rearrange("(k p) t -> p k t", p=128)
    hvn = hTn_d.rearrange("(k p) t -> p k t", p=128) if hTn_d is not None else None
    sm = M["small"]; tsm = M["t_small"]
    g = 0
    for ci, (t0, n) in enumerate(chunks or CHUNKS2):
        v = 1 if ci == 0 else 0
        gate, t_gate = M["mod"][(key, v, 2)]
        P.dma("sp", [], [t_hTc], hTc[:, :, 0:n], hv[:, :, t0:t0 + n])
        P.dma(POOLD, [], [t_yTc], yTc[:, :, 0:n], yv[:, :, t0:t0 + n])
        for oc in range(8):
            for nb in range(3):
                pg = M["ps"][g % 2]; tpg = M["t_ps"][g % 2]
                pp = M["ps"][2 + g % 2]; tpp = M["t_ps"][2 + g % 2]
                g += 1
                for k in range(8):
                    P.op("pe", "matmul", [t_Wgm, t_hTc], [tpg], pg[:, 0:n], lhsT=Wgm[:, k, nb * 1024 + oc * 128:nb * 1024 + (oc + 1) * 128], rhs=hTc[:, k, 0:n],
                         start=(k == 0), stop=(k == 7))
                for kk in range(4):
                    P.op("pe", "matmul", [t_Wbr, t_yTc], [tpp], pp[:, 0:n], lhsT=Wbr[:, nb * 4 + kk, oc * 128:(oc + 1) * 128], rhs=yTc[:, nb * 4 + kk, 0:n],
                         start=(kk == 0), stop=(kk == 3))
                P.op("act", "activation", [tpg], [t_sg], out=sg[:, 0:n], in_=pg[:, 0:n], func=AF.Sigmoid)
                if nb == 0:
                    P.op("dve", "tensor_tensor", [tpp, t_sg], [t_uacc], out=uacc[:, 0:n], in0=pp[:, 0:n], in1=sg[:, 0:n], op=ALU.mult)
                else:
                    P.op("dve", "tensor_tensor", [tpp, t_sg], [t_ut], out=ut[:, 0:n], in0=pp[:, 0:n], in1=sg[:, 0:n], op=ALU.mult)
                    if nb == 1:
                        P.op(POOLC, "tensor_tensor", [t_ut, t_uacc], [t_uacc], out=uacc[:, 0:n], in0=uacc[:, 0:n], in1=ut[:, 0:n], op=ALU.add)
                    else:
                        P.op(POOLC, "tensor_tensor", [t_ut, t_uacc], [t_uT], out=uT[:, oc, 0:n], in0=uacc[:, 0:n], in1=ut[:, 0:n], op=ALU.add)
        for tt in range(n // 128):
            i = t0 // 128 + tt
            xt = M["xt"][i % 2]; t_xt = M["t_xt"][i % 2]
            xo = xn[i % 2]; t_xo = t_xn[i % 2]
            P.dma(POOLD, [], [t_xt], xt[:], x_d[i * 128:(i + 1) * 128, :])
            for half in range(2):
                po = M["ps"][4 + half]; tpo = M["t_ps"][4 + half]
                cs = slice(half * 512, (half + 1) * 512)
                for oc in range(8):
                    P.op("pe", "matmul", [t_uT, t_Wout], [tpo], po[:, :], lhsT=uT[:, oc, tt * 128:(tt + 1) * 128], rhs=Wout[:, oc, cs], start=(oc == 0), stop=(oc == 7))
                P.op("dve", "tensor_tensor", [tpo, t_gate], [t_xo], out=xo[:, cs], in0=po[:, :], in1=gate[:, cs], op=ALU.mult)
                P.op(POOLC, "tensor_tensor", [t_xo, t_xt], [t_xo], out=xo[:, cs], in0=xo[:, cs], in1=xt[:, cs], op=ALU.add)
            if xout_d is not None:
                P.dma("sp", [t_xo], [], xout_d[i * 128:(i + 1) * 128, :], xo[:])
            if next_key is not None:
                norm_tile(C, M, xo, t_xo, M["mod"][(next_key, v, 1)], M["mod"][(next_key, v, 0)], hvn, i)
            if finalg_d is not None:
                os_ = ost[i % 2]; t_os = t_ost[i % 2]
                P.op("act", "activation", [t_xo], [M["t_junk"], tsm], out=M["junk"][:], in_=xo[:], func=AF.Square, accum_out=sm[:, 2:3])
                rms_scale(P, sm, tsm, 2, 3, 1024)
                P.op("dve", "scalar_tensor_tensor", [t_xo, tsm, t_fg], [t_os], out=os_[:], in0=xo[:], scalar=sm[:, 3:4], in1=fg[:], op0=ALU.mult, op1=ALU.mult)
                P.dma("sp", [t_os], [], out_d[i * 128:(i + 1) * 128, :], os_[:])


def rope_tables():
    t = np.arange(S)
    row = (t // GRID_W).astype(np.float32)
    col = (t % GRID_W).astype(np.float32)
    inv = (10000.0 ** (-np.arange(0, 32, 2, dtype=np.float32) / 32)).astype(np.float32)
    ang = np.concatenate([row[:, None] * inv, col[:, None] * inv], axis=-1).astype(np.float32)
    cos = np.cos(ang).astype(np.float32); sin = np.sin(ang).astype(np.float32)
    cosT = np.ones((128, T), np.float32); sinT = np.zeros((128, T), np.float32)
    for c in range(2):
        cosT[c * 64:c * 64 + 32, LC:] = cos.T
        cosT[c * 64 + 32:c * 64 + 64, LC:] = cos.T
        sinT[c * 64:c * 64 + 32, LC:] = -sin.T
        sinT[c * 64 + 32:c * 64 + 64, LC:] = sin.T
    return cosT, sinT


def wA_cols(g):
    cols = []
    ev = np.arange(0, 64, 2); od = np.arange(1, 64, 2)
    for base in (OFF["aq"], OFF["ak"]):
        main = []; swp = []
        for c in range(2):
            b0 = base + g * 128 + c * 64
            main += list(b0 + ev) + list(b0 + od)
            swp += list(b0 + od) + list(b0 + ev)
        cols += main + swp
    cols += list(OFF["az"] + g * 128 + np.arange(128)) + list(OFF["av"] + g * 128 + np.arange(128))
    return np.array(cols)


def rep128(v):
    return np.ascontiguousarray(np.broadcast_to(np.asarray(v, np.float32).reshape(1, -1), (128, v.size)))


def col128(v):
    return np.ascontiguousarray(np.broadcast_to(np.asarray(v, np.float32).reshape(-1, 1), (128, 128)))


def wB_cols(g):
    r = np.arange(128)
    cols = list(OFF["bq"] + g * 128 + r) + list(OFF["bk"] + g * 128 + r)
    cols += list(OFF["bv"] + g * 128 + r) + list(OFF["bo"] + g * 128 + r) + list(OFF["bz"] + g * 128 + r)
    cols += [OFF["bg"] + 0 + g, OFF["bg"] + 4 + g, OFF["bg"] + 8 + g, OFF["bg"] + 12 + g]
    return np.array(cols)


def convp_host(conv_w, conv_b, g):
    out = np.zeros((128, 8), np.float32)
    for j, base in enumerate((g * 128, 512 + g * 128)):
        out[:, 4 * j:4 * j + 3] = conv_w[:, base:base + 128].T
        out[:, 4 * j + 3] = conv_b[base:base + 128]
    return out


def tri_host():
    s = np.arange(128)
    triF = (s[:, None] <= s[None, :]).astype(np.float32)
    return np.ascontiguousarray(np.concatenate([triF, triF.T, np.ones((128, 128), np.float32)], axis=1))


def wC_cols(g):
    r = np.arange(128)
    return np.array(list(OFF["cq"] + g * 128 + r) + list(OFF["ck"] + g * 128 + r) + list(OFF["cv"] + g * 128 + r) + list(OFF["cz"] + g * 128 + r))


def na_geometry():
    pats = [(0, 0), (1, 0), (10, 8), (62, 59), (63, 59)]
    ridx = np.zeros((5, 5, 128, 128), np.int64); cidx = np.zeros_like(ridx); valid = np.zeros(ridx.shape, bool)
    ki = np.arange(128); qi = np.arange(128)
    for p, (j, kt0) in enumerate(pats):
        rq = 2 * j + qi // 64; cq = qi % 64
        rs = np.clip(rq - 4, 0, 120); cs = np.clip(cq - 8, 0, 48)
        for a in range(5):
            rk = 2 * (kt0 + a) + ki // 64; ck = ki % 64
            v = (rk[:, None] >= rs[None, :]) & (rk[:, None] < rs[None, :] + 8) & (ck[:, None] >= cs[None, :]) & (ck[:, None] < cs[None, :] + 16)
            valid[p, a] = v
            ridx[p, a] = np.clip(rk[:, None] - rq[None, :] + 7, 0, 14)
            cidx[p, a] = np.clip(ck[:, None] - cq[None, :] + 15, 0, 30)
    return ridx, cidx, valid


def na_bias_host(rpb, g):
    ridx, cidx, valid = na_geometry()
    out = np.zeros((128, 5, 2, 5, 128), np.float32)
    for hh in range(2):
        gathered = rpb[2 * g + hh][ridx, cidx]
        out[:, :, hh, :, :] = np.transpose(gathered, (2, 0, 1, 3))
    return np.ascontiguousarray(out.reshape(128, 6400))


def na_mask_host():
    ridx, cidx, valid = na_geometry()
    m = np.where(valid, 0.0, NEG).astype(np.float32)
    out = np.zeros((128, 5, 2, 5, 128), np.float32)
    for hh in range(2):
        out[:, :, hh, :, :] = np.transpose(m, (2, 0, 1, 3))
    return np.ascontiguousarray(out.reshape(128, 6400))


def build_N0():
    nc = bass.Bass("TRN2", target_bir_lowering=False)
    with contextlib.ExitStack() as es:
        C = Ctx(nc, es)
        x_d = C.dram("x_in", [NTOK2, D], F32, "ExternalInput")
        crepL = C.dram("crepL", [128, 8, 128], F32, "ExternalInput")
        crepC = C.dram("crepC", [128, 8, 128], F32, "ExternalInput")
        wmod = C.dram("wmod", [D, 3 * D], F32, "ExternalInput")
        bmod = C.dram("bmodrep", [128, 3 * D], F32, "ExternalInput")
        normg = C.dram("normgrep", [128, D], F32, "ExternalInput")
        ident_d = C.dram("ident", [128, 128], F32, "ExternalInput")
        hT_d = C.dram("hT_out", [D, NTOK2], BF16, "ExternalOutput")
        M = alloc_m2(C)
        load_ident(C, M, ident_d)
        alloc_mod_out(C, M, "n", [0, 1])
        outer = C.es
        with contextlib.ExitStack() as es2:
            C.es = es2
            alloc_mod_tmp(C, M)
            mod_vectors(C, M, "n", crepL, crepC, wmod, bmod, [0, 1], normg)
        C.es = outer
        C.P.barrier()
        stage_N(C, M, "n", x_d, hT_d)
        C.P.emit()
    return nc


def build_M1(l):
    nc = bass.Bass("TRN2", target_bir_lowering=False)
    with contextlib.ExitStack() as es:
        C = Ctx(nc, es)
        hT_d = C.dram("hT", [D, T], BF16, "ExternalInput")
        ident_d = C.dram("ident", [128, 128], F32, "ExternalInput")
        yT_d = C.dram("yT", [384, T], BF16, "ExternalOutput")
        wA_d = C.dram("wA", [D, 768], F32, "ExternalInput")
        cos_d = C.dram("cosT", [128, T], F32, "ExternalInput")
        sin_d = C.dram("sinT", [128, T], F32, "ExternalInput")
        lamq_d = C.dram("lamq", [128, 128], F32, "ExternalInput")
        lamk_d = C.dram("lamk", [128, 128], F32, "ExternalInput")
        dag_d = C.dram("dag", [128, 128], F32, "ExternalInput")
        wB_d = C.dram("wB", [D, 644], F32, "ExternalInput")
        convp_d = C.dram("convp", [128, 8], F32, "ExternalInput")
        gateb_d = C.dram("gateb", [128, 4], F32, "ExternalInput")
        mlg_d = C.dram("mlg", [128, 128], F32, "ExternalInput")
        tri_d = C.dram("tri", [128, 384], F32, "ExternalInput")
        wC_d = C.dram("wC", [D, 512], F32, "ExternalInput")
        biasg_d = C.dram("biasg", [128, 6400], F32, "ExternalInput")
        maskc_d = C.dram("maskc", [128, 6400], F32, "ExternalInput")
        R = alloc_m1(C)
        load_ident(C, R, ident_d)
        outer = C.es
        with contextlib.ExitStack() as es2:
            C.es = es2
            pass_A(C, R, l, hT_d, wA_d, cos_d, sin_d, lamq_d, lamk_d, dag_d, yT_d)
        C.P.barrier()
        with contextlib.ExitStack() as es2:
            C.es = es2
            pass_B(C, R, l, hT_d, wB_d, convp_d, gateb_d, mlg_d, tri_d, yT_d)
        C.P.barrier()
        with contextlib.ExitStack() as es2:
            C.es = es2
            pass_C(C, R, l, hT_d, wC_d, biasg_d, maskc_d, yT_d)
        C.es = outer
        C.P.emit()
    return nc


def build_M2(last):
    nc = bass.Bass("TRN2", target_bir_lowering=False)
    with contextlib.ExitStack() as es:
        C = Ctx(nc, es)
        x_d = C.dram("x_in", [NTOK2, D], F32, "ExternalInput")
        hT_d = C.dram("hT", [D, NTOK2], BF16, "ExternalInput")
        yT_d = C.dram("yT", [1536, NTOK2], BF16, "ExternalInput")
        wgm = C.dram("wgm", [D, 3 * D], F32, "ExternalInput")
        wbr = C.dram("wbr", [1536, D], F32, "ExternalInput")
        wout = C.dram("wout", [D, D], F32, "ExternalInput")
        crepL = C.dram("crepL", [128, 8, 128], F32, "ExternalInput")
        crepC = C.dram("crepC", [128, 8, 128], F32, "ExternalInput")
        wmod = C.dram("wmod", [D, 3 * D], F32, "ExternalInput")
        bmod = C.dram("bmodrep", [128, 3 * D], F32, "ExternalInput")
        ident_d = C.dram("ident", [128, 128], F32, "ExternalInput")
        M = alloc_m2(C)
        load_ident(C, M, ident_d)
        alloc_mod_out(C, M, "g", [2])
        if not last:
            alloc_mod_out(C, M, "n", [0, 1])
            wmod2 = C.dram("wmod2", [D, 3 * D], F32, "ExternalInput")
            bmod2 = C.dram("bmodrep2", [128, 3 * D], F32, "ExternalInput")
            normg2 = C.dram("normgrep2", [128, D], F32, "ExternalInput")
            xout_d = C.dram("x_out", [NTOK2, D], F32, "ExternalOutput")
            hTn_d = C.dram("hT_out", [D, NTOK2], BF16, "ExternalOutput")
        outer = C.es
        with contextlib.ExitStack() as es2:
            C.es = es2
            alloc_mod_tmp(C, M)
            mod_vectors(C, M, "g", crepL, crepC, wmod, bmod, [2])
            if not last:
                mod_vectors(C, M, "n", crepL, crepC, wmod2, bmod2, [0, 1], normg2)
        C.es = outer
        C.P.barrier()
        if not last:
            stage_M2(C, M, "g", x_d, hT_d, yT_d, wgm, wbr, wout, xout_d, next_key="n", hTn_d=hTn_d)
        else:
            fg = C.dram("finalg", [128, D], F32, "ExternalInput")
            out_d = C.dram("out", [NTOK2, D], F32, "ExternalOutput")
            stage_M2(C, M, "g", x_d, hT_d, yT_d, wgm, wbr, wout, None, finalg_d=fg, out_d=out_d)
        C.P.emit()
    return nc


def crep_host(c):
    a = np.asarray(c, np.float32).reshape(8, 128).T
    return np.ascontiguousarray(np.broadcast_to(a[:, :, None], (128, 8, 128)))


def assemble_hT(parts, b):
    return np.ascontiguousarray(np.concatenate([parts[(b, 0)][:, 0:256]] + [parts[(b, s)][:, 256:] for s in range(4)], axis=1))


def tok_slice(a, s):
    return np.concatenate([a[..., 0:256], a[..., 256 + s * 2048:256 + (s + 1) * 2048]], axis=-1)


CORES = [(b, i) for b in range(B) for i in range(4)]


def kernel_unfused(x, c, ctx, c_ctx, w_mod, b_mod, norm_g, w_in, da_lam_q, da_lam_k, da_norm_g,
           ml_conv_w, ml_conv_b, ml_gate_b, ml_norm_g, na_rpb, w_br, w_out, final_g):
    f32 = lambda a: np.ascontiguousarray(np.asarray(a, np.float32))
    x = f32(x); c = f32(c); ctx = f32(ctx); c_ctx = f32(c_ctx); w_mod = f32(w_mod); b_mod = f32(b_mod); norm_g = f32(norm_g)
    w_in = f32(w_in); w_br = f32(w_br); w_out = f32(w_out); final_g = f32(final_g)
    ident = np.eye(128, dtype=np.float32)
    cosT, sinT = rope_tables()
    tri = tri_host()
    maskc = na_mask_host()
    ids = list(range(8))
    crepC = crep_host(c_ctx)
    crepL = {b: crep_host(c[b]) for b in range(B)}

    x_cur = {(b, s): np.ascontiguousarray(np.concatenate([ctx[b], x[b, s * 2048:(s + 1) * 2048]], axis=0)) for (b, s) in CORES}
    ncA = build_N0()
    im = [dict(x_in=x_cur[(b, s)], crepL=crepL[b], crepC=crepC, wmod=w_mod[0], bmodrep=rep128(b_mod[0]), normgrep=rep128(norm_g[0]), ident=ident)
          for (b, s) in CORES]
    res = run_bass_kernel_spmd(ncA, im, core_ids=ids).results
    hparts = {CORES[i]: np.asarray(res[i]["hT_out"]) for i in range(8)}
    out = np.zeros((B, S, D), np.float32)
    for l in range(DEPTH):
        hT = {b: assemble_hT(hparts, b) for b in range(B)}
        nc1 = build_M1(l)
        im = []
        for (b, g) in CORES:
            gb = np.asarray(ml_gate_b[l], np.float32)
            im.append(dict(
                hT=hT[b], ident=ident,
                wA=np.ascontiguousarray(w_in[l][:, wA_cols(g)]), cosT=cosT, sinT=sinT,
                lamq=rep128(np.asarray(da_lam_q[l], np.float32).reshape(-1)), lamk=rep128(np.asarray(da_lam_k[l], np.float32).reshape(-1)),
                dag=col128(np.asarray(da_norm_g[l], np.float32)),
                wB=np.ascontiguousarray(w_in[l][:, wB_cols(g)]),
                convp=convp_host(np.asarray(ml_conv_w[l], np.float32), np.asarray(ml_conv_b[l], np.float32), g),
                gateb=rep128(np.array([gb[0, 0, g], gb[0, 1, g], gb[1, 0, g], gb[1, 1, g]], np.float32)),
                mlg=rep128(np.asarray(ml_norm_g[l], np.float32)), tri=tri,
                wC=np.ascontiguousarray(w_in[l][:, wC_cols(g)]), biasg=na_bias_host(np.asarray(na_rpb[l], np.float32), g), maskc=maskc))
        res = run_bass_kernel_spmd(nc1, im, core_ids=ids).results
        yT = {CORES[i]: np.asarray(res[i]["yT"]) for i in range(8)}
        last = (l == DEPTH - 1)
        nc2 = build_M2(last)
        im = []
        for (b, s) in CORES:
            yall = np.concatenate([yT[(b, g)][n * 128:(n + 1) * 128] for n in range(3) for g in range(4)], axis=0)
            d = dict(x_in=x_cur[(b, s)], hT=np.ascontiguousarray(tok_slice(hT[b], s)), yT=np.ascontiguousarray(tok_slice(yall, s)),
                     wgm=np.ascontiguousarray(w_in[l][:, OFF["gm"]:]), wbr=np.ascontiguousarray(w_br[l].reshape(1536, D)), wout=w_out[l],
                     crepL=crepL[b], crepC=crepC, wmod=w_mod[l], bmodrep=rep128(b_mod[l]), ident=ident)
            if not last:
                d.update(wmod2=w_mod[l + 1], bmodrep2=rep128(b_mod[l + 1]), normgrep2=rep128(norm_g[l + 1]))
            else:
                d.update(finalg=rep128(final_g))
            im.append(d)
        res = run_bass_kernel_spmd(nc2, im, core_ids=ids).results
        if not last:
            x_cur = {CORES[i]: np.asarray(res[i]["x_out"]) for i in range(8)}
            hparts = {CORES[i]: np.asarray(res[i]["hT_out"]) for i in range(8)}
        else:
            for i, (b, s) in enumerate(CORES):
                out[b, s * 2048:(s + 1) * 2048] = np.asarray(res[i]["out"])[256:]
    return out


@contextlib.contextmanager
def scope(C):
    outer = C.es
    with contextlib.ExitStack() as es2:
        C.es = es2
        try:
            yield
        finally:
            C.es = outer


def build_fused():
    nc = bass.Bass("TRN2", target_bir_lowering=False)
    with contextlib.ExitStack() as es:
        C = Ctx(nc, es)
        P = C.P
        ext = lambda name, shape, dt=F32: C.dram(name, shape, dt, "ExternalInput")
        x_in = ext("x_in", [T, D])
        crepL = ext("crepL", [128, 8, 128]); crepC = ext("crepC", [128, 8, 128]); ident_d = ext("ident", [128, 128])
        cos_d = ext("cosT", [128, T]); sin_d = ext("sinT", [128, T]); tri_d = ext("tri", [128, 384]); maskc_d = ext("maskc", [128, 6400])
        fg_d = ext("finalg", [128, D])
        L = []
        for l in range(DEPTH):
            L.append(dict(
                wmod=ext(f"wmod{l}", [D, 3 * D]), bmod=ext(f"bmodrep{l}", [128, 3 * D]), normg=ext(f"normgrep{l}", [128, D]),
                wA=ext(f"wA{l}", [4, D, 768]), wB=ext(f"wB{l}", [4, D, 644]), wC=ext(f"wC{l}", [4, D, 512]),
                lamq=ext(f"lamq{l}", [128, 128]), lamk=ext(f"lamk{l}", [128, 128]), dag=ext(f"dag{l}", [128, 128]),
                convp=ext(f"convp{l}", [4, 128, 8]), gateb=ext(f"gateb{l}", [4, 128, 4]), mlg=ext(f"mlg{l}", [128, 128]),
                biasg=ext(f"biasg{l}", [4, 128, 6400]),
                wgm=ext(f"wgm{l}", [D, 3 * D]), wbr=ext(f"wbr{l}", [1536, D]), wout=ext(f"wout{l}", [D, D])))
        out_d = C.dram("out", [T, D], F32, "ExternalOutput")
        hTs = [nc.dram_tensor(f"hT_scr{i}", [D, T], BF16).ap() for i in range(2)]
        yT = nc.dram_tensor("yT_scr", [1536, T], BF16).ap()
        xbuf = nc.dram_tensor("x_scr", [T, D], F32).ap()

        G = alloc_common(C)
        load_ident(C, G, ident_d)
        with scope(C):
            M = alloc_m2(C, G)
            alloc_mod_out(C, M, "n", [0, 1])
            with scope(C):
                alloc_mod_tmp(C, M)
                mod_vectors(C, M, "n", crepL, crepC, L[0]["wmod"], L[0]["bmod"], [0, 1], L[0]["normg"])
            P.barrier()
            stage_N(C, M, "n", x_in, hTs[0], ntok=T)
        P.barrier()
        for l in range(DEPTH):
            last = (l == DEPTH - 1)
            W = L[l]
            hT = hTs[l % 2]
            with scope(C):
                R = alloc_m1(C, G)
                for g in range(4):
                    with scope(C):
                        pass_A(C, R, l, hT, W["wA"][g], cos_d, sin_d, W["lamq"], W["lamk"], W["dag"], yT, yrow=0 * 512 + g * 128)
                    P.barrier()
                    with scope(C):
                        pass_B(C, R, l, hT, W["wB"][g], W["convp"][g], W["gateb"][g], W["mlg"], tri_d, yT, yrow=1 * 512 + g * 128)
                    P.barrier()
                    with scope(C):
                        pass_C(C, R, l, hT, W["wC"][g], W["biasg"][g], maskc_d, yT, yrow=2 * 512 + g * 128)
                    P.barrier()
            with scope(C):
                M = alloc_m2(C, G)
                alloc_mod_out(C, M, "g", [2])
                if not last:
                    alloc_mod_out(C, M, "n", [0, 1])
                with scope(C):
                    alloc_mod_tmp(C, M)
                    mod_vectors(C, M, "g", crepL, crepC, W["wmod"], W["bmod"], [2])
                    if not last:
                        mod_vectors(C, M, "n", crepL, crepC, L[l + 1]["wmod"], L[l + 1]["bmod"], [0, 1], L[l + 1]["normg"])
                P.barrier()
                x_src = x_in if l == 0 else xbuf
                if not last:
                    stage_M2(C, M, "g", x_src, hT, yT, W["wgm"], W["wbr"], W["wout"], xbuf, next_key="n", hTn_d=hTs[(l + 1) % 2], chunks=CHUNKS)
                else:
                    stage_M2(C, M, "g", x_src, hT, yT, W["wgm"], W["wbr"], W["wout"], None, finalg_d=fg_d, out_d=out_d, chunks=CHUNKS)
            P.barrier()
        print("[kernel] instruction counts", {k: len(v) for k, v in P.q.items()}, flush=True)
        P.emit()
    return nc


def kernel(x, c, ctx, c_ctx, w_mod, b_mod, norm_g, w_in, da_lam_q, da_lam_k, da_norm_g,
           ml_conv_w, ml_conv_b, ml_gate_b, ml_norm_g, na_rpb, w_br, w_out, final_g):
    f32 = lambda a: np.ascontiguousarray(np.asarray(a, np.float32))
    x = f32(x); c = f32(c); ctx = f32(ctx); c_ctx = f32(c_ctx); w_mod = f32(w_mod); b_mod = f32(b_mod); norm_g = f32(norm_g)
    w_in = f32(w_in); w_br = f32(w_br); w_out = f32(w_out); final_g = f32(final_g)
    cosT, sinT = rope_tables()
    shared = dict(ident=np.eye(128, dtype=np.float32), cosT=cosT, sinT=sinT, tri=tri_host(), maskc=na_mask_host(),
                  finalg=rep128(final_g), crepC=crep_host(c_ctx))
    for l in range(DEPTH):
        gb = np.asarray(ml_gate_b[l], np.float32)
        shared.update({
            f"wmod{l}": w_mod[l], f"bmodrep{l}": rep128(b_mod[l]), f"normgrep{l}": rep128(norm_g[l]),
            f"wA{l}": np.ascontiguousarray(np.stack([w_in[l][:, wA_cols(g)] for g in range(4)])),
            f"wB{l}": np.ascontiguousarray(np.stack([w_in[l][:, wB_cols(g)] for g in range(4)])),
            f"wC{l}": np.ascontiguousarray(np.stack([w_in[l][:, wC_cols(g)] for g in range(4)])),
            f"lamq{l}": rep128(np.asarray(da_lam_q[l], np.float32).reshape(-1)), f"lamk{l}": rep128(np.asarray(da_lam_k[l], np.float32).reshape(-1)),
            f"dag{l}": col128(np.asarray(da_norm_g[l], np.float32)),
            f"convp{l}": np.stack([convp_host(np.asarray(ml_conv_w[l], np.float32), np.asarray(ml_conv_b[l], np.float32), g) for g in range(4)]),
            f"gateb{l}": np.stack([rep128(np.array([gb[0, 0, g], gb[0, 1, g], gb[1, 0, g], gb[1, 1, g]], np.float32)) for g in range(4)]),
            f"mlg{l}": rep128(np.asarray(ml_norm_g[l], np.float32)),
            f"biasg{l}": np.stack([na_bias_host(np.asarray(na_rpb[l], np.float32), g) for g in range(4)]),
            f"wgm{l}": np.ascontiguousarray(w_in[l][:, OFF["gm"]:]), f"wbr{l}": np.ascontiguousarray(w_br[l].reshape(1536, D)), f"wout{l}": w_out[l]})
    im = []
    for b in range(B):
        d = dict(shared)
        d["x_in"] = np.ascontiguousarray(np.concatenate([ctx[b], x[b]], axis=0))
        d["crepL"] = crep_host(c[b])
        im.append(d)
    nc = build_fused()
    res = run_bass_kernel_spmd(nc, im, core_ids=list(range(B))).results
    return np.stack([np.asarray(res[b]["out"])[LC:] for b in range(B)]).astype(np.float32)
```

```python
import contextlib
import math
import numpy as np
import ml_dtypes
import concourse.bass as bass
import concourse.mybir as mybir
from concourse.bass_utils import run_bass_kernel_spmd

F32 = mybir.dt.float32
BF16 = mybir.dt.bfloat16
AF = mybir.ActivationFunctionType
ALU = mybir.AluOpType
AX = mybir.AxisListType

D = 1024
B = 2
S = 8192
LC = 256
T = S + LC
NT = T // 128
DEPTH = 2
EPS = 1e-6
NDSEM = 8
GRID_W = 64
NEG = -30000.0
import os
POOLC = os.environ.get("K_POOLC", "pool")
POOLD = os.environ.get("K_POOLD", "pool")

OFF = dict(aq=0, ak=512, av=1024, az=1536, bq=2048, bk=2560, bv=3072, bo=3584, bz=4096, bg=4608,
           cq=4624, ck=5136, cv=5648, cz=6160, gm=6672)
D_IN = 9744

CHUNKS = [(0, 256)] + [(256 + i * 512, 512) for i in range(16)]


class TK:
    __slots__ = ("w", "r", "excl")

    def __init__(self, excl=False):
        self.w = None
        self.r = []
        self.excl = excl


class Prog:
    def __init__(self, nc):
        self.nc = nc
        self.q = {e: [] for e in ("pe", "act", "dve", "pool", "sp")}
        self.cnt = {e: 0 for e in self.q}
        self.dcnt = {e: 0 for e in ("sp", "act", "pool")}
        self.seen = {e: {} for e in self.q}
        self.dma_tokens = []
        self.pending = {e: [] for e in self.q}

    def barrier(self):
        toks = [(("e", e), c) for e, c in self.cnt.items() if c > 0]
        last = {}
        for k, v in self.dma_tokens:
            last[k] = max(last.get(k, 0), v)
        toks += list(last.items())
        for e in self.q:
            self.pending[e] = list(toks)

    def _waits(self, eng, reads, writes):
        w = {}
        seen = self.seen[eng]

        def need(tok):
            if tok is None:
                return
            k, v = tok
            if eng == "pe" and k == ("e", "pe"):
                return
            if seen.get(k, 0) >= v:
                return
            if w.get(k, 0) < v:
                w[k] = v
        me = ("e", eng)
        if self.pending[eng]:
            for tok in self.pending[eng]:
                if tok[0] != me:
                    need(tok)
            self.pending[eng] = []
        for t in reads:
            need(t.w)
            if t.excl:
                for r in t.r:
                    if r[0] != me:
                        need(r)
        for t in writes:
            need(t.w)
            for r in t.r:
                need(r)
        for k, v in w.items():
            seen[k] = v
        return w

    def _update(self, tok, reads, writes):
        for t in reads:
            if len(t.r) > 24:
                m = {}
                for k, v in t.r:
                    if m.get(k, 0) < v:
                        m[k] = v
                t.r = list(m.items())
            t.r.append(tok)
        for t in writes:
            t.w = tok
            t.r = []

    def op(self, eng, meth, reads, writes, *a, **kw):
        w = self._waits(eng, reads, writes)
        self.cnt[eng] += 1
        tok = (("e", eng), self.cnt[eng])
        self.q[eng].append((w, (meth, a, kw), None))
        self._update(tok, reads, writes)
        return tok

    def dma(self, queue, reads, writes, out=None, in_=None, fn=None):
        if fn is None:
            fn = ("dma_start", (), dict(out=out, in_=in_))
        i = self.dcnt[queue]
        self.dcnt[queue] += 1
        s = i % NDSEM
        prev = 16 * (i // NDSEM)
        w = self._waits(queue, reads, writes)
        key = ("d", queue, s)
        if prev > 0 and self.seen[queue].get(key, 0) < prev:
            w[key] = prev
            self.seen[queue][key] = prev
        tok = (key, prev + 16)
        self.q[queue].append((w, fn, key))
        self._update(tok, reads, writes)
        self.dma_tokens.append(tok)
        return tok

    def emit(self):
        nc = self.nc
        with contextlib.ExitStack() as es:
            sems = {}
            for e in self.q:
                sems[("e", e)] = es.enter_context(nc.semaphore("s_" + e))
            for qn in self.dcnt:
                for s in range(NDSEM):
                    sems[("d", qn, s)] = es.enter_context(nc.semaphore(f"d_{qn}_{s}"))
            fin = {}
            for k, v in self.dma_tokens:
                fin[k] = max(fin.get(k, 0), v)
            block = es.enter_context(nc.Block())

            def run(name, e):
                for w, fn, dkey in self.q[name]:
                    for k, v in w.items():
                        e.wait_ge(sems[k], v)
                    inst = getattr(e, fn[0])(*fn[1], **fn[2])
                    if dkey is None:
                        inst.then_inc(sems[("e", name)], 1)
                    else:
                        inst.then_inc(sems[dkey], 16)
                if name == "sp":
                    for k, v in fin.items():
                        e.wait_ge(sems[k], v)

            @block.tensor
            def _(e):
                run("pe", e)

            @block.scalar
            def _(e):
                run("act", e)

            @block.vector
            def _(e):
                run("dve", e)

            @block.gpsimd
            def _(e):
                run("pool", e)

            @block.sync
            def _(e):
                run("sp", e)


class Ctx:
    def __init__(self, nc, es):
        self.nc = nc
        self.es = es
        self.P = Prog(nc)
        self.n = 0

    def sb(self, shape, dt, name=None):
        self.n += 1
        return self.es.enter_context(self.nc.sbuf_tensor(f"sb{self.n}_{name or 'x'}", list(shape), dt))

    def ps(self, shape, dt, name=None):
        self.n += 1
        return self.es.enter_context(self.nc.psum_tensor(f"ps{self.n}_{name or 'x'}", list(shape), dt))

    def dram(self, name, shape, dt, kind):
        return self.nc.dram_tensor(name, list(shape), dt, kind=kind).ap()


def dma_rr(C):
    C._rr = getattr(C, "_rr", 0) + 1
    return "sp" if C._rr % 2 else POOLD


def alloc_common(C):
    G = {}
    G["ident"] = C.sb([128, 128], BF16, "ident_sb"); G["t_ident"] = TK()
    G["identf"] = C.sb([128, 128], F32, "identf")
    G["ps"] = [C.ps([128, 512], F32, f"psb{i}") for i in range(7)]; G["t_ps"] = [TK(True) for _ in range(7)]
    G["pst"] = C.ps([128, 1024], BF16, "pstr"); G["t_pst"] = TK(True)
    G["small"] = C.sb([128, 64], F32, "small"); G["t_small"] = TK()
    return G


def alloc_m1(C, G=None):
    R = dict(G) if G is not None else alloc_common(C)
    R["qT"] = C.sb([128, T], BF16, "qT"); R["t_qT"] = [TK() for _ in range(17)]
    R["kT"] = C.sb([128, T], BF16, "kT"); R["t_kT"] = [TK() for _ in range(17)]
    R["V"] = C.sb([128, NT, 132], BF16, "Vaug"); R["t_V"] = [TK() for _ in range(NT)]
    R["zg"] = C.sb([128, NT, 128], BF16, "zg"); R["t_zg"] = [TK() for _ in range(NT)]
    R["hbuf"] = [C.sb([128, 8, 512], BF16, f"hbuf{i}") for i in range(2)]; R["t_hbuf"] = [TK(), TK()]
    R["wst"] = [C.sb([128, 8, 128], F32, f"wst{i}") for i in range(2)]; R["t_wst"] = [TK(), TK()]
    R["W"] = C.sb([128, 8, 768], BF16, "Wbf"); R["t_W"] = TK()
    R["ystage"] = [C.sb([128, 512], BF16, f"ystage{i}") for i in range(2)]; R["t_ystage"] = [TK(), TK()]
    return R


def load_ident(C, R, ident_d):
    P = C.P
    P.dma("sp", [], [R["t_ident"]], R["identf"][:], ident_d[:, :])
    P.op("dve", "tensor_copy", [R["t_ident"]], [R["t_ident"]], out=R["ident"][:], in_=R["identf"][:])


def load_weights(C, R, w_d, ncols):
    P = C.P
    wv = w_d.rearrange("(k p) n -> p k n", p=128)
    i = 0
    for c0 in range(0, ncols, 128):
        cn = min(128, ncols - c0)
        st = R["wst"][i % 2]; tst = R["t_wst"][i % 2]
        P.dma(dma_rr(C), [], [tst], st[:, :, 0:cn], wv[:, :, c0:c0 + cn])
        P.op(POOLC, "tensor_copy", [tst], [R["t_W"]], out=R["W"][:, :, c0:c0 + cn], in_=st[:, :, 0:cn])
        i += 1


def project(C, R, hT_d, fm_groups, tm_cols, fm_cb, tm_cb, pre_chunk=None):
    P = C.P
    hv = hT_d.rearrange("(k p) t -> p k t", p=128)
    W = R["W"]
    for ci, (t0, n) in enumerate(CHUNKS):
        hb = R["hbuf"][ci % 2]; thb = R["t_hbuf"][ci % 2]
        P.dma("sp", [], [thb], hb[:, 0:4, 0:n], hv[:, 0:4, t0:t0 + n])
        P.dma(POOLD, [], [thb], hb[:, 4:8, 0:n], hv[:, 4:8, t0:t0 + n])
        if pre_chunk is not None:
            pre_chunk(ci, t0, n)
        for j, c0 in enumerate(fm_groups):
            pb = R["ps"][j]; tpb = R["t_ps"][j]
            for k in range(8):
                P.op("pe", "matmul", [R["t_W"], thb], [tpb], pb[:, 0:n], lhsT=W[:, k, c0:c0 + 128], rhs=hb[:, k, 0:n],
                     start=(k == 0), stop=(k == 7))
            fm_cb(j, ci, t0, n, pb, tpb)
        if tm_cols is not None:
            c0, nc_ = tm_cols
            for tt in range(n // 128):
                ti = t0 // 128 + tt
                bi = max(4, len(fm_groups)) + (ti % 2)
                pb = R["ps"][bi]; tpb = R["t_ps"][bi]
                for k in range(8):
                    P.op("pe", "matmul", [R["t_W"], thb], [tpb], pb[:, 0:nc_], lhsT=hb[:, k, tt * 128:(tt + 1) * 128],
                         rhs=W[:, k, c0:c0 + nc_], start=(k == 0), stop=(k == 7))
                tm_cb(ti, pb, tpb)


def emit_yT(C, R, src_tiles, t0, n, yT_d, row0):
    P = C.P
    C._ys = getattr(C, "_ys", 0) + 1
    ys = R["ystage"][C._ys % 2]; tys = R["t_ystage"][C._ys % 2]
    for j, (ap, tk) in enumerate(src_tiles):
        P.op("pe", "transpose", [tk, R["t_ident"]], [R["t_pst"]], R["pst"][:, j * 128:(j + 1) * 128], ap, R["ident"][:])
    P.op("dve", "tensor_copy", [R["t_pst"]], [tys], out=ys[:, 0:n], in_=R["pst"][:, 0:n])
    P.dma("sp", [tys], [], yT_d[row0:row0 + 128, t0:t0 + n], ys[:, 0:n])


def pass_A(C, R, l, hT_d, wA_d, cos_d, sin_d, lamq_d, lamk_d, dag_d, yT_d, yrow=0):
    P = C.P
    lam_init = 0.8 - 0.6 * math.exp(-0.3 * l)
    load_weights(C, R, wA_d, 768)
    sm = R["small"]; tsm = R["t_small"]
    lq = C.sb([128, 128], F32, "lq"); lk = C.sb([128, 128], F32, "lk"); dag = C.sb([128, 128], F32, "dag")
    ones = C.sb([128, 128], F32, "onesA"); t_ones = TK()
    t_l = TK(); t_dag = TK()
    P.dma("sp", [], [t_l], lq[:], lamq_d[:, :])
    P.dma("sp", [], [t_l], lk[:], lamk_d[:, :])
    P.dma("sp", [], [t_dag], dag[:], dag_d[:, :])
    P.op(POOLC, "memset", [], [t_ones], ones[:], 1.0)
    P.op("dve", "tensor_tensor", [t_l], [t_l], out=lq[:], in0=lq[:], in1=lk[:], op=ALU.mult)
    P.op("dve", "reduce_sum", [t_l], [tsm], out=sm[:, 0:2], in_=lq[:].rearrange("p (c d) -> p c d", c=2), axis=AX.X)
    P.op("act", "activation", [tsm], [tsm], out=sm[:, 2:4], in_=sm[:, 0:2], func=AF.Exp)
    P.op("dve", "tensor_tensor", [tsm], [tsm], out=sm[:, 4:5], in0=sm[:, 3:4], in1=sm[:, 2:3], op=ALU.subtract)
    P.op("dve", "tensor_scalar_add", [tsm], [tsm], out=sm[:, 4:5], in0=sm[:, 4:5], scalar1=-lam_init)
    P.op("dve", "tensor_scalar_mul", [t_dag], [t_dag], out=dag[:], in0=dag[:], scalar1=(1.0 - lam_init))

    q0z = R["qT"]
    q1z = C.sb([128, T], BF16, "q1z")
    P.op(POOLC, "memset", [], R["t_qT"], q0z[64:128, :], 0.0)
    P.op(POOLC, "memset", [], R["t_qT"], q1z[0:64, :], 0.0)
    zgT = R["zg"][:].rearrange("p a b -> p (a b)")
    t_zgT = R["t_zg"][:17]

    cosb = [C.sb([128, 512], F32, f"cosb{i}") for i in range(2)]
    sinb = [C.sb([128, 512], F32, f"sinb{i}") for i in range(2)]
    t_cs = [TK(), TK()]
    tmp1 = C.sb([128, 512], F32, "ropet1"); tmp2 = C.sb([128, 512], F32, "ropet2"); t_tmp = TK()
    ztmp = C.sb([128, 512], F32, "ztmp"); t_ztmp = TK()

    def pre_chunk(ci, t0, n):
        P.dma("sp", [], [t_cs[ci % 2]], cosb[ci % 2][:, 0:n], cos_d[:, t0:t0 + n])
        P.dma(POOLD, [], [t_cs[ci % 2]], sinb[ci % 2][:, 0:n], sin_d[:, t0:t0 + n])

    def fm_cb(j, ci, t0, n, pb, tpb):
        if j == 4:
            P.op("act", "activation", [tpb], [t_ztmp], out=ztmp[:, 0:n], in_=pb[:, 0:n], func=AF.Silu)
            P.op("dve", "tensor_scalar", [t_ztmp, t_dag], [t_zgT[ci]], out=zgT[:, t0:t0 + n], in0=ztmp[:, 0:n], scalar1=dag[:, 0:1], scalar2=None, op0=ALU.mult)
        elif j % 2 == 0:
            P.op("dve", "tensor_tensor", [tpb, t_cs[ci % 2]], [t_tmp], out=tmp1[:, 0:n], in0=pb[:, 0:n], in1=cosb[ci % 2][:, 0:n], op=ALU.mult)
        else:
            P.op("dve", "tensor_tensor", [tpb, t_cs[ci % 2]], [t_tmp], out=tmp2[:, 0:n], in0=pb[:, 0:n], in1=sinb[ci % 2][:, 0:n], op=ALU.mult)
            if j == 1:
                P.op("dve", "tensor_tensor", [t_tmp], [R["t_qT"][ci]], out=q0z[0:64, t0:t0 + n], in0=tmp1[0:64, 0:n], in1=tmp2[0:64, 0:n], op=ALU.add)
                P.op("dve", "tensor_tensor", [t_tmp], [R["t_qT"][ci]], out=q1z[64:128, t0:t0 + n], in0=tmp1[64:128, 0:n], in1=tmp2[64:128, 0:n], op=ALU.add)
            else:
                P.op("dve", "tensor_tensor", [t_tmp], [R["t_kT"][ci]], out=R["kT"][:, t0:t0 + n], in0=tmp1[:, 0:n], in1=tmp2[:, 0:n], op=ALU.add)

    def tm_cb(ti, pb, tpb):
        P.op("act", "copy", [tpb], [R["t_V"][ti]], out=R["V"][:, ti, 0:128], in_=pb[:, 0:128])

    project(C, R, hT_d, [0, 128, 256, 384, 512], (640, 128), fm_cb, tm_cb, pre_chunk)

    qz = [q0z, q1z]
    pbuf = [C.sb([128, 512], BF16, f"pbuf{i}") for i in range(4)]; t_pbuf = [TK() for _ in range(4)]
    sacc = [C.sb([128, 512], F32, f"sacc{c}") for c in range(2)]; t_sacc = [TK(), TK()]
    rr = [C.sb([128, 512], F32, f"rr{c}") for c in range(2)]; t_rr = [TK(), TK()]
    osb = C.sb([128, 512], F32, "osb"); t_osb = TK()
    osq = C.sb([128, 512], F32, "osq"); t_osq = TK()
    kci = lambda kt: 0 if kt < 2 else 1 + (kt - 2) // 4
    ps = R["ps"]; tps = R["t_ps"]
    for ci, (t0, n) in enumerate(CHUNKS):
        kts = [0, 1] if ci == 0 else list(range(NT))

        def s_mm(i):
            kt = kts[i]
            for c in range(2):
                bk = (i % 2) * 2 + c
                P.op("pe", "matmul", [R["t_kT"][kci(kt)], R["t_qT"][ci]], [tps[bk]], ps[bk][:, 0:n],
                     lhsT=R["kT"][:, kt * 128:(kt + 1) * 128], rhs=qz[c][:, t0:t0 + n], start=True, stop=True)

        s_mm(0)
        for i, kt in enumerate(kts):
            if i + 1 < len(kts):
                s_mm(i + 1)
            for c in range(2):
                bk = (i % 2) * 2 + c
                P.op("act", "activation", [tps[bk]], [t_pbuf[bk]], out=pbuf[bk][:, 0:n], in_=ps[bk][:, 0:n], func=AF.Exp, scale=0.125)
            for c in range(2):
                bk = (i % 2) * 2 + c
                P.op("pe", "matmul", [t_pbuf[bk], R["t_V"][kt]], [tps[4 + c]], ps[4 + c][:, 0:n], lhsT=R["V"][:, kt, 0:128], rhs=pbuf[bk][:, 0:n],
                     start=(i == 0), stop=(i == len(kts) - 1))
                if i == 0:
                    P.op("dve", "tensor_copy", [t_pbuf[bk]], [t_sacc[c]], out=sacc[c][:, 0:n], in_=pbuf[bk][:, 0:n])
                else:
                    P.op("dve", "tensor_tensor", [t_pbuf[bk], t_sacc[c]], [t_sacc[c]], out=sacc[c][:, 0:n], in0=sacc[c][:, 0:n], in1=pbuf[bk][:, 0:n], op=ALU.add)
        for c in range(2):
            P.op("pe", "matmul", [t_sacc[c], t_ones], [tps[6]], ps[6][:, 0:n], lhsT=ones[:], rhs=sacc[c][:, 0:n], start=True, stop=True)
            P.op("dve", "reciprocal", [tps[6]], [t_rr[c]], out=rr[c][:, 0:n], in_=ps[6][:, 0:n])
        P.op("dve", "tensor_scalar", [t_rr[1], tsm], [t_rr[1]], out=rr[1][:, 0:n], in0=rr[1][:, 0:n], scalar1=sm[:, 4:5], scalar2=None, op0=ALU.mult)
        P.op("dve", "tensor_tensor", [tps[4], t_rr[0]], [t_osb], out=osb[:, 0:n], in0=ps[4][:, 0:n], in1=rr[0][:, 0:n], op=ALU.mult)
        P.op("dve", "tensor_tensor", [tps[5], t_rr[1]], [t_rr[1]], out=rr[1][:, 0:n], in0=ps[5][:, 0:n], in1=rr[1][:, 0:n], op=ALU.mult)
        P.op("dve", "tensor_tensor", [t_osb, t_rr[1]], [t_osb], out=osb[:, 0:n], in0=osb[:, 0:n], in1=rr[1][:, 0:n], op=ALU.add)
        P.op("act", "activation", [t_osb], [t_osq], out=osq[:, 0:n], in_=osb[:, 0:n], func=AF.Square)
        P.op("pe", "matmul", [t_osq, t_ones], [tps[6]], ps[6][:, 0:n], lhsT=ones[:], rhs=osq[:, 0:n], start=True, stop=True)
        P.op("act", "activation", [tps[6]], [t_osq], out=osq[:, 0:n], in_=ps[6][:, 0:n], func=AF.Ln, scale=1.0 / 128, bias=EPS)
        P.op("act", "activation", [t_osq], [t_osq], out=osq[:, 0:n], in_=osq[:, 0:n], func=AF.Exp, scale=-0.5)
        P.op("dve", "tensor_tensor", [t_osb, t_osq], [t_osb], out=osb[:, 0:n], in0=osb[:, 0:n], in1=osq[:, 0:n], op=ALU.mult)
        C._ys = getattr(C, "_ys", 0) + 1
        ys = R["ystage"][C._ys % 2]; tys = R["t_ystage"][C._ys % 2]
        P.op("dve", "tensor_tensor", [t_osb, t_zgT[ci]], [tys], out=ys[:, 0:n], in0=osb[:, 0:n], in1=zgT[:, t0:t0 + n], op=ALU.mult)
        P.dma("sp", [tys], [], yT_d[yrow:yrow + 128, t0:t0 + n], ys[:, 0:n])


def rms_scale(P, sm, tsm, ssq_col, out_col, n_feat):
    P.op("act", "activation", [tsm], [tsm], out=sm[:, out_col:out_col + 1], in_=sm[:, ssq_col:ssq_col + 1], func=AF.Ln, scale=1.0 / n_feat, bias=EPS)
    P.op("act", "activation", [tsm], [tsm], out=sm[:, out_col:out_col + 1], in_=sm[:, out_col:out_col + 1], func=AF.Exp, scale=-0.5)


def pass_B(C, R, l, hT_d, wB_d, convp_d, gateb_d, mlg_d, tri_d, yT_d, yrow=128):
    P = C.P
    sm = R["small"]; tsm = R["t_small"]
    load_weights(C, R, wB_d, 644)
    convp = C.sb([128, 8], F32, "convp"); gateb = C.sb([128, 4], F32, "gateb"); mlg = C.sb([128, 128], F32, "mlg")
    tri = C.sb([128, 384], F32, "tri")
    t_cst = TK()
    P.dma("sp", [], [t_cst], convp[:], convp_d[:, :])
    P.dma("sp", [], [t_cst], gateb[:], gateb_d[:, :])
    P.dma("sp", [], [t_cst], mlg[:], mlg_d[:, :])
    P.dma("sp", [], [t_cst], tri[:], tri_d[:, :])
    P.op(POOLC, "memset", [], R["t_V"], R["V"][:, :, 128:129], 1.0)
    Ktok = C.sb([128, NT, 128], BF16, "Ktok"); t_Ktok = [TK() for _ in range(NT)]
    hs = C.sb([128, NT, 128], F32, "hs"); t_hs = [TK() for _ in range(NT)]
    g4 = C.sb([128, 4, NT], F32, "g4"); t_g4 = TK()
    Rb = [C.sb([128, 520], F32, f"Rb{i}") for i in range(2)]; t_Rb = [TK(), TK()]
    ctmp = C.sb([128, 516], F32, "ctmp"); t_ctmp = TK()
    ctmp2 = C.sb([128, 516], F32, "ctmp2"); t_ctmp2 = TK()
    zt12 = C.sb([128, 256], F32, "zt12"); t_zt = TK()
    for i in range(2):
        P.op(POOLC, "memset", [], [t_Rb[i]], Rb[i][:], 0.0)

    def fm_cb(j, ci, t0, n, pb, tpb):
        rb = Rb[j]; trb = t_Rb[j]
        first = ci in (0, 1)
        last = ci in (0, 16)
        if first:
            P.op(POOLC, "memset", [], [trb], rb[:, 0:2], 0.0)
        if last:
            P.op(POOLC, "memset", [], [trb], rb[:, n + 2:n + 3], 0.0)
        P.op("act", "copy", [tpb], [trb], out=rb[:, 2:2 + n], in_=pb[:, 0:n])
        ta = t0 - 1 + (1 if first else 0)
        tb = t0 + n - 1 + (1 if last else 0)
        m = tb - ta
        o = ta - t0
        wc = 4 * j
        P.op("dve", "tensor_scalar", [trb, t_cst], [t_ctmp], out=ctmp[:, 0:m], in0=rb[:, o + 1:o + 1 + m], scalar1=convp[:, wc:wc + 1], scalar2=None, op0=ALU.mult)
        P.op("dve", "scalar_tensor_tensor", [trb, t_cst, t_ctmp], [t_ctmp], out=ctmp[:, 0:m], in0=rb[:, o + 2:o + 2 + m], scalar=convp[:, wc + 1:wc + 2],
             in1=ctmp[:, 0:m], op0=ALU.mult, op1=ALU.add)
        P.op("dve", "scalar_tensor_tensor", [trb, t_cst, t_ctmp], [t_ctmp], out=ctmp[:, 0:m], in0=rb[:, o + 3:o + 3 + m], scalar=convp[:, wc + 2:wc + 3],
             in1=ctmp[:, 0:m], op0=ALU.mult, op1=ALU.add)
        dst = R["qT"] if j == 0 else R["kT"]
        tds = R["t_qT"] if j == 0 else R["t_kT"]
        wr = [tds[ci]] + ([tds[ci - 1]] if not first else [])
        bc = convp[:, 3:4] if j == 0 else convp[:, 7:8]
        P.op("act", "activation", [t_ctmp, t_cst], [t_ctmp2], out=ctmp2[:, 0:m], in_=ctmp[:, 0:m], func=AF.Sigmoid, bias=bc)
        if j == 0:
            P.op("dve", "scalar_tensor_tensor", [t_ctmp, t_ctmp2, t_cst], [t_ctmp2], out=ctmp2[:, 0:m], in0=ctmp[:, 0:m], scalar=bc, in1=ctmp2[:, 0:m],
                 op0=ALU.add, op1=ALU.mult)
            P.op("dve", "tensor_scalar_mul", [t_ctmp2], wr, out=dst[:, ta:tb], in0=ctmp2[:, 0:m], scalar1=128.0 ** -0.5)
        else:
            P.op("dve", "scalar_tensor_tensor", [t_ctmp, t_ctmp2, t_cst], wr, out=dst[:, ta:tb], in0=ctmp[:, 0:m], scalar=bc, in1=ctmp2[:, 0:m],
                 op0=ALU.add, op1=ALU.mult)
        if not last:
            P.op(POOLC, "tensor_copy", [trb], [trb], out=rb[:, 0:2], in_=rb[:, n:n + 2])

    def tm_cb(ti, pb, tpb):
        P.op("act", "copy", [tpb], [R["t_V"][ti]], out=R["V"][:, ti, 0:128], in_=pb[:, 0:128])
        P.op("act", "activation", [tpb], [t_zt], out=zt12[:], in_=pb[:, 128:384], func=AF.Sigmoid)
        P.op("dve", "tensor_tensor", [tpb, t_cst], [t_g4], out=g4[:, :, ti], in0=pb[:, 384:388], in1=gateb[:], op=ALU.add)
        P.op("dve", "tensor_tensor", [t_zt], [t_zt], out=zt12[:, 0:128], in0=zt12[:, 0:128], in1=zt12[:, 128:256], op=ALU.mult)
        P.op("dve", "tensor_tensor", [tpb, t_zt], [t_zt], out=zt12[:, 0:128], in0=pb[:, 256:384], in1=zt12[:, 0:128], op=ALU.mult)
        P.op("dve", "tensor_tensor", [t_zt, t_cst], [R["t_zg"][ti]], out=R["zg"][:, ti, :], in0=zt12[:, 0:128], in1=mlg[:], op=ALU.mult)

    project(C, R, hT_d, [0, 128], (256, 388), fm_cb, tm_cb)

    for t0 in range(0, NT, 8):
        nn = min(8, NT - t0)
        kci = lambda ti: 0 if ti < 2 else 1 + (ti - 2) // 4
        for j in range(nn):
            ti = t0 + j
            P.op("pe", "transpose", [R["t_kT"][kci(ti)], R["t_ident"]], [R["t_pst"]], R["pst"][:, j * 128:(j + 1) * 128],
                 R["kT"][:, ti * 128:(ti + 1) * 128], R["ident"][:])
        P.op("dve", "tensor_copy", [R["t_pst"]], [t_Ktok[t0 + j] for j in range(nn)], out=Ktok[:, t0:t0 + nn, :],
             in_=R["pst"][:, 0:nn * 128].rearrange("p (a b) -> p a b", b=128))

    LFP = C.sb([128, 2, NT], F32, "LFP"); t_LFP = TK()
    GA = C.sb([128, 2, NT], F32, "GA"); GU = C.sb([128, 2, NT], F32, "GU"); GL = C.sb([128, 2, NT], F32, "GL"); t_G = TK()
    GAi = C.sb([128, 2, NT], F32, "GAi")
    hst_ = C.sb([128, 128], F32, "hstmp"); t_hst = TK()
    for d in range(2):
        P.op("act", "activation", [t_g4], [t_LFP], out=LFP[:, d, :], in_=g4[:, 2 * d + 1, :], func=AF.Exp, scale=-1.0)
    for d in range(2):
        P.op("act", "activation", [t_LFP], [t_LFP], out=LFP[:, d, :], in_=LFP[:, d, :], func=AF.Ln, bias=1.0)
    pg = R["ps"][6]; tpg = R["t_ps"][6]
    for d in range(2):
        P.op("pe", "matmul", [t_LFP, t_cst], [tpg], pg[:, d * NT:(d + 1) * NT], lhsT=tri[:, d * 128:(d + 1) * 128], rhs=LFP[:, d, :], start=True, stop=True,
             skip_group_check=True)
        P.op("pe", "matmul", [t_LFP, t_cst], [tpg], pg[:, (2 + d) * NT:(3 + d) * NT], lhsT=tri[:, 256:384], rhs=LFP[:, d, :], start=True, stop=True,
             skip_group_check=True)
    for d in range(2):
        P.op("act", "activation", [tpg], [t_G], out=GA[:, d, :], in_=pg[:, d * NT:(d + 1) * NT], func=AF.Exp, scale=-1.0)
        P.op("dve", "tensor_tensor", [tpg, t_g4], [t_G], out=GU[:, d, :], in0=pg[:, d * NT:(d + 1) * NT], in1=g4[:, 2 * d, :], op=ALU.add)
        P.op("act", "activation", [t_G], [t_G], out=GU[:, d, :], in_=GU[:, d, :], func=AF.Exp)
        P.op("act", "activation", [tpg], [t_G], out=GL[:, d, :], in_=pg[:, (2 + d) * NT:(3 + d) * NT], func=AF.Exp, scale=-1.0)
        P.op("act", "activation", [tpg], [t_G], out=GAi[:, d, :], in_=pg[:, d * NT:(d + 1) * NT], func=AF.Exp)

    Cst = [C.sb([128, 132], F32, f"Cst{d}") for d in range(2)]; t_Cst = [TK(), TK()]
    Cb = [C.sb([128, 132], BF16, f"Cb{d}") for d in range(2)]; t_Cb = [TK(), TK()]
    Vp = [[C.sb([128, 132], BF16, f"Vp{d}{i}") for i in range(2)] for d in range(2)]; t_Vp = [[TK(), TK()], [TK(), TK()]]
    Sm = [[C.sb([128, 128], BF16, f"Sm{d}{i}") for i in range(2)] for d in range(2)]; t_Sm = [[TK(), TK()], [TK(), TK()]]
    Up = [[C.sb([128, 132], F32, f"Up{d}{i}") for i in range(2)] for d in range(2)]; t_Up = [[TK(), TK()], [TK(), TK()]]
    for d in range(2):
        P.op(POOLC, "memset", [], [t_Cst[d]], Cst[d][:], 0.0)
        P.op(POOLC, "memset", [], [t_Cb[d]], Cb[d][:], 0.0)
    order = [list(range(NT)), [1, 0] + list(range(NT - 1, 1, -1))]
    inited = set()
    kci = lambda ti: 0 if ti < 2 else 1 + (ti - 2) // 4
    for step in range(NT):
        for d in range(2):
            ti = order[d][step]
            rr = step % 2
            vp = Vp[d][rr]; tvp = t_Vp[d][rr]; smb = Sm[d][rr]; tsmb = t_Sm[d][rr]
            psS = R["ps"][0 + d]; tpsS = R["t_ps"][0 + d]
            psA = R["ps"][2 + d]; tpsA = R["t_ps"][2 + d]
            psU = R["ps"][4 + d]; tpsU = R["t_ps"][4 + d]
            P.op("dve", "tensor_scalar", [R["t_V"][ti], t_G], [tvp], out=vp[:, 0:129], in0=R["V"][:, ti, 0:129], scalar1=GU[:, d, ti:ti + 1], scalar2=None, op0=ALU.mult)
            P.op("pe", "matmul", [R["t_kT"][kci(ti)], R["t_qT"][kci(ti)]], [tpsS], psS[:, 0:128], lhsT=R["kT"][:, ti * 128:(ti + 1) * 128],
                 rhs=R["qT"][:, ti * 128:(ti + 1) * 128], start=True, stop=True)
            P.op("dve", "tensor_tensor", [tpsS, t_cst], [tsmb], out=smb[:], in0=psS[:, 0:128], in1=tri[:, d * 128:(d + 1) * 128], op=ALU.mult)
            P.op("pe", "matmul", [tsmb, tvp], [tpsA], psA[:, 0:129], lhsT=smb[:], rhs=vp[:, 0:129], start=True, stop=False)
            P.op("pe", "matmul", [R["t_qT"][kci(ti)], t_Cb[d]], [tpsA], psA[:, 0:129], lhsT=R["qT"][:, ti * 128:(ti + 1) * 128], rhs=Cb[d][:, 0:129],
                 start=False, stop=True)
            P.op("pe", "matmul", [t_Ktok[ti], tvp], [tpsU], psU[:, 0:129], lhsT=Ktok[:, ti, :], rhs=vp[:, 0:129], start=True, stop=True)
            c0 = 20 + 4 * d
            P.op("dve", "tensor_scalar", [tpsA, t_G], [tsm], out=sm[:, c0:c0 + 1], in0=psA[:, 128:129], scalar1=GAi[:, d, ti:ti + 1], scalar2=None, op0=ALU.max)
            P.op("dve", "tensor_scalar", [tpsA, tsm], [tsm], out=sm[:, c0 + 1:c0 + 2], in0=psA[:, 128:129], scalar1=-1.0, scalar2=sm[:, c0:c0 + 1],
                 op0=ALU.mult, op1=ALU.max)
            P.op("dve", "reciprocal", [tsm], [tsm], out=sm[:, c0 + 2:c0 + 3], in_=sm[:, c0 + 1:c0 + 2])
            if ti not in inited:
                inited.add(ti)
                P.op("dve", "tensor_scalar", [tpsA, tsm], [t_hs[ti]], out=hs[:, ti, :], in0=psA[:, 0:128], scalar1=sm[:, c0 + 2:c0 + 3], scalar2=None, op0=ALU.mult)
            else:
                P.op("dve", "scalar_tensor_tensor", [tpsA, tsm, t_hs[ti]], [t_hs[ti]], out=hs[:, ti, :], in0=psA[:, 0:128], scalar=sm[:, c0 + 2:c0 + 3],
                     in1=hs[:, ti, :], op0=ALU.mult, op1=ALU.add)
            up = Up[d][rr]; tup = t_Up[d][rr]
            P.op("act", "activation", [tpsU, t_G], [tup], out=up[:, 0:129], in_=psU[:, 0:129], func=AF.Copy, scale=GL[:, d, ti:ti + 1])
            P.op("dve", "scalar_tensor_tensor", [tup, t_G, t_Cst[d]], [t_Cst[d]], out=Cst[d][:, 0:129], in0=Cst[d][:, 0:129], scalar=GL[:, d, ti:ti + 1],
                 in1=up[:, 0:129], op0=ALU.mult, op1=ALU.add)
            P.op("act", "copy", [t_Cst[d]], [t_Cb[d]], out=Cb[d][:, 0:129], in_=Cst[d][:, 0:129])

    ysb = [C.sb([128, 128], BF16, f"ysbB{i}") for i in range(4)]; t_ysb = [TK() for _ in range(4)]
    junk = C.sb([128, 128], F32, "junkB"); t_junk = TK()
    for ci, (t0, n) in enumerate(CHUNKS):
        tiles = []
        for j in range(n // 128):
            ti = t0 // 128 + j
            P.op("act", "activation", [t_hs[ti]], [t_junk, tsm], out=junk[:], in_=hs[:, ti, :], func=AF.Square, accum_out=sm[:, 30:31])
            rms_scale(P, sm, tsm, 30, 31, 128)
            P.op("dve", "scalar_tensor_tensor", [t_hs[ti], tsm, R["t_zg"][ti]], [t_ysb[j]], out=ysb[j][:], in0=hs[:, ti, :], scalar=sm[:, 31:32],
                 in1=R["zg"][:, ti, :], op0=ALU.mult, op1=ALU.mult)
            tiles.append((ysb[j][:], t_ysb[j]))
        emit_yT(C, R, tiles, t0, n, yT_d, yrow)


def pass_C(C, R, l, hT_d, wC_d, biasg_d, maskc_d, yT_d, yrow=256):
    P = C.P
    sm = R["small"]; tsm = R["t_small"]
    load_weights(C, R, wC_d, 512)
    bias = C.sb([128, 6400], F32, "biasC"); t_bias = TK()
    mtmp = C.sb([128, 640], F32, "mtmpC"); t_mtmp = TK()
    P.dma("sp", [], [t_bias], bias[:, 0:3200], biasg_d[:, 0:3200])
    P.dma(POOLD, [], [t_bias], bias[:, 3200:6400], biasg_d[:, 3200:6400])
    for i in range(10):
        P.dma("sp", [], [t_mtmp], mtmp[:], maskc_d[:, i * 640:(i + 1) * 640])
        P.op("dve", "tensor_tensor", [t_mtmp, t_bias], [t_bias], out=bias[:, i * 640:(i + 1) * 640], in0=bias[:, i * 640:(i + 1) * 640], in1=mtmp[:], op=ALU.add)
    P.op(POOLC, "memset", [], R["t_V"], R["V"][:, :, 64:65], 1.0)
    P.op(POOLC, "memset", [], R["t_V"], R["V"][:, :, 129:130], 1.0)

    def fm_cb(j, ci, t0, n, pb, tpb):
        dst = R["qT"] if j == 0 else R["kT"]
        tds = R["t_qT"] if j == 0 else R["t_kT"]
        P.op("act", "copy", [tpb], [tds[ci]], out=dst[:, t0:t0 + n], in_=pb[:, 0:n])

    def tm_cb(ti, pb, tpb):
        P.op("dve", "tensor_copy", [tpb], [R["t_V"][ti]], out=R["V"][:, ti, 0:64], in_=pb[:, 0:64])
        P.op("dve", "tensor_copy", [tpb], [R["t_V"][ti]], out=R["V"][:, ti, 65:129], in_=pb[:, 64:128])
        P.op("act", "activation", [tpb], [R["t_zg"][ti]], out=R["zg"][:, ti, :], in_=pb[:, 128:256], func=AF.Silu)

    project(C, R, hT_d, [0, 128], (256, 256), fm_cb, tm_cb)

    tmpS = [C.sb([128, 640], F32, f"tmpS{i}") for i in range(2)]; t_tmpS = [TK(), TK()]
    Pl = [C.sb([128, 896], BF16, f"PlC{i}") for i in range(2)]; t_Pl = [TK(), TK()]
    ysb = [C.sb([128, 128], BF16, f"ysbC{i}") for i in range(4)]; t_ysb = [TK() for _ in range(4)]
    kci = lambda ti: 0 if ti < 2 else 1 + (ti - 2) // 4
    u = 0
    for ci, (t0, n) in enumerate(CHUNKS):
        tiles = []
        for jj in range(n // 128):
            ti = t0 // 128 + jj
            for hh in range(2):
                hs_ = slice(hh * 64, (hh + 1) * 64)
                bx = R["ps"][2 * (u % 2)]; tbx = R["t_ps"][2 * (u % 2)]
                by = R["ps"][2 * (u % 2) + 1]; tby = R["t_ps"][2 * (u % 2) + 1]
                pa = R["ps"][4 + u % 3]; tpa = R["t_ps"][4 + u % 3]
                pl = Pl[u % 2]; tpl = t_Pl[u % 2]
                ts_ = tmpS[u % 2]; tts = t_tmpS[u % 2]
                qsl = R["qT"][hs_, ti * 128:(ti + 1) * 128]
                if ti < 2:
                    keyt = []
                else:
                    j = ti - 2
                    kt0 = min(max(j - 2, 0), 59)
                    pat = 0 if j == 0 else 1 if j == 1 else 3 if j == 62 else 4 if j == 63 else 2
                    keyt = [2 + kt0 + a for a in range(5)]
                for a, kt in enumerate(keyt):
                    dst = bx[:, a * 128:(a + 1) * 128] if a < 4 else by[:, 0:128]
                    P.op("pe", "matmul", [R["t_kT"][kci(kt)], R["t_qT"][kci(ti)]], [tbx if a < 4 else tby], dst,
                         lhsT=R["kT"][hs_, kt * 128:(kt + 1) * 128], rhs=qsl, start=True, stop=True, skip_group_check=True)
                for a in range(2):
                    P.op("pe", "matmul", [R["t_kT"][0], R["t_qT"][kci(ti)]], [tby], by[:, (1 + a) * 128:(2 + a) * 128],
                         lhsT=R["kT"][hs_, a * 128:(a + 1) * 128], rhs=qsl, start=True, stop=True, skip_group_check=True)
                if keyt:
                    bo = (pat * 2 + hh) * 640
                    P.op("dve", "scalar_tensor_tensor", [tbx, t_bias], [tts], out=ts_[:, 0:512], in0=bx[:, 0:512], scalar=0.125, in1=bias[:, bo:bo + 512],
                         op0=ALU.mult, op1=ALU.add)
                    P.op("dve", "scalar_tensor_tensor", [tby, t_bias], [tts], out=ts_[:, 512:640], in0=by[:, 0:128], scalar=0.125, in1=bias[:, bo + 512:bo + 640],
                         op0=ALU.mult, op1=ALU.add)
                    P.op("act", "activation", [tts], [tpl], out=pl[:, 0:640], in_=ts_[:, 0:640], func=AF.Exp)
                P.op("act", "activation", [tby], [tpl], out=pl[:, 640:896], in_=by[:, 128:384], func=AF.Exp, scale=0.125)
                vs = slice(hh * 65, (hh + 1) * 65)
                allk = [(pl[:, a * 128:(a + 1) * 128], kt) for a, kt in enumerate(keyt)] + [(pl[:, 640 + a * 128:640 + (a + 1) * 128], a) for a in range(2)]
                for i, (pap, kt) in enumerate(allk):
                    P.op("pe", "matmul", [tpl, R["t_V"][kt]], [tpa], pa[:, 0:65], lhsT=pap, rhs=R["V"][:, kt, vs], start=(i == 0), stop=(i == len(allk) - 1))
                P.op("dve", "reciprocal", [tpa], [tsm], out=sm[:, 40:41], in_=pa[:, 64:65])
                P.op("dve", "scalar_tensor_tensor", [tpa, tsm, R["t_zg"][ti]], [t_ysb[jj]], out=ysb[jj][:, hs_], in0=pa[:, 0:64], scalar=sm[:, 40:41],
                     in1=R["zg"][:, ti, hs_], op0=ALU.mult, op1=ALU.mult)
                u += 1
            tiles.append((ysb[jj][:], t_ysb[jj]))
        emit_yT(C, R, tiles, t0, n, yT_d, yrow)


NTOK2 = 2304
CHUNKS2 = [(0, 256)] + [(256 + i * 512, 512) for i in range(4)]


def alloc_m2(C, G=None):
    M = dict(G) if G is not None else alloc_common(C)
    M["wst"] = [C.sb([128, 8, 128], F32, f"wst{i}") for i in range(3)]; M["t_wst"] = [TK(), TK(), TK()]
    M["xt"] = [C.sb([128, 1024], F32, f"xt{i}") for i in range(2)]; M["t_xt"] = [TK(), TK()]
    M["htmp"] = C.sb([128, 1024], F32, "htmp"); M["t_htmp"] = TK()
    M["hbf"] = C.sb([128, 1024], BF16, "hbf"); M["t_hbf"] = TK()
    M["hst"] = [C.sb([128, 8, 128], BF16, f"hst{i}") for i in range(2)]; M["t_hst"] = [TK(), TK()]
    M["junk"] = C.sb([128, 1024], F32, "junk"); M["t_junk"] = TK()
    M["mod"] = {}
    M["nrm"] = 0
    return M


def alloc_mod_out(C, M, key, parts):
    for part in parts:
        for v in range(2):
            M["mod"][(key, v, part)] = (C.sb([128, 1024], F32, f"mod_{key}_{v}_{part}"), TK())


def alloc_mod_tmp(C, M):
    M["wmst"] = C.sb([128, 8, 512], F32, "wmst"); M["t_wmst"] = TK()
    M["bst"] = C.sb([128, 512], F32, "bst"); M["t_bst"] = TK()
    M["screp"] = [C.sb([128, 8, 128], F32, f"screp{v}") for v in range(2)]; M["t_screp"] = TK()


def mod_vectors(C, M, key, crepL_d, crepC_d, wmod_d, bmodrep_d, parts, normgrep_d=None):
    P = C.P
    if "sc_done" not in M:
        M["sc_done"] = True
        for v, cd in enumerate((crepL_d, crepC_d)):
            P.dma("sp", [], [M["t_screp"]], M["screp"][v][:], cd[:, :, :])
            P.op("act", "activation", [M["t_screp"]], [M["t_screp"]], out=M["screp"][v][:], in_=M["screp"][v][:], func=AF.Silu)
    wv = wmod_d.rearrange("(k p) n -> p k n", p=128)
    pm = M["ps"][6]; tpm = M["t_ps"][6]
    if normgrep_d is not None:
        ng = C.sb([128, 1024], F32, "normg"); t_ng = TK()
        P.dma("sp", [], [t_ng], ng[:], normgrep_d[:, :])
    for part in parts:
        for half in range(2):
            c0 = part * 1024 + half * 512
            P.dma("sp", [], [M["t_wmst"]], M["wmst"][:, 0:4, :], wv[:, 0:4, c0:c0 + 512])
            P.dma(POOLD, [], [M["t_wmst"]], M["wmst"][:, 4:8, :], wv[:, 4:8, c0:c0 + 512])
            P.dma("sp", [], [M["t_bst"]], M["bst"][:], bmodrep_d[:, c0:c0 + 512])
            for v in range(2):
                dst, tdst = M["mod"][(key, v, part)]
                for k in range(8):
                    P.op("pe", "matmul", [M["t_screp"], M["t_wmst"]], [tpm], pm[:, :], lhsT=M["screp"][v][:, k, :], rhs=M["wmst"][:, k, :],
                         start=(k == 0), stop=(k == 7))
                P.op("dve", "tensor_tensor", [tpm, M["t_bst"]], [tdst], out=dst[:, half * 512:(half + 1) * 512], in0=pm[:, :], in1=M["bst"][:], op=ALU.add)
        if part == 1:
            for v in range(2):
                dst, tdst = M["mod"][(key, v, part)]
                P.op("dve", "scalar_tensor_tensor", [tdst, t_ng], [tdst], out=dst[:], in0=dst[:], scalar=1.0, in1=ng[:], op0=ALU.add, op1=ALU.mult)


def norm_tile(C, M, xt, t_xt, scale_bc, shift_bc, hT_view, i):
    P = C.P
    sm = M["small"]; tsm = M["t_small"]
    P.op("act", "activation", [t_xt], [M["t_junk"], tsm], out=M["junk"][:], in_=xt[:], func=AF.Square, accum_out=sm[:, 0:1])
    rms_scale(P, sm, tsm, 0, 1, 1024)
    P.op("dve", "scalar_tensor_tensor", [t_xt, tsm, scale_bc[1]], [M["t_htmp"]], out=M["htmp"][:], in0=xt[:], scalar=sm[:, 1:2], in1=scale_bc[0][:],
         op0=ALU.mult, op1=ALU.mult)
    P.op(POOLC, "tensor_tensor", [M["t_htmp"], shift_bc[1]], [M["t_hbf"]], out=M["hbf"][:], in0=M["htmp"][:], in1=shift_bc[0][:], op=ALU.add)
    M["nrm"] += 1
    hst = M["hst"][M["nrm"] % 2]; thst = M["t_hst"][M["nrm"] % 2]
    for k in range(8):
        P.op("pe", "transpose", [M["t_hbf"], M["t_ident"]], [M["t_pst"]], M["pst"][:, k * 128:(k + 1) * 128], M["hbf"][:, k * 128:(k + 1) * 128], M["ident"][:])
    P.op("act", "copy", [M["t_pst"]], [thst], out=hst[:], in_=M["pst"][:, :].rearrange("p (k t) -> p k t", t=128))
    P.dma("sp", [thst], [], hT_view[:, :, i * 128:(i + 1) * 128], hst[:])


def stage_N(C, M, key, x_d, hT_d, ntok=NTOK2):
    P = C.P
    hv = hT_d.rearrange("(k p) t -> p k t", p=128)
    for i in range(ntok // 128):
        v = 1 if i < 2 else 0
        xt = M["xt"][i % 2]; t_xt = M["t_xt"][i % 2]
        P.dma(POOLD, [], [t_xt], xt[:], x_d[i * 128:(i + 1) * 128, :])
        norm_tile(C, M, xt, t_xt, M["mod"][(key, v, 1)], M["mod"][(key, v, 0)], hv, i)


def load_w_bf16(C, M, w_d, dst, tdst, nk, ncols):
    P = C.P
    wv = w_d.rearrange("(k p) n -> p k n", p=128)
    i = 0
    for k0 in range(0, nk, 8):
        kn = min(8, nk - k0)
        for c0 in range(0, ncols, 128):
            st = M["wst"][i % 3]; tst = M["t_wst"][i % 3]
            P.dma(dma_rr(C), [], [tst], st[:, 0:kn, :], wv[:, k0:k0 + kn, c0:c0 + 128])
            if i % 3 == 0:
                P.op(POOLC, "tensor_copy", [tst], [tdst], out=dst[:, k0:k0 + kn, c0:c0 + 128], in_=st[:, 0:kn, :])
            elif i % 3 == 1:
                P.op("dve", "tensor_copy", [tst], [tdst], out=dst[:, k0:k0 + kn, c0:c0 + 128], in_=st[:, 0:kn, :])
            else:
                P.op("act", "copy", [tst], [tdst], out=dst[:, k0:k0 + kn, c0:c0 + 128], in_=st[:, 0:kn, :])
            i += 1


def stage_M2(C, M, key, x_d, hT_d, yT_d, wgm_d, wbr_d, wout_d, xout_d, next_key=None, hTn_d=None, finalg_d=None, out_d=None, chunks=None):
    P = C.P
    Wgm = C.sb([128, 8, 3072], BF16, "Wgm"); t_Wgm = TK()
    Wbr = C.sb([128, 12, 1024], BF16, "Wbr"); t_Wbr = TK()
    Wout = C.sb([128, 8, 1024], BF16, "Wout"); t_Wout = TK()
    load_w_bf16(C, M, wgm_d, Wgm, t_Wgm, 8, 3072)
    load_w_bf16(C, M, wbr_d, Wbr, t_Wbr, 12, 1024)
    load_w_bf16(C, M, wout_d, Wout, t_Wout, 8, 1024)
    hTc = C.sb([128, 8, 512], BF16, "hTc"); t_hTc = TK()
    yTc = C.sb([128, 12, 512], BF16, "yTc"); t_yTc = TK()
    uT = C.sb([128, 8, 512], BF16, "uT"); t_uT = TK()
    sg = C.sb([128, 512], F32, "sg"); t_sg = TK()
    uacc = C.sb([128, 512], F32, "uacc"); t_uacc = TK()
    ut = C.sb([128, 512], F32, "ut"); t_ut = TK()
    xn = [C.sb([128, 1024], F32, f"xn{i}") for i in range(2)]; t_xn = [TK(), TK()]
    if finalg_d is not None:
        fg = C.sb([128, 1024], F32, "fg"); t_fg = TK()
        P.dma("sp", [], [t_fg], fg[:], finalg_d[:, :])
        ost = [C.sb([128, 1024], F32, f"ost{i}") for i in range(2)]; t_ost = [TK(), TK()]
    hv = hT_d.rearrange("(k p) t -> p k t", p=128)
    yv = yT_d.rearrange("(k p) t -> p k t", p=128)
    hvn = hTn_d.rearrange("(k p) t -> p k t", p=128) if hTn_d is not None else None
    sm = M["small"]; tsm = M["t_small"]
    g = 0
    for ci, (t0, n) in enumerate(chunks or CHUNKS2):
        v = 1 if ci == 0 else 0
        gate, t_gate = M["mod"][(key, v, 2)]
        P.dma("sp", [], [t_hTc], hTc[:, :, 0:n], hv[:, :, t0:t0 + n])
        P.dma(POOLD, [], [t_yTc], yTc[:, :, 0:n], yv[:, :, t0:t0 + n])
        for oc in range(8):
            for nb in range(3):
                pg = M["ps"][g % 2]; tpg = M["t_ps"][g % 2]
                pp = M["ps"][2 + g % 2]; tpp = M["t_ps"][2 + g % 2]
                g += 1
                for k in range(8):
                    P.op("pe", "matmul", [t_Wgm, t_hTc], [tpg], pg[:, 0:n], lhsT=Wgm[:, k, nb * 1024 + oc * 128:nb * 1024 + (oc + 1) * 128], rhs=hTc[:, k, 0:n],
                         start=(k == 0), stop=(k == 7))
                for kk in range(4):
                    P.op("pe", "matmul", [t_Wbr, t_yTc], [tpp], pp[:, 0:n], lhsT=Wbr[:, nb * 4 + kk, oc * 128:(oc + 1) * 128], rhs=yTc[:, nb * 4 + kk, 0:n],
                         start=(kk == 0), stop=(kk == 3))
                P.op("act", "activation", [tpg], [t_sg], out=sg[:, 0:n], in_=pg[:, 0:n], func=AF.Sigmoid)
                if nb == 0:
                    P.op("dve", "tensor_tensor", [tpp, t_sg], [t_uacc], out=uacc[:, 0:n], in0=pp[:, 0:n], in1=sg[:, 0:n], op=ALU.mult)
                else:
                    P.op("dve", "tensor_tensor", [tpp, t_sg], [t_ut], out=ut[:, 0:n], in0=pp[:, 0:n], in1=sg[:, 0:n], op=ALU.mult)
                    if nb == 1:
                        P.op(POOLC, "tensor_tensor", [t_ut, t_uacc], [t_uacc], out=uacc[:, 0:n], in0=uacc[:, 0:n], in1=ut[:, 0:n], op=ALU.add)
                    else:
                        P.op(POOLC, "tensor_tensor", [t_ut, t_uacc], [t_uT], out=uT[:, oc, 0:n], in0=uacc[:, 0:n], in1=ut[:, 0:n], op=ALU.add)
        for tt in range(n // 128):
            i = t0 // 128 + tt
            xt = M["xt"][i % 2]; t_xt = M["t_xt"][i % 2]
            xo = xn[i % 2]; t_xo = t_xn[i % 2]
            P.dma(POOLD, [], [t_xt], xt[:], x_d[i * 128:(i + 1) * 128, :])
            for half in range(2):
                po = M["ps"][4 + half]; tpo = M["t_ps"][4 + half]
                cs = slice(half * 512, (half + 1) * 512)
                for oc in range(8):
                    P.op("pe", "matmul", [t_uT, t_Wout], [tpo], po[:, :], lhsT=uT[:, oc, tt * 128:(tt + 1) * 128], rhs=Wout[:, oc, cs], start=(oc == 0), stop=(oc == 7))
                P.op("dve", "tensor_tensor", [tpo, t_gate], [t_xo], out=xo[:, cs], in0=po[:, :], in1=gate[:, cs], op=ALU.mult)
                P.op(POOLC, "tensor_tensor", [t_xo, t_xt], [t_xo], out=xo[:, cs], in0=xo[:, cs], in1=xt[:, cs], op=ALU.add)
            if xout_d is not None:
                P.dma("sp", [t_xo], [], xout_d[i * 128:(i + 1) * 128, :], xo[:])
            if next_key is not None:
                norm_tile(C, M, xo, t_xo, M["mod"][(next_key, v, 1)], M["mod"][(next_key, v, 0)], hvn, i)
            if finalg_d is not None:
                os_ = ost[i % 2]; t_os = t_ost[i % 2]
                P.op("act", "activation", [t_xo], [M["t_junk"], tsm], out=M["junk"][:], in_=xo[:], func=AF.Square, accum_out=sm[:, 2:3])
                rms_scale(P, sm, tsm, 2, 3, 1024)
                P.op("dve", "scalar_tensor_tensor", [t_xo, tsm, t_fg], [t_os], out=os_[:], in0=xo[:], scalar=sm[:, 3:4], in1=fg[:], op0=ALU.mult, op1=ALU.mult)
                P.dma("sp", [t_os], [], out_d[i * 128:(i + 1) * 128, :], os_[:])


def rope_tables():
    t = np.arange(S)
    row = (t // GRID_W).astype(np.float32)
    col = (t % GRID_W).astype(np.float32)
    inv = (10000.0 ** (-np.arange(0, 32, 2, dtype=np.float32) / 32)).astype(np.float32)
    ang = np.concatenate([row[:, None] * inv, col[:, None] * inv], axis=-1).astype(np.float32)
    cos = np.cos(ang).astype(np.float32); sin = np.sin(ang).astype(np.float32)
    cosT = np.ones((128, T), np.float32); sinT = np.zeros((128, T), np.float32)
    for c in range(2):
        cosT[c * 64:c * 64 + 32, LC:] = cos.T
        cosT[c * 64 + 32:c * 64 + 64, LC:] = cos.T
        sinT[c * 64:c * 64 + 32, LC:] = -sin.T
        sinT[c * 64 + 32:c * 64 + 64, LC:] = sin.T
    return cosT, sinT


def wA_cols(g):
    cols = []
    ev = np.arange(0, 64, 2); od = np.arange(1, 64, 2)
    for base in (OFF["aq"], OFF["ak"]):
        main = []; swp = []
        for c in range(2):
            b0 = base + g * 128 + c * 64
            main += list(b0 + ev) + list(b0 + od)
            swp += list(b0 + od) + list(b0 + ev)
        cols += main + swp
    cols += list(OFF["az"] + g * 128 + np.arange(128)) + list(OFF["av"] + g * 128 + np.arange(128))
    return np.array(cols)


def rep128(v):
    return np.ascontiguousarray(np.broadcast_to(np.asarray(v, np.float32).reshape(1, -1), (128, v.size)))


def col128(v):
    return np.ascontiguousarray(np.broadcast_to(np.asarray(v, np.float32).reshape(-1, 1), (128, 128)))


def wB_cols(g):
    r = np.arange(128)
    cols = list(OFF["bq"] + g * 128 + r) + list(OFF["bk"] + g * 128 + r)
    cols += list(OFF["bv"] + g * 128 + r) + list(OFF["bo"] + g * 128 + r) + list(OFF["bz"] + g * 128 + r)
    cols += [OFF["bg"] + 0 + g, OFF["bg"] + 4 + g, OFF["bg"] + 8 + g, OFF["bg"] + 12 + g]
    return np.array(cols)


def convp_host(conv_w, conv_b, g):
    out = np.zeros((128, 8), np.float32)
    for j, base in enumerate((g * 128, 512 + g * 128)):
        out[:, 4 * j:4 * j + 3] = conv_w[:, base:base + 128].T
        out[:, 4 * j + 3] = conv_b[base:base + 128]
    return out


def tri_host():
    s = np.arange(128)
    triF = (s[:, None] <= s[None, :]).astype(np.float32)
    return np.ascontiguousarray(np.concatenate([triF, triF.T, np.ones((128, 128), np.float32)], axis=1))


def wC_cols(g):
    r = np.arange(128)
    return np.array(list(OFF["cq"] + g * 128 + r) + list(OFF["ck"] + g * 128 + r) + list(OFF["cv"] + g * 128 + r) + list(OFF["cz"] + g * 128 + r))


def na_geometry():
    pats = [(0, 0), (1, 0), (10, 8), (62, 59), (63, 59)]
    ridx = np.zeros((5, 5, 128, 128), np.int64); cidx = np.zeros_like(ridx); valid = np.zeros(ridx.shape, bool)
    ki = np.arange(128); qi = np.arange(128)
    for p, (j, kt0) in enumerate(pats):
        rq = 2 * j + qi // 64; cq = qi % 64
        rs = np.clip(rq - 4, 0, 120); cs = np.clip(cq - 8, 0, 48)
        for a in range(5):
            rk = 2 * (kt0 + a) + ki // 64; ck = ki % 64
            v = (rk[:, None] >= rs[None, :]) & (rk[:, None] < rs[None, :] + 8) & (ck[:, None] >= cs[None, :]) & (ck[:, None] < cs[None, :] + 16)
            valid[p, a] = v
            ridx[p, a] = np.clip(rk[:, None] - rq[None, :] + 7, 0, 14)
            cidx[p, a] = np.clip(ck[:, None] - cq[None, :] + 15, 0, 30)
    return ridx, cidx, valid


def na_bias_host(rpb, g):
    ridx, cidx, valid = na_geometry()
    out = np.zeros((128, 5, 2, 5, 128), np.float32)
    for hh in range(2):
        gathered = rpb[2 * g + hh][ridx, cidx]
        out[:, :, hh, :, :] = np.transpose(gathered, (2, 0, 1, 3))
    return np.ascontiguousarray(out.reshape(128, 6400))


def na_mask_host():
    ridx, cidx, valid = na_geometry()
    m = np.where(valid, 0.0, NEG).astype(np.float32)
    out = np.zeros((128, 5, 2, 5, 128), np.float32)
    for hh in range(2):
        out[:, :, hh, :, :] = np.transpose(m, (2, 0, 1, 3))
    return np.ascontiguousarray(out.reshape(128, 6400))


def build_N0():
    nc = bass.Bass("TRN2", target_bir_lowering=False)
    with contextlib.ExitStack() as es:
        C = Ctx(nc, es)
        x_d = C.dram("x_in", [NTOK2, D], F32, "ExternalInput")
        crepL = C.dram("crepL", [128, 8, 128], F32, "ExternalInput")
        crepC = C.dram("crepC", [128, 8, 128], F32, "ExternalInput")
        wmod = C.dram("wmod", [D, 3 * D], F32, "ExternalInput")
        bmod = C.dram("bmodrep", [128, 3 * D], F32, "ExternalInput")
        normg = C.dram("normgrep", [128, D], F32, "ExternalInput")
        ident_d = C.dram("ident", [128, 128], F32, "ExternalInput")
        hT_d = C.dram("hT_out", [D, NTOK2], BF16, "ExternalOutput")
        M = alloc_m2(C)
        load_ident(C, M, ident_d)
        alloc_mod_out(C, M, "n", [0, 1])
        outer = C.es
        with contextlib.ExitStack() as es2:
            C.es = es2
            alloc_mod_tmp(C, M)
            mod_vectors(C, M, "n", crepL, crepC, wmod, bmod, [0, 1], normg)
        C.es = outer
        C.P.barrier()
        stage_N(C, M, "n", x_d, hT_d)
        C.P.emit()
    return nc


def build_M1(l):
    nc = bass.Bass("TRN2", target_bir_lowering=False)
    with contextlib.ExitStack() as es:
        C = Ctx(nc, es)
        hT_d = C.dram("hT", [D, T], BF16, "ExternalInput")
        ident_d = C.dram("ident", [128, 128], F32, "ExternalInput")
        yT_d = C.dram("yT", [384, T], BF16, "ExternalOutput")
        wA_d = C.dram("wA", [D, 768], F32, "ExternalInput")
        cos_d = C.dram("cosT", [128, T], F32, "ExternalInput")
        sin_d = C.dram("sinT", [128, T], F32, "ExternalInput")
        lamq_d = C.dram("lamq", [128, 128], F32, "ExternalInput")
        lamk_d = C.dram("lamk", [128, 128], F32, "ExternalInput")
        dag_d = C.dram("dag", [128, 128], F32, "ExternalInput")
        wB_d = C.dram("wB", [D, 644], F32, "ExternalInput")
        convp_d = C.dram("convp", [128, 8], F32, "ExternalInput")
        gateb_d = C.dram("gateb", [128, 4], F32, "ExternalInput")
        mlg_d = C.dram("mlg", [128, 128], F32, "ExternalInput")
        tri_d = C.dram("tri", [128, 384], F32, "ExternalInput")
        wC_d = C.dram("wC", [D, 512], F32, "ExternalInput")
        biasg_d = C.dram("biasg", [128, 6400], F32, "ExternalInput")
        maskc_d = C.dram("maskc", [128, 6400], F32, "ExternalInput")
        R = alloc_m1(C)
        load_ident(C, R, ident_d)
        outer = C.es
        with contextlib.ExitStack() as es2:
            C.es = es2
            pass_A(C, R, l, hT_d, wA_d, cos_d, sin_d, lamq_d, lamk_d, dag_d, yT_d)
        C.P.barrier()
        with contextlib.ExitStack() as es2:
            C.es = es2
            pass_B(C, R, l, hT_d, wB_d, convp_d, gateb_d, mlg_d, tri_d, yT_d)
        C.P.barrier()
        with contextlib.ExitStack() as es2:
            C.es = es2
            pass_C(C, R, l, hT_d, wC_d, biasg_d, maskc_d, yT_d)
        C.es = outer
        C.P.emit()
    return nc


def build_M2(last):
    nc = bass.Bass("TRN2", target_bir_lowering=False)
    with contextlib.ExitStack() as es:
        C = Ctx(nc, es)
        x_d = C.dram("x_in", [NTOK2, D], F32, "ExternalInput")
        hT_d = C.dram("hT", [D, NTOK2], BF16, "ExternalInput")
        yT_d = C.dram("yT", [1536, NTOK2], BF16, "ExternalInput")
        wgm = C.dram("wgm", [D, 3 * D], F32, "ExternalInput")
        wbr = C.dram("wbr", [1536, D], F32, "ExternalInput")
        wout = C.dram("wout", [D, D], F32, "ExternalInput")
        crepL = C.dram("crepL", [128, 8, 128], F32, "ExternalInput")
        crepC = C.dram("crepC", [128, 8, 128], F32, "ExternalInput")
        wmod = C.dram("wmod", [D, 3 * D], F32, "ExternalInput")
        bmod = C.dram("bmodrep", [128, 3 * D], F32, "ExternalInput")
        ident_d = C.dram("ident", [128, 128], F32, "ExternalInput")
        M = alloc_m2(C)
        load_ident(C, M, ident_d)
        alloc_mod_out(C, M, "g", [2])
        if not last:
            alloc_mod_out(C, M, "n", [0, 1])
            wmod2 = C.dram("wmod2", [D, 3 * D], F32, "ExternalInput")
            bmod2 = C.dram("bmodrep2", [128, 3 * D], F32, "ExternalInput")
            normg2 = C.dram("normgrep2", [128, D], F32, "ExternalInput")
            xout_d = C.dram("x_out", [NTOK2, D], F32, "ExternalOutput")
            hTn_d = C.dram("hT_out", [D, NTOK2], BF16, "ExternalOutput")
        outer = C.es
        with contextlib.ExitStack() as es2:
            C.es = es2
            alloc_mod_tmp(C, M)
            mod_vectors(C, M, "g", crepL, crepC, wmod, bmod, [2])
            if not last:
                mod_vectors(C, M, "n", crepL, crepC, wmod2, bmod2, [0, 1], normg2)
        C.es = outer
        C.P.barrier()
        if not last:
            stage_M2(C, M, "g", x_d, hT_d, yT_d, wgm, wbr, wout, xout_d, next_key="n", hTn_d=hTn_d)
        else:
            fg = C.dram("finalg", [128, D], F32, "ExternalInput")
            out_d = C.dram("out", [NTOK2, D], F32, "ExternalOutput")
            stage_M2(C, M, "g", x_d, hT_d, yT_d, wgm, wbr, wout, None, finalg_d=fg, out_d=out_d)
        C.P.emit()
    return nc


def crep_host(c):
    a = np.asarray(c, np.float32).reshape(8, 128).T
    return np.ascontiguousarray(np.broadcast_to(a[:, :, None], (128, 8, 128)))


def assemble_hT(parts, b):
    return np.ascontiguousarray(np.concatenate([parts[(b, 0)][:, 0:256]] + [parts[(b, s)][:, 256:] for s in range(4)], axis=1))


def tok_slice(a, s):
    return np.concatenate([a[..., 0:256], a[..., 256 + s * 2048:256 + (s + 1) * 2048]], axis=-1)


CORES = [(b, i) for b in range(B) for i in range(4)]


def kernel_unfused(x, c, ctx, c_ctx, w_mod, b_mod, norm_g, w_in, da_lam_q, da_lam_k, da_norm_g,
           ml_conv_w, ml_conv_b, ml_gate_b, ml_norm_g, na_rpb, w_br, w_out, final_g):
    f32 = lambda a: np.ascontiguousarray(np.asarray(a, np.float32))
    x = f32(x); c = f32(c); ctx = f32(ctx); c_ctx = f32(c_ctx); w_mod = f32(w_mod); b_mod = f32(b_mod); norm_g = f32(norm_g)
    w_in = f32(w_in); w_br = f32(w_br); w_out = f32(w_out); final_g = f32(final_g)
    ident = np.eye(128, dtype=np.float32)
    cosT, sinT = rope_tables()
    tri = tri_host()
    maskc = na_mask_host()
    ids = list(range(8))
    crepC = crep_host(c_ctx)
    crepL = {b: crep_host(c[b]) for b in range(B)}

    x_cur = {(b, s): np.ascontiguousarray(np.concatenate([ctx[b], x[b, s * 2048:(s + 1) * 2048]], axis=0)) for (b, s) in CORES}
    ncA = build_N0()
    im = [dict(x_in=x_cur[(b, s)], crepL=crepL[b], crepC=crepC, wmod=w_mod[0], bmodrep=rep128(b_mod[0]), normgrep=rep128(norm_g[0]), ident=ident)
          for (b, s) in CORES]
    res = run_bass_kernel_spmd(ncA, im, core_ids=ids).results
    hparts = {CORES[i]: np.asarray(res[i]["hT_out"]) for i in range(8)}
    out = np.zeros((B, S, D), np.float32)
    for l in range(DEPTH):
        hT = {b: assemble_hT(hparts, b) for b in range(B)}
        nc1 = build_M1(l)
        im = []
        for (b, g) in CORES:
            gb = np.asarray(ml_gate_b[l], np.float32)
            im.append(dict(
                hT=hT[b], ident=ident,
                wA=np.ascontiguousarray(w_in[l][:, wA_cols(g)]), cosT=cosT, sinT=sinT,
                lamq=rep128(np.asarray(da_lam_q[l], np.float32).reshape(-1)), lamk=rep128(np.asarray(da_lam_k[l], np.float32).reshape(-1)),
                dag=col128(np.asarray(da_norm_g[l], np.float32)),
                wB=np.ascontiguousarray(w_in[l][:, wB_cols(g)]),
                convp=convp_host(np.asarray(ml_conv_w[l], np.float32), np.asarray(ml_conv_b[l], np.float32), g),
                gateb=rep128(np.array([gb[0, 0, g], gb[0, 1, g], gb[1, 0, g], gb[1, 1, g]], np.float32)),
                mlg=rep128(np.asarray(ml_norm_g[l], np.float32)), tri=tri,
                wC=np.ascontiguousarray(w_in[l][:, wC_cols(g)]), biasg=na_bias_host(np.asarray(na_rpb[l], np.float32), g), maskc=maskc))
        res = run_bass_kernel_spmd(nc1, im, core_ids=ids).results
        yT = {CORES[i]: np.asarray(res[i]["yT"]) for i in range(8)}
        last = (l == DEPTH - 1)
        nc2 = build_M2(last)
        im = []
        for (b, s) in CORES:
            yall = np.concatenate([yT[(b, g)][n * 128:(n + 1) * 128] for n in range(3) for g in range(4)], axis=0)
            d = dict(x_in=x_cur[(b, s)], hT=np.ascontiguousarray(tok_slice(hT[b], s)), yT=np.ascontiguousarray(tok_slice(yall, s)),
                     wgm=np.ascontiguousarray(w_in[l][:, OFF["gm"]:]), wbr=np.ascontiguousarray(w_br[l].reshape(1536, D)), wout=w_out[l],
                     crepL=crepL[b], crepC=crepC, wmod=w_mod[l], bmodrep=rep128(b_mod[l]), ident=ident)
            if not last:
                d.update(wmod2=w_mod[l + 1], bmodrep2=rep128(b_mod[l + 1]), normgrep2=rep128(norm_g[l + 1]))
            else:
                d.update(finalg=rep128(final_g))
            im.append(d)
        res = run_bass_kernel_spmd(nc2, im, core_ids=ids).results
        if not last:
            x_cur = {CORES[i]: np.asarray(res[i]["x_out"]) for i in range(8)}
            hparts = {CORES[i]: np.asarray(res[i]["hT_out"]) for i in range(8)}
        else:
            for i, (b, s) in enumerate(CORES):
                out[b, s * 2048:(s + 1) * 2048] = np.asarray(res[i]["out"])[256:]
    return out


@contextlib.contextmanager
def scope(C):
    outer = C.es
    with contextlib.ExitStack() as es2:
        C.es = es2
        try:
            yield
        finally:
            C.es = outer


def build_fused():
    nc = bass.Bass("TRN2", target_bir_lowering=False)
    with contextlib.ExitStack() as es:
        C = Ctx(nc, es)
        P = C.P
        ext = lambda name, shape, dt=F32: C.dram(name, shape, dt, "ExternalInput")
        x_in = ext("x_in", [T, D])
        crepL = ext("crepL", [128, 8, 128]); crepC = ext("crepC", [128, 8, 128]); ident_d = ext("ident", [128, 128])
        cos_d = ext("cosT", [128, T]); sin_d = ext("sinT", [128, T]); tri_d = ext("tri", [128, 384]); maskc_d = ext("maskc", [128, 6400])
        fg_d = ext("finalg", [128, D])
        L = []
        for l in range(DEPTH):
            L.append(dict(
                wmod=ext(f"wmod{l}", [D, 3 * D]), bmod=ext(f"bmodrep{l}", [128, 3 * D]), normg=ext(f"normgrep{l}", [128, D]),
                wA=ext(f"wA{l}", [4, D, 768]), wB=ext(f"wB{l}", [4, D, 644]), wC=ext(f"wC{l}", [4, D, 512]),
                lamq=ext(f"lamq{l}", [128, 128]), lamk=ext(f"lamk{l}", [128, 128]), dag=ext(f"dag{l}", [128, 128]),
                convp=ext(f"convp{l}", [4, 128, 8]), gateb=ext(f"gateb{l}", [4, 128, 4]), mlg=ext(f"mlg{l}", [128, 128]),
                biasg=ext(f"biasg{l}", [4, 128, 6400]),
                wgm=ext(f"wgm{l}", [D, 3 * D]), wbr=ext(f"wbr{l}", [1536, D]), wout=ext(f"wout{l}", [D, D])))
        out_d = C.dram("out", [T, D], F32, "ExternalOutput")
        hTs = [nc.dram_tensor(f"hT_scr{i}", [D, T], BF16).ap() for i in range(2)]
        yT = nc.dram_tensor("yT_scr", [1536, T], BF16).ap()
        xbuf = nc.dram_tensor("x_scr", [T, D], F32).ap()

        G = alloc_common(C)
        load_ident(C, G, ident_d)
        with scope(C):
            M = alloc_m2(C, G)
            alloc_mod_out(C, M, "n", [0, 1])
            with scope(C):
                alloc_mod_tmp(C, M)
                mod_vectors(C, M, "n", crepL, crepC, L[0]["wmod"], L[0]["bmod"], [0, 1], L[0]["normg"])
            P.barrier()
            stage_N(C, M, "n", x_in, hTs[0], ntok=T)
        P.barrier()
        for l in range(DEPTH):
            last = (l == DEPTH - 1)
            W = L[l]
            hT = hTs[l % 2]
            with scope(C):
                R = alloc_m1(C, G)
                for g in range(4):
                    with scope(C):
                        pass_A(C, R, l, hT, W["wA"][g], cos_d, sin_d, W["lamq"], W["lamk"], W["dag"], yT, yrow=0 * 512 + g * 128)
                    P.barrier()
                    with scope(C):
                        pass_B(C, R, l, hT, W["wB"][g], W["convp"][g], W["gateb"][g], W["mlg"], tri_d, yT, yrow=1 * 512 + g * 128)
                    P.barrier()
                    with scope(C):
                        pass_C(C, R, l, hT, W["wC"][g], W["biasg"][g], maskc_d, yT, yrow=2 * 512 + g * 128)
                    P.barrier()
            with scope(C):
                M = alloc_m2(C, G)
                alloc_mod_out(C, M, "g", [2])
                if not last:
                    alloc_mod_out(C, M, "n", [0, 1])
                with scope(C):
                    alloc_mod_tmp(C, M)
                    mod_vectors(C, M, "g", crepL, crepC, W["wmod"], W["bmod"], [2])
                    if not last:
                        mod_vectors(C, M, "n", crepL, crepC, L[l + 1]["wmod"], L[l + 1]["bmod"], [0, 1], L[l + 1]["normg"])
                P.barrier()
                x_src = x_in if l == 0 else xbuf
                if not last:
                    stage_M2(C, M, "g", x_src, hT, yT, W["wgm"], W["wbr"], W["wout"], xbuf, next_key="n", hTn_d=hTs[(l + 1) % 2], chunks=CHUNKS)
                else:
                    stage_M2(C, M, "g", x_src, hT, yT, W["wgm"], W["wbr"], W["wout"], None, finalg_d=fg_d, out_d=out_d, chunks=CHUNKS)
            P.barrier()
        print("[kernel] instruction counts", {k: len(v) for k, v in P.q.items()}, flush=True)
        P.emit()
    return nc


def kernel(x, c, ctx, c_ctx, w_mod, b_mod, norm_g, w_in, da_lam_q, da_lam_k, da_norm_g,
           ml_conv_w, ml_conv_b, ml_gate_b, ml_norm_g, na_rpb, w_br, w_out, final_g):
    f32 = lambda a: np.ascontiguousarray(np.asarray(a, np.float32))
    x = f32(x); c = f32(c); ctx = f32(ctx); c_ctx = f32(c_ctx); w_mod = f32(w_mod); b_mod = f32(b_mod); norm_g = f32(norm_g)
    w_in = f32(w_in); w_br = f32(w_br); w_out = f32(w_out); final_g = f32(final_g)
    cosT, sinT = rope_tables()
    shared = dict(ident=np.eye(128, dtype=np.float32), cosT=cosT, sinT=sinT, tri=tri_host(), maskc=na_mask_host(),
                  finalg=rep128(final_g), crepC=crep_host(c_ctx))
    for l in range(DEPTH):
        gb = np.asarray(ml_gate_b[l], np.float32)
        shared.update({
            f"wmod{l}": w_mod[l], f"bmodrep{l}": rep128(b_mod[l]), f"normgrep{l}": rep128(norm_g[l]),
            f"wA{l}": np.ascontiguousarray(np.stack([w_in[l][:, wA_cols(g)] for g in range(4)])),
            f"wB{l}": np.ascontiguousarray(np.stack([w_in[l][:, wB_cols(g)] for g in range(4)])),
            f"wC{l}": np.ascontiguousarray(np.stack([w_in[l][:, wC_cols(g)] for g in range(4)])),
            f"lamq{l}": rep128(np.asarray(da_lam_q[l], np.float32).reshape(-1)), f"lamk{l}": rep128(np.asarray(da_lam_k[l], np.float32).reshape(-1)),
            f"dag{l}": col128(np.asarray(da_norm_g[l], np.float32)),
            f"convp{l}": np.stack([convp_host(np.asarray(ml_conv_w[l], np.float32), np.asarray(ml_conv_b[l], np.float32), g) for g in range(4)]),
            f"gateb{l}": np.stack([rep128(np.array([gb[0, 0, g], gb[0, 1, g], gb[1, 0, g], gb[1, 1, g]], np.float32)) for g in range(4)]),
            f"mlg{l}": rep128(np.asarray(ml_norm_g[l], np.float32)),
            f"biasg{l}": np.stack([na_bias_host(np.asarray(na_rpb[l], np.float32), g) for g in range(4)]),
            f"wgm{l}": np.ascontiguousarray(w_in[l][:, OFF["gm"]:]), f"wbr{l}": np.ascontiguousarray(w_br[l].reshape(1536, D)), f"wout{l}": w_out[l]})
    im = []
    for b in range(B):
        d = dict(shared)
        d["x_in"] = np.ascontiguousarray(np.concatenate([ctx[b], x[b]], axis=0))
        d["crepL"] = crep_host(c[b])
        im.append(d)
    nc = build_fused()
    res = run_bass_kernel_spmd(nc, im, core_ids=list(range(B))).results
    return np.stack([np.asarray(res[b]["out"])[LC:] for b in range(B)]).astype(np.float32)
```

```python
import contextlib
import math
import numpy as np
import ml_dtypes
import concourse.bass as bass
import concourse.mybir as mybir
from concourse.bass_utils import run_bass_kernel_spmd

F32 = mybir.dt.float32
BF16 = mybir.dt.bfloat16
AF = mybir.ActivationFunctionType
ALU = mybir.AluOpType
AX = mybir.AxisListType

D = 1024
B = 2
S = 8192
LC = 256
T = S + LC
NT = T // 128
DEPTH = 2
EPS = 1e-6
NDSEM = 8
GRID_W = 64
NEG = -30000.0
import os
POOLC = os.environ.get("K_POOLC", "pool")
POOLD = os.environ.get("K_POOLD", "pool")

OFF = dict(aq=0, ak=512, av=1024, az=1536, bq=2048, bk=2560, bv=3072, bo=3584, bz=4096, bg=4608,
           cq=4624, ck=5136, cv=5648, cz=6160, gm=6672)
D_IN = 9744

CHUNKS = [(0, 256)] + [(256 + i * 512, 512) for i in range(16)]


class TK:
    __slots__ = ("w", "r", "excl")

    def __init__(self, excl=False):
        self.w = None
        self.r = []
        self.excl = excl


class Prog:
    def __init__(self, nc):
        self.nc = nc
        self.q = {e: [] for e in ("pe", "act", "dve", "pool", "sp")}
        self.cnt = {e: 0 for e in self.q}
        self.dcnt = {e: 0 for e in ("sp", "act", "pool")}
        self.seen = {e: {} for e in self.q}
        self.dma_tokens = []
        self.pending = {e: [] for e in self.q}

    def barrier(self):
        toks = [(("e", e), c) for e, c in self.cnt.items() if c > 0]
        last = {}
        for k, v in self.dma_tokens:
            last[k] = max(last.get(k, 0), v)
        toks += list(last.items())
        for e in self.q:
            self.pending[e] = list(toks)

    def _waits(self, eng, reads, writes):
        w = {}
        seen = self.seen[eng]

        def need(tok):
            if tok is None:
                return
            k, v = tok
            if eng == "pe" and k == ("e", "pe"):
                return
            if seen.get(k, 0) >= v:
                return
            if w.get(k, 0) < v:
                w[k] = v
        me = ("e", eng)
        if self.pending[eng]:
            for tok in self.pending[eng]:
                if tok[0] != me:
                    need(tok)
            self.pending[eng] = []
        for t in reads:
            need(t.w)
            if t.excl:
                for r in t.r:
                    if r[0] != me:
                        need(r)
        for t in writes:
            need(t.w)
            for r in t.r:
                need(r)
        for k, v in w.items():
            seen[k] = v
        return w

    def _update(self, tok, reads, writes):
        for t in reads:
            if len(t.r) > 24:
                m = {}
                for k, v in t.r:
                    if m.get(k, 0) < v:
                        m[k] = v
                t.r = list(m.items())
            t.r.append(tok)
        for t in writes:
            t.w = tok
            t.r = []

    def op(self, eng, meth, reads, writes, *a, **kw):
        w = self._waits(eng, reads, writes)
        self.cnt[eng] += 1
        tok = (("e", eng), self.cnt[eng])
        self.q[eng].append((w, (meth, a, kw), None))
        self._update(tok, reads, writes)
        return tok

    def dma(self, queue, reads, writes, out=None, in_=None, fn=None):
        if fn is None:
            fn = ("dma_start", (), dict(out=out, in_=in_))
        i = self.dcnt[queue]
        self.dcnt[queue] += 1
        s = i % NDSEM
        prev = 16 * (i // NDSEM)
        w = self._waits(queue, reads, writes)
        key = ("d", queue, s)
        if prev > 0 and self.seen[queue].get(key, 0) < prev:
            w[key] = prev
            self.seen[queue][key] = prev
        tok = (key, prev + 16)
        self.q[queue].append((w, fn, key))
        self._update(tok, reads, writes)
        self.dma_tokens.append(tok)
        return tok

    def emit(self):
        nc = self.nc
        with contextlib.ExitStack() as es:
            sems = {}
            for e in self.q:
                sems[("e", e)] = es.enter_context(nc.semaphore("s_" + e))
            for qn in self.dcnt:
                for s in range(NDSEM):
                    sems[("d", qn, s)] = es.enter_context(nc.semaphore(f"d_{qn}_{s}"))
            fin = {}
            for k, v in self.dma_tokens:
                fin[k] = max(fin.get(k, 0), v)
            block = es.enter_context(nc.Block())

            def run(name, e):
                for w, fn, dkey in self.q[name]:
                    for k, v in w.items():
                        e.wait_ge(sems[k], v)
                    inst = getattr(e, fn[0])(*fn[1], **fn[2])
                    if dkey is None:
                        inst.then_inc(sems[("e", name)], 1)
                    else:
                        inst.then_inc(sems[dkey], 16)
                if name == "sp":
                    for k, v in fin.items():
                        e.wait_ge(sems[k], v)

            @block.tensor
            def _(e):
                run("pe", e)

            @block.scalar
            def _(e):
                run("act", e)

            @block.vector
            def _(e):
                run("dve", e)

            @block.gpsimd
            def _(e):
                run("pool", e)

            @block.sync
            def _(e):
                run("sp", e)


class Ctx:
    def __init__(self, nc, es):
        self.nc = nc
        self.es = es
        self.P = Prog(nc)
        self.n = 0

    def sb(self, shape, dt, name=None):
        self.n += 1
        return self.es.enter_context(self.nc.sbuf_tensor(f"sb{self.n}_{name or 'x'}", list(shape), dt))

    def ps(self, shape, dt, name=None):
        self.n += 1
        return self.es.enter_context(self.nc.psum_tensor(f"ps{self.n}_{name or 'x'}", list(shape), dt))

    def dram(self, name, shape, dt, kind):
        return self.nc.dram_tensor(name, list(shape), dt, kind=kind).ap()


def dma_rr(C):
    C._rr = getattr(C, "_rr", 0) + 1
    return "sp" if C._rr % 2 else POOLD


def alloc_common(C):
    G = {}
    G["ident"] = C.sb([128, 128], BF16, "ident_sb"); G["t_ident"] = TK()
    G["identf"] = C.sb([128, 128], F32, "identf")
    G["ps"] = [C.ps([128, 512], F32, f"psb{i}") for i in range(7)]; G["t_ps"] = [TK(True) for _ in range(7)]
    G["pst"] = C.ps([128, 1024], BF16, "pstr"); G["t_pst"] = TK(True)
    G["small"] = C.sb([128, 64], F32, "small"); G["t_small"] = TK()
    return G


def alloc_m1(C, G=None):
    R = dict(G) if G is not None else alloc_common(C)
    R["qT"] = C.sb([128, T], BF16, "qT"); R["t_qT"] = [TK() for _ in range(17)]
    R["kT"] = C.sb([128, T], BF16, "kT"); R["t_kT"] = [TK() for _ in range(17)]
    R["V"] = C.sb([128, NT, 132], BF16, "Vaug"); R["t_V"] = [TK() for _ in range(NT)]
    R["zg"] = C.sb([128, NT, 128], BF16, "zg"); R["t_zg"] = [TK() for _ in range(NT)]
    R["hbuf"] = [C.sb([128, 8, 512], BF16, f"hbuf{i}") for i in range(2)]; R["t_hbuf"] = [TK(), TK()]
    R["wst"] = [C.sb([128, 8, 128], F32, f"wst{i}") for i in range(2)]; R["t_wst"] = [TK(), TK()]
    R["W"] = C.sb([128, 8, 768], BF16, "Wbf"); R["t_W"] = TK()
    R["ystage"] = [C.sb([128, 512], BF16, f"ystage{i}") for i in range(2)]; R["t_ystage"] = [TK(), TK()]
    return R


def load_ident(C, R, ident_d):
    P = C.P
    P.dma("sp", [], [R["t_ident"]], R["identf"][:], ident_d[:, :])
    P.op("dve", "tensor_copy", [R["t_ident"]], [R["t_ident"]], out=R["ident"][:], in_=R["identf"][:])


def load_weights(C, R, w_d, ncols):
    P = C.P
    wv = w_d.rearrange("(k p) n -> p k n", p=128)
    i = 0
    for c0 in range(0, ncols, 128):
        cn = min(128, ncols - c0)
        st = R["wst"][i % 2]; tst = R["t_wst"][i % 2]
        P.dma(dma_rr(C), [], [tst], st[:, :, 0:cn], wv[:, :, c0:c0 + cn])
        P.op(POOLC, "tensor_copy", [tst], [R["t_W"]], out=R["W"][:, :, c0:c0 + cn], in_=st[:, :, 0:cn])
        i += 1


def project(C, R, hT_d, fm_groups, tm_cols, fm_cb, tm_cb, pre_chunk=None):
    P = C.P
    hv = hT_d.rearrange("(k p) t -> p k t", p=128)
    W = R["W"]
    for ci, (t0, n) in enumerate(CHUNKS):
        hb = R["hbuf"][ci % 2]; thb = R["t_hbuf"][ci % 2]
        P.dma("sp", [], [thb], hb[:, 0:4, 0:n], hv[:, 0:4, t0:t0 + n])
        P.dma(POOLD, [], [thb], hb[:, 4:8, 0:n], hv[:, 4:8, t0:t0 + n])
        if pre_chunk is not None:
            pre_chunk(ci, t0, n)
        for j, c0 in enumerate(fm_groups):
            pb = R["ps"][j]; tpb = R["t_ps"][j]
            for k in range(8):
                P.op("pe", "matmul", [R["t_W"], thb], [tpb], pb[:, 0:n], lhsT=W[:, k, c0:c0 + 128], rhs=hb[:, k, 0:n],
                     start=(k == 0), stop=(k == 7))
            fm_cb(j, ci, t0, n, pb, tpb)
        if tm_cols is not None:
            c0, nc_ = tm_cols
            for tt in range(n // 128):
                ti = t0 // 128 + tt
                bi = max(4, len(fm_groups)) + (ti % 2)
                pb = R["ps"][bi]; tpb = R["t_ps"][bi]
                for k in range(8):
                    P.op("pe", "matmul", [R["t_W"], thb], [tpb], pb[:, 0:nc_], lhsT=hb[:, k, tt * 128:(tt + 1) * 128],
                         rhs=W[:, k, c0:c0 + nc_], start=(k == 0), stop=(k == 7))
                tm_cb(ti, pb, tpb)


def emit_yT(C, R, src_tiles, t0, n, yT_d, row0):
    P = C.P
    C._ys = getattr(C, "_ys", 0) + 1
    ys = R["ystage"][C._ys % 2]; tys = R["t_ystage"][C._ys % 2]
    for j, (ap, tk) in enumerate(src_tiles):
        P.op("pe", "transpose", [tk, R["t_ident"]], [R["t_pst"]], R["pst"][:, j * 128:(j + 1) * 128], ap, R["ident"][:])
    P.op("dve", "tensor_copy", [R["t_pst"]], [tys], out=ys[:, 0:n], in_=R["pst"][:, 0:n])
    P.dma("sp", [tys], [], yT_d[row0:row0 + 128, t0:t0 + n], ys[:, 0:n])


def pass_A(C, R, l, hT_d, wA_d, cos_d, sin_d, lamq_d, lamk_d, dag_d, yT_d, yrow=0):
    P = C.P
    lam_init = 0.8 - 0.6 * math.exp(-0.3 * l)
    load_weights(C, R, wA_d, 768)
    sm = R["small"]; tsm = R["t_small"]
    lq = C.sb([128, 128], F32, "lq"); lk = C.sb([128, 128], F32, "lk"); dag = C.sb([128, 128], F32, "dag")
    ones = C.sb([128, 128], F32, "onesA"); t_ones = TK()
    t_l = TK(); t_dag = TK()
    P.dma("sp", [], [t_l], lq[:], lamq_d[:, :])
    P.dma("sp", [], [t_l], lk[:], lamk_d[:, :])
    P.dma("sp", [], [t_dag], dag[:], dag_d[:, :])
    P.op(POOLC, "memset", [], [t_ones], ones[:], 1.0)
    P.op("dve", "tensor_tensor", [t_l], [t_l], out=lq[:], in0=lq[:], in1=lk[:], op=ALU.mult)
    P.op("dve", "reduce_sum", [t_l], [tsm], out=sm[:, 0:2], in_=lq[:].rearrange("p (c d) -> p c d", c=2), axis=AX.X)
    P.op("act", "activation", [tsm], [tsm], out=sm[:, 2:4], in_=sm[:, 0:2], func=AF.Exp)
    P.op("dve", "tensor_tensor", [tsm], [tsm], out=sm[:, 4:5], in0=sm[:, 3:4], in1=sm[:, 2:3], op=ALU.subtract)
    P.op("dve", "tensor_scalar_add", [tsm], [tsm], out=sm[:, 4:5], in0=sm[:, 4:5], scalar1=-lam_init)
    P.op("dve", "tensor_scalar_mul", [t_dag], [t_dag], out=dag[:], in0=dag[:], scalar1=(1.0 - lam_init))

    q0z = R["qT"]
    q1z = C.sb([128, T], BF16, "q1z")
    P.op(POOLC, "memset", [], R["t_qT"], q0z[64:128, :], 0.0)
    P.op(POOLC, "memset", [], R["t_qT"], q1z[0:64, :], 0.0)
    zgT = R["zg"][:].rearrange("p a b -> p (a b)")
    t_zgT = R["t_zg"][:17]

    cosb = [C.sb([128, 512], F32, f"cosb{i}") for i in range(2)]
    sinb = [C.sb([128, 512], F32, f"sinb{i}") for i in range(2)]
    t_cs = [TK(), TK()]
    tmp1 = C.sb([128, 512], F32, "ropet1"); tmp2 = C.sb([128, 512], F32, "ropet2"); t_tmp = TK()
    ztmp = C.sb([128, 512], F32, "ztmp"); t_ztmp = TK()

    def pre_chunk(ci, t0, n):
        P.dma("sp", [], [t_cs[ci % 2]], cosb[ci % 2][:, 0:n], cos_d[:, t0:t0 + n])
        P.dma(POOLD, [], [t_cs[ci % 2]], sinb[ci % 2][:, 0:n], sin_d[:, t0:t0 + n])

    def fm_cb(j, ci, t0, n, pb, tpb):
        if j == 4:
            P.op("act", "activation", [tpb], [t_ztmp], out=ztmp[:, 0:n], in_=pb[:, 0:n], func=AF.Silu)
            P.op("dve", "tensor_scalar", [t_ztmp, t_dag], [t_zgT[ci]], out=zgT[:, t0:t0 + n], in0=ztmp[:, 0:n], scalar1=dag[:, 0:1], scalar2=None, op0=ALU.mult)
        elif j % 2 == 0:
            P.op("dve", "tensor_tensor", [tpb, t_cs[ci % 2]], [t_tmp], out=tmp1[:, 0:n], in0=pb[:, 0:n], in1=cosb[ci % 2][:, 0:n], op=ALU.mult)
        else:
            P.op("dve", "tensor_tensor", [tpb, t_cs[ci % 2]], [t_tmp], out=tmp2[:, 0:n], in0=pb[:, 0:n], in1=sinb[ci % 2][:, 0:n], op=ALU.mult)
            if j == 1:
                P.op("dve", "tensor_tensor", [t_tmp], [R["t_qT"][ci]], out=q0z[0:64, t0:t0 + n], in0=tmp1[0:64, 0:n], in1=tmp2[0:64, 0:n], op=ALU.add)
                P.op("dve", "tensor_tensor", [t_tmp], [R["t_qT"][ci]], out=q1z[64:128, t0:t0 + n], in0=tmp1[64:128, 0:n], in1=tmp2[64:128, 0:n], op=ALU.add)
            else:
                P.op("dve", "tensor_tensor", [t_tmp], [R["t_kT"][ci]], out=R["kT"][:, t0:t0 + n], in0=tmp1[:, 0:n], in1=tmp2[:, 0:n], op=ALU.add)

    def tm_cb(ti, pb, tpb):
        P.op("act", "copy", [tpb], [R["t_V"][ti]], out=R["V"][:, ti, 0:128], in_=pb[:, 0:128])

    project(C, R, hT_d, [0, 128, 256, 384, 512], (640, 128), fm_cb, tm_cb, pre_chunk)

    qz = [q0z, q1z]
    pbuf = [C.sb([128, 512], BF16, f"pbuf{i}") for i in range(4)]; t_pbuf = [TK() for _ in range(4)]
    sacc = [C.sb([128, 512], F32, f"sacc{c}") for c in range(2)]; t_sacc = [TK(), TK()]
    rr = [C.sb([128, 512], F32, f"rr{c}") for c in range(2)]; t_rr = [TK(), TK()]
    osb = C.sb([128, 512], F32, "osb"); t_osb = TK()
    osq = C.sb([128, 512], F32, "osq"); t_osq = TK()
    kci = lambda kt: 0 if kt < 2 else 1 + (kt - 2) // 4
    ps = R["ps"]; tps = R["t_ps"]
    for ci, (t0, n) in enumerate(CHUNKS):
        kts = [0, 1] if ci == 0 else list(range(NT))

        def s_mm(i):
            kt = kts[i]
            for c in range(2):
                bk = (i % 2) * 2 + c
                P.op("pe", "matmul", [R["t_kT"][kci(kt)], R["t_qT"][ci]], [tps[bk]], ps[bk][:, 0:n],
                     lhsT=R["kT"][:, kt * 128:(kt + 1) * 128], rhs=qz[c][:, t0:t0 + n], start=True, stop=True)

        s_mm(0)
        for i, kt in enumerate(kts):
            if i + 1 < len(kts):
                s_mm(i + 1)
            for c in range(2):
                bk = (i % 2) * 2 + c
                P.op("act", "activation", [tps[bk]], [t_pbuf[bk]], out=pbuf[bk][:, 0:n], in_=ps[bk][:, 0:n], func=AF.Exp, scale=0.125)
            for c in range(2):
                bk = (i % 2) * 2 + c
                P.op("pe", "matmul", [t_pbuf[bk], R["t_V"][kt]], [tps[4 + c]], ps[4 + c][:, 0:n], lhsT=R["V"][:, kt, 0:128], rhs=pbuf[bk][:, 0:n],
                     start=(i == 0), stop=(i == len(kts) - 1))
                if i == 0:
                    P.op("dve", "tensor_copy", [t_pbuf[bk]], [t_sacc[c]], out=sacc[c][:, 0:n], in_=pbuf[bk][:, 0:n])
                else:
                    P.op("dve", "tensor_tensor", [t_pbuf[bk], t_sacc[c]], [t_sacc[c]], out=sacc[c][:, 0:n], in0=sacc[c][:, 0:n], in1=pbuf[bk][:, 0:n], op=ALU.add)
        for c in range(2):
            P.op("pe", "matmul", [t_sacc[c], t_ones], [tps[6]], ps[6][:, 0:n], lhsT=ones[:], rhs=sacc[c][:, 0:n], start=True, stop=True)
            P.op("dve", "reciprocal", [tps[6]], [t_rr[c]], out=rr[c][:, 0:n], in_=ps[6][:, 0:n])
        P.op("dve", "tensor_scalar", [t_rr[1], tsm], [t_rr[1]], out=rr[1][:, 0:n], in0=rr[1][:, 0:n], scalar1=sm[:, 4:5], scalar2=None, op0=ALU.mult)
        P.op("dve", "tensor_tensor", [tps[4], t_rr[0]], [t_osb], out=osb[:, 0:n], in0=ps[4][:, 0:n], in1=rr[0][:, 0:n], op=ALU.mult)
        P.op("dve", "tensor_tensor", [tps[5], t_rr[1]], [t_rr[1]], out=rr[1][:, 0:n], in0=ps[5][:, 0:n], in1=rr[1][:, 0:n], op=ALU.mult)
        P.op("dve", "tensor_tensor", [t_osb, t_rr[1]], [t_osb], out=osb[:, 0:n], in0=osb[:, 0:n], in1=rr[1][:, 0:n], op=ALU.add)
        P.op("act", "activation", [t_osb], [t_osq], out=osq[:, 0:n], in_=osb[:, 0:n], func=AF.Square)
        P.op("pe", "matmul", [t_osq, t_ones], [tps[6]], ps[6][:, 0:n], lhsT=ones[:], rhs=osq[:, 0:n], start=True, stop=True)
        P.op("act", "activation", [tps[6]], [t_osq], out=osq[:, 0:n], in_=ps[6][:, 0:n], func=AF.Ln, scale=1.0 / 128, bias=EPS)
        P.op("act", "activation", [t_osq], [t_osq], out=osq[:, 0:n], in_=osq[:, 0:n], func=AF.Exp, scale=-0.5)
        P.op("dve", "tensor_tensor", [t_osb, t_osq], [t_osb], out=osb[:, 0:n], in0=osb[:, 0:n], in1=osq[:, 0:n], op=ALU.mult)
        C._ys = getattr(C, "_ys", 0) + 1
        ys = R["ystage"][C._ys % 2]; tys = R["t_ystage"][C._ys % 2]
        P.op("dve", "tensor_tensor", [t_osb, t_zgT[ci]], [tys], out=ys[:, 0:n], in0=osb[:, 0:n], in1=zgT[:, t0:t0 + n], op=ALU.mult)
        P.dma("sp", [tys], [], yT_d[yrow:yrow + 128, t0:t0 + n], ys[:, 0:n])


def rms_scale(P, sm, tsm, ssq_col, out_col, n_feat):
    P.op("act", "activation", [tsm], [tsm], out=sm[:, out_col:out_col + 1], in_=sm[:, ssq_col:ssq_col + 1], func=AF.Ln, scale=1.0 / n_feat, bias=EPS)
    P.op("act", "activation", [tsm], [tsm], out=sm[:, out_col:out_col + 1], in_=sm[:, out_col:out_col + 1], func=AF.Exp, scale=-0.5)


def pass_B(C, R, l, hT_d, wB_d, convp_d, gateb_d, mlg_d, tri_d, yT_d, yrow=128):
    P = C.P
    sm = R["small"]; tsm = R["t_small"]
    load_weights(C, R, wB_d, 644)
    convp = C.sb([128, 8], F32, "convp"); gateb = C.sb([128, 4], F32, "gateb"); mlg = C.sb([128, 128], F32, "mlg")
    tri = C.sb([128, 384], F32, "tri")
    t_cst = TK()
    P.dma("sp", [], [t_cst], convp[:], convp_d[:, :])
    P.dma("sp", [], [t_cst], gateb[:], gateb_d[:, :])
    P.dma("sp", [], [t_cst], mlg[:], mlg_d[:, :])
    P.dma("sp", [], [t_cst], tri[:], tri_d[:, :])
    P.op(POOLC, "memset", [], R["t_V"], R["V"][:, :, 128:129], 1.0)
    Ktok = C.sb([128, NT, 128], BF16, "Ktok"); t_Ktok = [TK() for _ in range(NT)]
    hs = C.sb([128, NT, 128], F32, "hs"); t_hs = [TK() for _ in range(NT)]
    g4 = C.sb([128, 4, NT], F32, "g4"); t_g4 = TK()
    Rb = [C.sb([128, 520], F32, f"Rb{i}") for i in range(2)]; t_Rb = [TK(), TK()]
    ctmp = C.sb([128, 516], F32, "ctmp"); t_ctmp = TK()
    ctmp2 = C.sb([128, 516], F32, "ctmp2"); t_ctmp2 = TK()
    zt12 = C.sb([128, 256], F32, "zt12"); t_zt = TK()
    for i in range(2):
        P.op(POOLC, "memset", [], [t_Rb[i]], Rb[i][:], 0.0)

    def fm_cb(j, ci, t0, n, pb, tpb):
        rb = Rb[j]; trb = t_Rb[j]
        first = ci in (0, 1)
        last = ci in (0, 16)
        if first:
            P.op(POOLC, "memset", [], [trb], rb[:, 0:2], 0.0)
        if last:
            P.op(POOLC, "memset", [], [trb], rb[:, n + 2:n + 3], 0.0)
        P.op("act", "copy", [tpb], [trb], out=rb[:, 2:2 + n], in_=pb[:, 0:n])
        ta = t0 - 1 + (1 if first else 0)
        tb = t0 + n - 1 + (1 if last else 0)
        m = tb - ta
        o = ta - t0
        wc = 4 * j
        P.op("dve", "tensor_scalar", [trb, t_cst], [t_ctmp], out=ctmp[:, 0:m], in0=rb[:, o + 1:o + 1 + m], scalar1=convp[:, wc:wc + 1], scalar2=None, op0=ALU.mult)
        P.op("dve", "scalar_tensor_tensor", [trb, t_cst, t_ctmp], [t_ctmp], out=ctmp[:, 0:m], in0=rb[:, o + 2:o + 2 + m], scalar=convp[:, wc + 1:wc + 2],
             in1=ctmp[:, 0:m], op0=ALU.mult, op1=ALU.add)
        P.op("dve", "scalar_tensor_tensor", [trb, t_cst, t_ctmp], [t_ctmp], out=ctmp[:, 0:m], in0=rb[:, o + 3:o + 3 + m], scalar=convp[:, wc + 2:wc + 3],
             in1=ctmp[:, 0:m], op0=ALU.mult, op1=ALU.add)
        dst = R["qT"] if j == 0 else R["kT"]
        tds = R["t_qT"] if j == 0 else R["t_kT"]
        wr = [tds[ci]] + ([tds[ci - 1]] if not first else [])
        bc = convp[:, 3:4] if j == 0 else convp[:, 7:8]
        P.op("act", "activation", [t_ctmp, t_cst], [t_ctmp2], out=ctmp2[:, 0:m], in_=ctmp[:, 0:m], func=AF.Sigmoid, bias=bc)
        if j == 0:
            P.op("dve", "scalar_tensor_tensor", [t_ctmp, t_ctmp2, t_cst], [t_ctmp2], out=ctmp2[:, 0:m], in0=ctmp[:, 0:m], scalar=bc, in1=ctmp2[:, 0:m],
                 op0=ALU.add, op1=ALU.mult)
            P.op("dve", "tensor_scalar_mul", [t_ctmp2], wr, out=dst[:, ta:tb], in0=ctmp2[:, 0:m], scalar1=128.0 ** -0.5)
        else:
            P.op("dve", "scalar_tensor_tensor", [t_ctmp, t_ctmp2, t_cst], wr, out=dst[:, ta:tb], in0=ctmp[:, 0:m], scalar=bc, in1=ctmp2[:, 0:m],
                 op0=ALU.add, op1=ALU.mult)
        if not last:
            P.op(POOLC, "tensor_copy", [trb], [trb], out=rb[:, 0:2], in_=rb[:, n:n + 2])

    def tm_cb(ti, pb, tpb):
        P.op("act", "copy", [tpb], [R["t_V"][ti]], out=R["V"][:, ti, 0:128], in_=pb[:, 0:128])
        P.op("act", "activation", [tpb], [t_zt], out=zt12[:], in_=pb[:, 128:384], func=AF.Sigmoid)
        P.op("dve", "tensor_tensor", [tpb, t_cst], [t_g4], out=g4[:, :, ti], in0=pb[:, 384:388], in1=gateb[:], op=ALU.add)
        P.op("dve", "tensor_tensor", [t_zt], [t_zt], out=zt12[:, 0:128], in0=zt12[:, 0:128], in1=zt12[:, 128:256], op=ALU.mult)
        P.op("dve", "tensor_tensor", [tpb, t_zt], [t_zt], out=zt12[:, 0:128], in0=pb[:, 256:384], in1=zt12[:, 0:128], op=ALU.mult)
        P.op("dve", "tensor_tensor", [t_zt, t_cst], [R["t_zg"][ti]], out=R["zg"][:, ti, :], in0=zt12[:, 0:128], in1=mlg[:], op=ALU.mult)

    project(C, R, hT_d, [0, 128], (256, 388), fm_cb, tm_cb)

    for t0 in range(0, NT, 8):
        nn = min(8, NT - t0)
        kci = lambda ti: 0 if ti < 2 else 1 + (ti - 2) // 4
        for j in range(nn):
            ti = t0 + j
            P.op("pe", "transpose", [R["t_kT"][kci(ti)], R["t_ident"]], [R["t_pst"]], R["pst"][:, j * 128:(j + 1) * 128],
                 R["kT"][:, ti * 128:(ti + 1) * 128], R["ident"][:])
        P.op("dve", "tensor_copy", [R["t_pst"]], [t_Ktok[t0 + j] for j in range(nn)], out=Ktok[:, t0:t0 + nn, :],
             in_=R["pst"][:, 0:nn * 128].rearrange("p (a b) -> p a b", b=128))

    LFP = C.sb([128, 2, NT], F32, "LFP"); t_LFP = TK()
    GA = C.sb([128, 2, NT], F32, "GA"); GU = C.sb([128, 2, NT], F32, "GU"); GL = C.sb([128, 2, NT], F32, "GL"); t_G = TK()
    GAi = C.sb([128, 2, NT], F32, "GAi")
    hst_ = C.sb([128, 128], F32, "hstmp"); t_hst = TK()
    for d in range(2):
        P.op("act", "activation", [t_g4], [t_LFP], out=LFP[:, d, :], in_=g4[:, 2 * d + 1, :], func=AF.Exp, scale=-1.0)
    for d in range(2):
        P.op("act", "activation", [t_LFP], [t_LFP], out=LFP[:, d, :], in_=LFP[:, d, :], func=AF.Ln, bias=1.0)
    pg = R["ps"][6]; tpg = R["t_ps"][6]
    for d in range(2):
        P.op("pe", "matmul", [t_LFP, t_cst], [tpg], pg[:, d * NT:(d + 1) * NT], lhsT=tri[:, d * 128:(d + 1) * 128], rhs=LFP[:, d, :], start=True, stop=True,
             skip_group_check=True)
        P.op("pe", "matmul", [t_LFP, t_cst], [tpg], pg[:, (2 + d) * NT:(3 + d) * NT], lhsT=tri[:, 256:384], rhs=LFP[:, d, :], start=True, stop=True,
             skip_group_check=True)
    for d in range(2):
        P.op("act", "activation", [tpg], [t_G], out=GA[:, d, :], in_=pg[:, d * NT:(d + 1) * NT], func=AF.Exp, scale=-1.0)
        P.op("dve", "tensor_tensor", [tpg, t_g4], [t_G], out=GU[:, d, :], in0=pg[:, d * NT:(d + 1) * NT], in1=g4[:, 2 * d, :], op=ALU.add)
        P.op("act", "activation", [t_G], [t_G], out=GU[:, d, :], in_=GU[:, d, :], func=AF.Exp)
        P.op("act", "activation", [tpg], [t_G], out=GL[:, d, :], in_=pg[:, (2 + d) * NT:(3 + d) * NT], func=AF.Exp, scale=-1.0)
        P.op("act", "activation", [tpg], [t_G], out=GAi[:, d, :], in_=pg[:, d * NT:(d + 1) * NT], func=AF.Exp)

    Cst = [C.sb([128, 132], F32, f"Cst{d}") for d in range(2)]; t_Cst = [TK(), TK()]
    Cb = [C.sb([128, 132], BF16, f"Cb{d}") for d in range(2)]; t_Cb = [TK(), TK()]
    Vp = [[C.sb([128, 132], BF16, f"Vp{d}{i}") for i in range(2)] for d in range(2)]; t_Vp = [[TK(), TK()], [TK(), TK()]]
    Sm = [[C.sb([128, 128], BF16, f"Sm{d}{i}") for i in range(2)] for d in range(2)]; t_Sm = [[TK(), TK()], [TK(), TK()]]
    Up = [[C.sb([128, 132], F32, f"Up{d}{i}") for i in range(2)] for d in range(2)]; t_Up = [[TK(), TK()], [TK(), TK()]]
    for d in range(2):
        P.op(POOLC, "memset", [], [t_Cst[d]], Cst[d][:], 0.0)
        P.op(POOLC, "memset", [], [t_Cb[d]], Cb[d][:], 0.0)
    order = [list(range(NT)), [1, 0] + list(range(NT - 1, 1, -1))]
    inited = set()
    kci = lambda ti: 0 if ti < 2 else 1 + (ti - 2) // 4
    for step in range(NT):
        for d in range(2):
            ti = order[d][step]
            rr = step % 2
            vp = Vp[d][rr]; tvp = t_Vp[d][rr]; smb = Sm[d][rr]; tsmb = t_Sm[d][rr]
            psS = R["ps"][0 + d]; tpsS = R["t_ps"][0 + d]
            psA = R["ps"][2 + d]; tpsA = R["t_ps"][2 + d]
            psU = R["ps"][4 + d]; tpsU = R["t_ps"][4 + d]
            P.op("dve", "tensor_scalar", [R["t_V"][ti], t_G], [tvp], out=vp[:, 0:129], in0=R["V"][:, ti, 0:129], scalar1=GU[:, d, ti:ti + 1], scalar2=None, op0=ALU.mult)
            P.op("pe", "matmul", [R["t_kT"][kci(ti)], R["t_qT"][kci(ti)]], [tpsS], psS[:, 0:128], lhsT=R["kT"][:, ti * 128:(ti + 1) * 128],
                 rhs=R["qT"][:, ti * 128:(ti + 1) * 128], start=True, stop=True)
            P.op("dve", "tensor_tensor", [tpsS, t_cst], [tsmb], out=smb[:], in0=psS[:, 0:128], in1=tri[:, d * 128:(d + 1) * 128], op=ALU.mult)
            P.op("pe", "matmul", [tsmb, tvp], [tpsA], psA[:, 0:129], lhsT=smb[:], rhs=vp[:, 0:129], start=True, stop=False)
            P.op("pe", "matmul", [R["t_qT"][kci(ti)], t_Cb[d]], [tpsA], psA[:, 0:129], lhsT=R["qT"][:, ti * 128:(ti + 1) * 128], rhs=Cb[d][:, 0:129],
                 start=False, stop=True)
            P.op("pe", "matmul", [t_Ktok[ti], tvp], [tpsU], psU[:, 0:129], lhsT=Ktok[:, ti, :], rhs=vp[:, 0:129], start=True, stop=True)
            c0 = 20 + 4 * d
            P.op("dve", "tensor_scalar", [tpsA, t_G], [tsm], out=sm[:, c0:c0 + 1], in0=psA[:, 128:129], scalar1=GAi[:, d, ti:ti + 1], scalar2=None, op0=ALU.max)
            P.op("dve", "tensor_scalar", [tpsA, tsm], [tsm], out=sm[:, c0 + 1:c0 + 2], in0=psA[:, 128:129], scalar1=-1.0, scalar2=sm[:, c0:c0 + 1],
                 op0=ALU.mult, op1=ALU.max)
            P.op("dve", "reciprocal", [tsm], [tsm], out=sm[:, c0 + 2:c0 + 3], in_=sm[:, c0 + 1:c0 + 2])
            if ti not in inited:
                inited.add(ti)
                P.op("dve", "tensor_scalar", [tpsA, tsm], [t_hs[ti]], out=hs[:, ti, :], in0=psA[:, 0:128], scalar1=sm[:, c0 + 2:c0 + 3], scalar2=None, op0=ALU.mult)
            else:
                P.op("dve", "scalar_tensor_tensor", [tpsA, tsm, t_hs[ti]], [t_hs[ti]], out=hs[:, ti, :], in0=psA[:, 0:128], scalar=sm[:, c0 + 2:c0 + 3],
                     in1=hs[:, ti, :], op0=ALU.mult, op1=ALU.add)
            up = Up[d][rr]; tup = t_Up[d][rr]
            P.op("act", "activation", [tpsU, t_G], [tup], out=up[:, 0:129], in_=psU[:, 0:129], func=AF.Copy, scale=GL[:, d, ti:ti + 1])
            P.op("dve", "scalar_tensor_tensor", [tup, t_G, t_Cst[d]], [t_Cst[d]], out=Cst[d][:, 0:129], in0=Cst[d][:, 0:129], scalar=GL[:, d, ti:ti + 1],
                 in1=up[:, 0:129], op0=ALU.mult, op1=ALU.add)
            P.op("act", "copy", [t_Cst[d]], [t_Cb[d]], out=Cb[d][:, 0:129], in_=Cst[d][:, 0:129])

    ysb = [C.sb([128, 128], BF16, f"ysbB{i}") for i in range(4)]; t_ysb = [TK() for _ in range(4)]
    junk = C.sb([128, 128], F32, "junkB"); t_junk = TK()
    for ci, (t0, n) in enumerate(CHUNKS):
        tiles = []
        for j in range(n // 128):
            ti = t0 // 128 + j
            P.op("act", "activation", [t_hs[ti]], [t_junk, tsm], out=junk[:], in_=hs[:, ti, :], func=AF.Square, accum_out=sm[:, 30:31])
            rms_scale(P, sm, tsm, 30, 31, 128)
            P.op("dve", "scalar_tensor_tensor", [t_hs[ti], tsm, R["t_zg"][ti]], [t_ysb[j]], out=ysb[j][:], in0=hs[:, ti, :], scalar=sm[:, 31:32],
                 in1=R["zg"][:, ti, :], op0=ALU.mult, op1=ALU.mult)
            tiles.append((ysb[j][:], t_ysb[j]))
        emit_yT(C, R, tiles, t0, n, yT_d, yrow)


def pass_C(C, R, l, hT_d, wC_d, biasg_d, maskc_d, yT_d, yrow=256):
    P = C.P
    sm = R["small"]; tsm = R["t_small"]
    load_weights(C, R, wC_d, 512)
    bias = C.sb([128, 6400], F32, "biasC"); t_bias = TK()
    mtmp = C.sb([128, 640], F32, "mtmpC"); t_mtmp = TK()
    P.dma("sp", [], [t_bias], bias[:, 0:3200], biasg_d[:, 0:3200])
    P.dma(POOLD, [], [t_bias], bias[:, 3200:6400], biasg_d[:, 3200:6400])
    for i in range(10):
        P.dma("sp", [], [t_mtmp], mtmp[:], maskc_d[:, i * 640:(i + 1) * 640])
        P.op("dve", "tensor_tensor", [t_mtmp, t_bias], [t_bias], out=bias[:, i * 640:(i + 1) * 640], in0=bias[:, i * 640:(i + 1) * 640], in1=mtmp[:], op=ALU.add)
    P.op(POOLC, "memset", [], R["t_V"], R["V"][:, :, 64:65], 1.0)
    P.op(POOLC, "memset", [], R["t_V"], R["V"][:, :, 129:130], 1.0)

    def fm_cb(j, ci, t0, n, pb, tpb):
        dst = R["qT"] if j == 0 else R["kT"]
        tds = R["t_qT"] if j == 0 else R["t_kT"]
        P.op("act", "copy", [tpb], [tds[ci]], out=dst[:, t0:t0 + n], in_=pb[:, 0:n])

    def tm_cb(ti, pb, tpb):
        P.op("dve", "tensor_copy", [tpb], [R["t_V"][ti]], out=R["V"][:, ti, 0:64], in_=pb[:, 0:64])
        P.op("dve", "tensor_copy", [tpb], [R["t_V"][ti]], out=R["V"][:, ti, 65:129], in_=pb[:, 64:128])
        P.op("act", "activation", [tpb], [R["t_zg"][ti]], out=R["zg"][:, ti, :], in_=pb[:, 128:256], func=AF.Silu)

    project(C, R, hT_d, [0, 128], (256, 256), fm_cb, tm_cb)

    tmpS = [C.sb([128, 640], F32, f"tmpS{i}") for i in range(2)]; t_tmpS = [TK(), TK()]
    Pl = [C.sb([128, 896], BF16, f"PlC{i}") for i in range(2)]; t_Pl = [TK(), TK()]
    ysb = [C.sb([128, 128], BF16, f"ysbC{i}") for i in range(4)]; t_ysb = [TK() for _ in range(4)]
    kci = lambda ti: 0 if ti < 2 else 1 + (ti - 2) // 4
    u = 0
    for ci, (t0, n) in enumerate(CHUNKS):
        tiles = []
        for jj in range(n // 128):
            ti = t0 // 128 + jj
            for hh in range(2):
                hs_ = slice(hh * 64, (hh + 1) * 64)
                bx = R["ps"][2 * (u % 2)]; tbx = R["t_ps"][2 * (u % 2)]
                by = R["ps"][2 * (u % 2) + 1]; tby = R["t_ps"][2 * (u % 2) + 1]
                pa = R["ps"][4 + u % 3]; tpa = R["t_ps"][4 + u % 3]
                pl = Pl[u % 2]; tpl = t_Pl[u % 2]
                ts_ = tmpS[u % 2]; tts = t_tmpS[u % 2]
                qsl = R["qT"][hs_, ti * 128:(ti + 1) * 128]
                if ti < 2:
                    keyt = []
                else:
                    j = ti - 2
                    kt0 = min(max(j - 2, 0), 59)
                    pat = 0 if j == 0 else 1 if j == 1 else 3 if j == 62 else 4 if j == 63 else 2
                    keyt = [2 + kt0 + a for a in range(5)]
                for a, kt in enumerate(keyt):
                    dst = bx[:, a * 128:(a + 1) * 128] if a < 4 else by[:, 0:128]
                    P.op("pe", "matmul", [R["t_kT"][kci(kt)], R["t_qT"][kci(ti)]], [tbx if a < 4 else tby], dst,
                         lhsT=R["kT"][hs_, kt * 128:(kt + 1) * 128], rhs=qsl, start=True, stop=True, skip_group_check=True)
                for a in range(2):
                    P.op("pe", "matmul", [R["t_kT"][0], R["t_qT"][kci(ti)]], [tby], by[:, (1 + a) * 128:(2 + a) * 128],
                         lhsT=R["kT"][hs_, a * 128:(a + 1) * 128], rhs=qsl, start=True, stop=True, skip_group_check=True)
                if keyt:
                    bo = (pat * 2 + hh) * 640
                    P.op("dve", "scalar_tensor_tensor", [tbx, t_bias], [tts], out=ts_[:, 0:512], in0=bx[:, 0:512], scalar=0.125, in1=bias[:, bo:bo + 512],
                         op0=ALU.mult, op1=ALU.add)
                    P.op("dve", "scalar_tensor_tensor", [tby, t_bias], [tts], out=ts_[:, 512:640], in0=by[:, 0:128], scalar=0.125, in1=bias[:, bo + 512:bo + 640],
                         op0=ALU.mult, op1=ALU.add)
                    P.op("act", "activation", [tts], [tpl], out=pl[:, 0:640], in_=ts_[:, 0:640], func=AF.Exp)
                P.op("act", "activation", [tby], [tpl], out=pl[:, 640:896], in_=by[:, 128:384], func=AF.Exp, scale=0.125)
                vs = slice(hh * 65, (hh + 1) * 65)
                allk = [(pl[:, a * 128:(a + 1) * 128], kt) for a, kt in enumerate(keyt)] + [(pl[:, 640 + a * 128:640 + (a + 1) * 128], a) for a in range(2)]
                for i, (pap, kt) in enumerate(allk):
                    P.op("pe", "matmul", [tpl, R["t_V"][kt]], [tpa], pa[:, 0:65], lhsT=pap, rhs=R["V"][:, kt, vs], start=(i == 0), stop=(i == len(allk) - 1))
                P.op("dve", "reciprocal", [tpa], [tsm], out=sm[:, 40:41], in_=pa[:, 64:65])
                P.op("dve", "scalar_tensor_tensor", [tpa, tsm, R["t_zg"][ti]], [t_ysb[jj]], out=ysb[jj][:, hs_], in0=pa[:, 0:64], scalar=sm[:, 40:41],
                     in1=R["zg"][:, ti, hs_], op0=ALU.mult, op1=ALU.mult)
                u += 1
            tiles.append((ysb[jj][:], t_ysb[jj]))
        emit_yT(C, R, tiles, t0, n, yT_d, yrow)


NTOK2 = 2304
CHUNKS2 = [(0, 256)] + [(256 + i * 512, 512) for i in range(4)]


def alloc_m2(C, G=None):
    M = dict(G) if G is not None else alloc_common(C)
    M["wst"] = [C.sb([128, 8, 128], F32, f"wst{i}") for i in range(3)]; M["t_wst"] = [TK(), TK(), TK()]
    M["xt"] = [C.sb([128, 1024], F32, f"xt{i}") for i in range(2)]; M["t_xt"] = [TK(), TK()]
    M["htmp"] = C.sb([128, 1024], F32, "htmp"); M["t_htmp"] = TK()
    M["hbf"] = C.sb([128, 1024], BF16, "hbf"); M["t_hbf"] = TK()
    M["hst"] = [C.sb([128, 8, 128], BF16, f"hst{i}") for i in range(2)]; M["t_hst"] = [TK(), TK()]
    M["junk"] = C.sb([128, 1024], F32, "junk"); M["t_junk"] = TK()
    M["mod"] = {}
    M["nrm"] = 0
    return M


def alloc_mod_out(C, M, key, parts):
    for part in parts:
        for v in range(2):
            M["mod"][(key, v, part)] = (C.sb([128, 1024], F32, f"mod_{key}_{v}_{part}"), TK())


def alloc_mod_tmp(C, M):
    M["wmst"] = C.sb([128, 8, 512], F32, "wmst"); M["t_wmst"] = TK()
    M["bst"] = C.sb([128, 512], F32, "bst"); M["t_bst"] = TK()
    M["screp"] = [C.sb([128, 8, 128], F32, f"screp{v}") for v in range(2)]; M["t_screp"] = TK()


def mod_vectors(C, M, key, crepL_d, crepC_d, wmod_d, bmodrep_d, parts, normgrep_d=None):
    P = C.P
    if "sc_done" not in M:
        M["sc_done"] = True
        for v, cd in enumerate((crepL_d, crepC_d)):
            P.dma("sp", [], [M["t_screp"]], M["screp"][v][:], cd[:, :, :])
            P.op("act", "activation", [M["t_screp"]], [M["t_screp"]], out=M["screp"][v][:], in_=M["screp"][v][:], func=AF.Silu)
    wv = wmod_d.rearrange("(k p) n -> p k n", p=128)
    pm = M["ps"][6]; tpm = M["t_ps"][6]
    if normgrep_d is not None:
        ng = C.sb([128, 1024], F32, "normg"); t_ng = TK()
        P.dma("sp", [], [t_ng], ng[:], normgrep_d[:, :])
    for part in parts:
        for half in range(2):
            c0 = part * 1024 + half * 512
            P.dma("sp", [], [M["t_wmst"]], M["wmst"][:, 0:4, :], wv[:, 0:4, c0:c0 + 512])
            P.dma(POOLD, [], [M["t_wmst"]], M["wmst"][:, 4:8, :], wv[:, 4:8, c0:c0 + 512])
            P.dma("sp", [], [M["t_bst"]], M["bst"][:], bmodrep_d[:, c0:c0 + 512])
            for v in range(2):
                dst, tdst = M["mod"][(key, v, part)]
                for k in range(8):
                    P.op("pe", "matmul", [M["t_screp"], M["t_wmst"]], [tpm], pm[:, :], lhsT=M["screp"][v][:, k, :], rhs=M["wmst"][:, k, :],
                         start=(k == 0), stop=(k == 7))
                P.op("dve", "tensor_tensor", [tpm, M["t_bst"]], [tdst], out=dst[:, half * 512:(half + 1) * 512], in0=pm[:, :], in1=M["bst"][:], op=ALU.add)
        if part == 1:
            for v in range(2):
                dst, tdst = M["mod"][(key, v, part)]
                P.op("dve", "scalar_tensor_tensor", [tdst, t_ng], [tdst], out=dst[:], in0=dst[:], scalar=1.0, in1=ng[:], op0=ALU.add, op1=ALU.mult)


def norm_tile(C, M, xt, t_xt, scale_bc, shift_bc, hT_view, i):
    P = C.P
    sm = M["small"]; tsm = M["t_small"]
    P.op("act", "activation", [t_xt], [M["t_junk"], tsm], out=M["junk"][:], in_=xt[:], func=AF.Square, accum_out=sm[:, 0:1])
    rms_scale(P, sm, tsm, 0, 1, 1024)
    P.op("dve", "scalar_tensor_tensor", [t_xt, tsm, scale_bc[1]], [M["t_htmp"]], out=M["htmp"][:], in0=xt[:], scalar=sm[:, 1:2], in1=scale_bc[0][:],
         op0=ALU.mult, op1=ALU.mult)
    P.op(POOLC, "tensor_tensor", [M["t_htmp"], shift_bc[1]], [M["t_hbf"]], out=M["hbf"][:], in0=M["htmp"][:], in1=shift_bc[0][:], op=ALU.add)
    M["nrm"] += 1
    hst = M["hst"][M["nrm"] % 2]; thst = M["t_hst"][M["nrm"] % 2]
    for k in range(8):
        P.op("pe", "transpose", [M["t_hbf"], M["t_ident"]], [M["t_pst"]], M["pst"][:, k * 128:(k + 1) * 128], M["hbf"][:, k * 128:(k + 1) * 128], M["ident"][:])
    P.op("act", "copy", [M["t_pst"]], [thst], out=hst[:], in_=M["pst"][:, :].rearrange("p (k t) -> p k t", t=128))
    P.dma("sp", [thst], [], hT_view[:, :, i * 128:(i + 1) * 128], hst[:])


def stage_N(C, M, key, x_d, hT_d, ntok=NTOK2):
    P = C.P
    hv = hT_d.rearrange("(k p) t -> p k t", p=128)
    for i in range(ntok // 128):
        v = 1 if i < 2 else 0
        xt = M["xt"][i % 2]; t_xt = M["t_xt"][i % 2]
        P.dma(POOLD, [], [t_xt], xt[:], x_d[i * 128:(i + 1) * 128, :])
        norm_tile(C, M, xt, t_xt, M["mod"][(key, v, 1)], M["mod"][(key, v, 0)], hv, i)


def load_w_bf16(C, M, w_d, dst, tdst, nk, ncols):
    P = C.P
    wv = w_d.rearrange("(k p) n -> p k n", p=128)
    i = 0
    for k0 in range(0, nk, 8):
        kn = min(8, nk - k0)
        for c0 in range(0, ncols, 128):
            st = M["wst"][i % 3]; tst = M["t_wst"][i % 3]
            P.dma(dma_rr(C), [], [tst], st[:, 0:kn, :], wv[:, k0:k0 + kn, c0:c0 + 128])
            if i % 3 == 0:
                P.op(POOLC, "tensor_copy", [tst], [tdst], out=dst[:, k0:k0 + kn, c0:c0 + 128], in_=st[:, 0:kn, :])
            elif i % 3 == 1:
                P.op("dve", "tensor_copy", [tst], [tdst], out=dst[:, k0:k0 + kn, c0:c0 + 128], in_=st[:, 0:kn, :])
            else:
                P.op("act", "copy", [tst], [tdst], out=dst[:, k0:k0 + kn, c0:c0 + 128], in_=st[:, 0:kn, :])
            i += 1


def stage_M2(C, M, key, x_d, hT_d, yT_d, wgm_d, wbr_d, wout_d, xout_d, next_key=None, hTn_d=None, finalg_d=None, out_d=None, chunks=None):
    P = C.P
    Wgm = C.sb([128, 8, 3072], BF16, "Wgm"); t_Wgm = TK()
    Wbr = C.sb([128, 12, 1024], BF16, "Wbr"); t_Wbr = TK()
    Wout = C.sb([128, 8, 1024], BF16, "Wout"); t_Wout = TK()
    load_w_bf16(C, M, wgm_d, Wgm, t_Wgm, 8, 3072)
    load_w_bf16(C, M, wbr_d, Wbr, t_Wbr, 12, 1024)
    load_w_bf16(C, M, wout_d, Wout, t_Wout, 8, 1024)
    hTc = C.sb([128, 8, 512], BF16, "hTc"); t_hTc = TK()
    yTc = C.sb([128, 12, 512], BF16, "yTc"); t_yTc = TK()
    uT = C.sb([128, 8, 512], BF16, "uT"); t_uT = TK()
    sg = C.sb([128, 512], F32, "sg"); t_sg = TK()
    uacc = C.sb([128, 512], F32, "uacc"); t_uacc = TK()
    ut = C.sb([128, 512], F32, "ut"); t_ut = TK()
    xn = [C.sb([128, 1024], F32, f"xn{i}") for i in range(2)]; t_xn = [TK(), TK()]
    if finalg_d is not None:
        fg = C.sb([128, 1024], F32, "fg"); t_fg = TK()
        P.dma("sp", [], [t_fg], fg[:], finalg_d[:, :])
        ost = [C.sb([128, 1024], F32, f"ost{i}") for i in range(2)]; t_ost = [TK(), TK()]
    hv = hT_d.rearrange("(k p) t -> p k t", p=128)
    yv = yT_d.rearrange("(k p) t -> p k t", p=128)
    hvn = hTn_d.rearrange("(k p) t -> p k t", p=128) if hTn_d is not None else None
    sm = M["small"]; tsm = M["t_small"]
    g = 0
    for ci, (t0, n) in enumerate(chunks or CHUNKS2):
        v = 1 if ci == 0 else 0
        gate, t_gate = M["mod"][(key, v, 2)]
        P.dma("sp", [], [t_hTc], hTc[:, :, 0:n], hv[:, :, t0:t0 + n])
        P.dma(POOLD, [], [t_yTc], yTc[:, :, 0:n], yv[:, :, t0:t0 + n])
        for oc in range(8):
            for nb in range(3):
                pg = M["ps"][g % 2]; tpg = M["t_ps"][g % 2]
                pp = M["ps"][2 + g % 2]; tpp = M["t_ps"][2 + g % 2]
                g += 1
                for k in range(8):
                    P.op("pe", "matmul", [t_Wgm, t_hTc], [tpg], pg[:, 0:n], lhsT=Wgm[:, k, nb * 1024 + oc * 128:nb * 1024 + (oc + 1) * 128], rhs=hTc[:, k, 0:n],
                         start=(k == 0), stop=(k == 7))
                for kk in range(4):
                    P.op("pe", "matmul", [t_Wbr, t_yTc], [tpp], pp[:, 0:n], lhsT=Wbr[:, nb * 4 + kk, oc * 128:(oc + 1) * 128], rhs=yTc[:, nb * 4 + kk, 0:n],
                         start=(kk == 0), stop=(kk == 3))
                P.op("act", "activation", [tpg], [t_sg], out=sg[:, 0:n], in_=pg[:, 0:n], func=AF.Sigmoid)
                if nb == 0:
                    P.op("dve", "tensor_tensor", [tpp, t_sg], [t_uacc], out=uacc[:, 0:n], in0=pp[:, 0:n], in1=sg[:, 0:n], op=ALU.mult)
                else:
                    P.op("dve", "tensor_tensor", [tpp, t_sg], [t_ut], out=ut[:, 0:n], in0=pp[:, 0:n], in1=sg[:, 0:n], op=ALU.mult)
                    if nb == 1:
                        P.op(POOLC, "tensor_tensor", [t_ut, t_uacc], [t_uacc], out=uacc[:, 0:n], in0=uacc[:, 0:n], in1=ut[:, 0:n], op=ALU.add)
                    else:
                        P.op(POOLC, "tensor_tensor", [t_ut, t_uacc], [t_uT], out=uT[:, oc, 0:n], in0=uacc[:, 0:n], in1=ut[:, 0:n], op=ALU.add)
        for tt in range(n // 128):
            i = t0 // 128 + tt
            xt = M["xt"][i % 2]; t_xt = M["t_xt"][i % 2]
            xo = xn[i % 2]; t_xo = t_xn[i % 2]
            P.dma(POOLD, [], [t_xt], xt[:], x_d[i * 128:(i + 1) * 128, :])
            for half in range(2):
                po = M["ps"][4 + half]; tpo = M["t_ps"][4 + half]
                cs = slice(half * 512, (half + 1) * 512)
                for oc in range(8):
                    P.op("pe", "matmul", [t_uT, t_Wout], [tpo], po[:, :], lhsT=uT[:, oc, tt * 128:(tt + 1) * 128], rhs=Wout[:, oc, cs], start=(oc == 0), stop=(oc == 7))
                P.op("dve", "tensor_tensor", [tpo, t_gate], [t_xo], out=xo[:, cs], in0=po[:, :], in1=gate[:, cs], op=ALU.mult)
                P.op(POOLC, "tensor_tensor", [t_xo, t_xt], [t_xo], out=xo[:, cs], in0=xo[:, cs], in1=xt[:, cs], op=ALU.add)
            if xout_d is not None:
                P.dma("sp", [t_xo], [], xout_d[i * 128:(i + 1) * 128, :], xo[:])
            if next_key is not None:
                norm_tile(C, M, xo, t_xo, M["mod"][(next_key, v, 1)], M["mod"][(next_key, v, 0)], hvn, i)
            if finalg_d is not None:
                os_ = ost[i % 2]; t_os = t_ost[i % 2]
                P.op("act", "activation", [t_xo], [M["t_junk"], tsm], out=M["junk"][:], in_=xo[:], func=AF.Square, accum_out=sm[:, 2:3])
                rms_scale(P, sm, tsm, 2, 3, 1024)
                P.op("dve", "scalar_tensor_tensor", [t_xo, tsm, t_fg], [t_os], out=os_[:], in0=xo[:], scalar=sm[:, 3:4], in1=fg[:], op0=ALU.mult, op1=ALU.mult)
                P.dma("sp", [t_os], [], out_d[i * 128:(i + 1) * 128, :], os_[:])


def rope_tables():
    t = np.arange(S)
    row = (t // GRID_W).astype(np.float32)
    col = (t % GRID_W).astype(np.float32)
    inv = (10000.0 ** (-np.arange(0, 32, 2, dtype=np.float32) / 32)).astype(np.float32)
    ang = np.concatenate([row[:, None] * inv, col[:, None] * inv], axis=-1).astype(np.float32)
    cos = np.cos(ang).astype(np.float32); sin = np.sin(ang).astype(np.float32)
    cosT = np.ones((128, T), np.float32); sinT = np.zeros((128, T), np.float32)
    for c in range(2):
        cosT[c * 64:c * 64 + 32, LC:] = cos.T
        cosT[c * 64 + 32:c * 64 + 64, LC:] = cos.T
        sinT[c * 64:c * 64 + 32, LC:] = -sin.T
        sinT[c * 64 + 32:c * 64 + 64, LC:] = sin.T
    return cosT, sinT


def wA_cols(g):
    cols = []
    ev = np.arange(0, 64, 2); od = np.arange(1, 64, 2)
    for base in (OFF["aq"], OFF["ak"]):
        main = []; swp = []
        for c in range(2):
            b0 = base + g * 128 + c * 64
            main += list(b0 + ev) + list(b0 + od)
            swp += list(b0 + od) + list(b0 + ev)
        cols += main + swp
    cols += list(OFF["az"] + g * 128 + np.arange(128)) + list(OFF["av"] + g * 128 + np.arange(128))
    return np.array(cols)


def rep128(v):
    return np.ascontiguousarray(np.broadcast_to(np.asarray(v, np.float32).reshape(1, -1), (128, v.size)))


def col128(v):
    return np.ascontiguousarray(np.broadcast_to(np.asarray(v, np.float32).reshape(-1, 1), (128, 128)))


def wB_cols(g):
    r = np.arange(128)
    cols = list(OFF["bq"] + g * 128 + r) + list(OFF["bk"] + g * 128 + r)
    cols += list(OFF["bv"] + g * 128 + r) + list(OFF["bo"] + g * 128 + r) + list(OFF["bz"] + g * 128 + r)
    cols += [OFF["bg"] + 0 + g, OFF["bg"] + 4 + g, OFF["bg"] + 8 + g, OFF["bg"] + 12 + g]
    return np.array(cols)


def convp_host(conv_w, conv_b, g):
    out = np.zeros((128, 8), np.float32)
    for j, base in enumerate((g * 128, 512 + g * 128)):
        out[:, 4 * j:4 * j + 3] = conv_w[:, base:base + 128].T
        out[:, 4 * j + 3] = conv_b[base:base + 128]
    return out


def tri_host():
    s = np.arange(128)
    triF = (s[:, None] <= s[None, :]).astype(np.float32)
    return np.ascontiguousarray(np.concatenate([triF, triF.T, np.ones((128, 128), np.float32)], axis=1))


def wC_cols(g):
    r = np.arange(128)
    return np.array(list(OFF["cq"] + g * 128 + r) + list(OFF["ck"] + g * 128 + r) + list(OFF["cv"] + g * 128 + r) + list(OFF["cz"] + g * 128 + r))


def na_geometry():
    pats = [(0, 0), (1, 0), (10, 8), (62, 59), (63, 59)]
    ridx = np.zeros((5, 5, 128, 128), np.int64); cidx = np.zeros_like(ridx); valid = np.zeros(ridx.shape, bool)
    ki = np.arange(128); qi = np.arange(128)
    for p, (j, kt0) in enumerate(pats):
        rq = 2 * j + qi // 64; cq = qi % 64
        rs = np.clip(rq - 4, 0, 120); cs = np.clip(cq - 8, 0, 48)
        for a in range(5):
            rk = 2 * (kt0 + a) + ki // 64; ck = ki % 64
            v = (rk[:, None] >= rs[None, :]) & (rk[:, None] < rs[None, :] + 8) & (ck[:, None] >= cs[None, :]) & (ck[:, None] < cs[None, :] + 16)
            valid[p, a] = v
            ridx[p, a] = np.clip(rk[:, None] - rq[None, :] + 7, 0, 14)
            cidx[p, a] = np.clip(ck[:, None] - cq[None, :] + 15, 0, 30)
    return ridx, cidx, valid


def na_bias_host(rpb, g):
    ridx, cidx, valid = na_geometry()
    out = np.zeros((128, 5, 2, 5, 128), np.float32)
    for hh in range(2):
        gathered = rpb[2 * g + hh][ridx, cidx]
        out[:, :, hh, :, :] = np.transpose(gathered, (2, 0, 1, 3))
    return np.ascontiguousarray(out.reshape(128, 6400))


def na_mask_host():
    ridx, cidx, valid = na_geometry()
    m = np.where(valid, 0.0, NEG).astype(np.float32)
    out = np.zeros((128, 5, 2, 5, 128), np.float32)
    for hh in range(2):
        out[:, :, hh, :, :] = np.transpose(m, (2, 0, 1, 3))
    return np.ascontiguousarray(out.reshape(128, 6400))


def build_N0():
    nc = bass.Bass("TRN2", target_bir_lowering=False)
    with contextlib.ExitStack() as es:
        C = Ctx(nc, es)
        x_d = C.dram("x_in", [NTOK2, D], F32, "ExternalInput")
        crepL = C.dram("crepL", [128, 8, 128], F32, "ExternalInput")
        crepC = C.dram("crepC", [128, 8, 128], F32, "ExternalInput")
        wmod = C.dram("wmod", [D, 3 * D], F32, "ExternalInput")
        bmod = C.dram("bmodrep", [128, 3 * D], F32, "ExternalInput")
        normg = C.dram("normgrep", [128, D], F32, "ExternalInput")
        ident_d = C.dram("ident", [128, 128], F32, "ExternalInput")
        hT_d = C.dram("hT_out", [D, NTOK2], BF16, "ExternalOutput")
        M = alloc_m2(C)
        load_ident(C, M, ident_d)
        alloc_mod_out(C, M, "n", [0, 1])
        outer = C.es
        with contextlib.ExitStack() as es2:
            C.es = es2
            alloc_mod_tmp(C, M)
            mod_vectors(C, M, "n", crepL, crepC, wmod, bmod, [0, 1], normg)
        C.es = outer
        C.P.barrier()
        stage_N(C, M, "n", x_d, hT_d)
        C.P.emit()
    return nc


def build_M1(l):
    nc = bass.Bass("TRN2", target_bir_lowering=False)
    with contextlib.ExitStack() as es:
        C = Ctx(nc, es)
        hT_d = C.dram("hT", [D, T], BF16, "ExternalInput")
        ident_d = C.dram("ident", [128, 128], F32, "ExternalInput")
        yT_d = C.dram("yT", [384, T], BF16, "ExternalOutput")
        wA_d = C.dram("wA", [D, 768], F32, "ExternalInput")
        cos_d = C.dram("cosT", [128, T], F32, "ExternalInput")
        sin_d = C.dram("sinT", [128, T], F32, "ExternalInput")
        lamq_d = C.dram("lamq", [128, 128], F32, "ExternalInput")
        lamk_d = C.dram("lamk", [128, 128], F32, "ExternalInput")
        dag_d = C.dram("dag", [128, 128], F32, "ExternalInput")
        wB_d = C.dram("wB", [D, 644], F32, "ExternalInput")
        convp_d = C.dram("convp", [128, 8], F32, "ExternalInput")
        gateb_d = C.dram("gateb", [128, 4], F32, "ExternalInput")
        mlg_d = C.dram("mlg", [128, 128], F32, "ExternalInput")
        tri_d = C.dram("tri", [128, 384], F32, "ExternalInput")
        wC_d = C.dram("wC", [D, 512], F32, "ExternalInput")
        biasg_d = C.dram("biasg", [128, 6400], F32, "ExternalInput")
        maskc_d = C.dram("maskc", [128, 6400], F32, "ExternalInput")
        R = alloc_m1(C)
        load_ident(C, R, ident_d)
        outer = C.es
        with contextlib.ExitStack() as es2:
            C.es = es2
            pass_A(C, R, l, hT_d, wA_d, cos_d, sin_d, lamq_d, lamk_d, dag_d, yT_d)
        C.P.barrier()
        with contextlib.ExitStack() as es2:
            C.es = es2
            pass_B(C, R, l, hT_d, wB_d, convp_d, gateb_d, mlg_d, tri_d, yT_d)
        C.P.barrier()
        with contextlib.ExitStack() as es2:
            C.es = es2
            pass_C(C, R, l, hT_d, wC_d, biasg_d, maskc_d, yT_d)
        C.es = outer
        C.P.emit()
    return nc


def build_M2(last):
    nc = bass.Bass("TRN2", target_bir_lowering=False)
    with contextlib.ExitStack() as es:
        C = Ctx(nc, es)
        x_d = C.dram("x_in", [NTOK2, D], F32, "ExternalInput")
        hT_d = C.dram("hT", [D, NTOK2], BF16, "ExternalInput")
        yT_d = C.dram("yT", [1536, NTOK2], BF16, "ExternalInput")
        wgm = C.dram("wgm", [D, 3 * D], F32, "ExternalInput")
        wbr = C.dram("wbr", [1536, D], F32, "ExternalInput")
        wout = C.dram("wout", [D, D], F32, "ExternalInput")
        crepL = C.dram("crepL", [128, 8, 128], F32, "ExternalInput")
        crepC = C.dram("crepC", [128, 8, 128], F32, "ExternalInput")
        wmod = C.dram("wmod", [D, 3 * D], F32, "ExternalInput")
        bmod = C.dram("bmodrep", [128, 3 * D], F32, "ExternalInput")
        ident_d = C.dram("ident", [128, 128], F32, "ExternalInput")
        M = alloc_m2(C)
        load_ident(C, M, ident_d)
        alloc_mod_out(C, M, "g", [2])
        if not last:
            alloc_mod_out(C, M, "n", [0, 1])
            wmod2 = C.dram("wmod2", [D, 3 * D], F32, "ExternalInput")
            bmod2 = C.dram("bmodrep2", [128, 3 * D], F32, "ExternalInput")
            normg2 = C.dram("normgrep2", [128, D], F32, "ExternalInput")
            xout_d = C.dram("x_out", [NTOK2, D], F32, "ExternalOutput")
            hTn_d = C.dram("hT_out", [D, NTOK2], BF16, "ExternalOutput")
        outer = C.es
        with contextlib.ExitStack() as es2:
            C.es = es2
            alloc_mod_tmp(C, M)
            mod_vectors(C, M, "g", crepL, crepC, wmod, bmod, [2])
            if not last:
                mod_vectors(C, M, "n", crepL, crepC, wmod2, bmod2, [0, 1], normg2)
        C.es = outer
        C.P.barrier()
        if not last:
            stage_M2(C, M, "g", x_d, hT_d, yT_d, wgm, wbr, wout, xout_d, next_key="n", hTn_d=hTn_d)
        else:
            fg = C.dram("finalg", [128, D], F32, "ExternalInput")
            out_d = C.dram("out", [NTOK2, D], F32, "ExternalOutput")
            stage_M2(C, M, "g", x_d, hT_d, yT_d, wgm, wbr, wout, None, finalg_d=fg, out_d=out_d)
        C.P.emit()
    return nc


def crep_host(c):
    a = np.asarray(c, np.float32).reshape(8, 128).T
    return np.ascontiguousarray(np.broadcast_to(a[:, :, None], (128, 8, 128)))


def assemble_hT(parts, b):
    return np.ascontiguousarray(np.concatenate([parts[(b, 0)][:, 0:256]] + [parts[(b, s)][:, 256:] for s in range(4)], axis=1))


def tok_slice(a, s):
    return np.concatenate([a[..., 0:256], a[..., 256 + s * 2048:256 + (s + 1) * 2048]], axis=-1)


CORES = [(b, i) for b in range(B) for i in range(4)]


def kernel_unfused(x, c, ctx, c_ctx, w_mod, b_mod, norm_g, w_in, da_lam_q, da_lam_k, da_norm_g,
           ml_conv_w, ml_conv_b, ml_gate_b, ml_norm_g, na_rpb, w_br, w_out, final_g):
    f32 = lambda a: np.ascontiguousarray(np.asarray(a, np.float32))
    x = f32(x); c = f32(c); ctx = f32(ctx); c_ctx = f32(c_ctx); w_mod = f32(w_mod); b_mod = f32(b_mod); norm_g = f32(norm_g)
    w_in = f32(w_in); w_br = f32(w_br); w_out = f32(w_out); final_g = f32(final_g)
    ident = np.eye(128, dtype=np.float32)
    cosT, sinT = rope_tables()
    tri = tri_host()
    maskc = na_mask_host()
    ids = list(range(8))
    crepC = crep_host(c_ctx)
    crepL = {b: crep_host(c[b]) for b in range(B)}

    x_cur = {(b, s): np.ascontiguousarray(np.concatenate([ctx[b], x[b, s * 2048:(s + 1) * 2048]], axis=0)) for (b, s) in CORES}
    ncA = build_N0()
    im = [dict(x_in=x_cur[(b, s)], crepL=crepL[b], crepC=crepC, wmod=w_mod[0], bmodrep=rep128(b_mod[0]), normgrep=rep128(norm_g[0]), ident=ident)
          for (b, s) in CORES]
    res = run_bass_kernel_spmd(ncA, im, core_ids=ids).results
    hparts = {CORES[i]: np.asarray(res[i]["hT_out"]) for i in range(8)}
    out = np.zeros((B, S, D), np.float32)
    for l in range(DEPTH):
        hT = {b: assemble_hT(hparts, b) for b in range(B)}
        nc1 = build_M1(l)
        im = []
        for (b, g) in CORES:
            gb = np.asarray(ml_gate_b[l], np.float32)
            im.append(dict(
                hT=hT[b], ident=ident,
                wA=np.ascontiguousarray(w_in[l][:, wA_cols(g)]), cosT=cosT, sinT=sinT,
                lamq=rep128(np.asarray(da_lam_q[l], np.float32).reshape(-1)), lamk=rep128(np.asarray(da_lam_k[l], np.float32).reshape(-1)),
                dag=col128(np.asarray(da_norm_g[l], np.float32)),
                wB=np.ascontiguousarray(w_in[l][:, wB_cols(g)]),
                convp=convp_host(np.asarray(ml_conv_w[l], np.float32), np.asarray(ml_conv_b[l], np.float32), g),
                gateb=rep128(np.array([gb[0, 0, g], gb[0, 1, g], gb[1, 0, g], gb[1, 1, g]], np.float32)),
                mlg=rep128(np.asarray(ml_norm_g[l], np.float32)), tri=tri,
                wC=np.ascontiguousarray(w_in[l][:, wC_cols(g)]), biasg=na_bias_host(np.asarray(na_rpb[l], np.float32), g), maskc=maskc))
        res = run_bass_kernel_spmd(nc1, im, core_ids=ids).results
        yT = {CORES[i]: np.asarray(res[i]["yT"]) for i in range(8)}
        last = (l == DEPTH - 1)
        nc2 = build_M2(last)
        im = []
        for (b, s) in CORES:
            yall = np.concatenate([yT[(b, g)][n * 128:(n + 1) * 128] for n in range(3) for g in range(4)], axis=0)
            d = dict(x_in=x_cur[(b, s)], hT=np.ascontiguousarray(tok_slice(hT[b], s)), yT=np.ascontiguousarray(tok_slice(yall, s)),
                     wgm=np.ascontiguousarray(w_in[l][:, OFF["gm"]:]), wbr=np.ascontiguousarray(w_br[l].reshape(1536, D)), wout=w_out[l],
                     crepL=crepL[b], crepC=crepC, wmod=w_mod[l], bmodrep=rep128(b_mod[l]), ident=ident)
            if not last:
                d.update(wmod2=w_mod[l + 1], bmodrep2=rep128(b_mod[l + 1]), normgrep2=rep128(norm_g[l + 1]))
            else:
                d.update(finalg=rep128(final_g))
            im.append(d)
        res = run_bass_kernel_spmd(nc2, im, core_ids=ids).results
        if not last:
            x_cur = {CORES[i]: np.asarray(res[i]["x_out"]) for i in range(8)}
            hparts = {CORES[i]: np.asarray(res[i]["hT_out"]) for i in range(8)}
        else:
            for i, (b, s) in enumerate(CORES):
                out[b, s * 2048:(s + 1) * 2048] = np.asarray(res[i]["out"])[256:]
    return out


@contextlib.contextmanager
def scope(C):
    outer = C.es
    with contextlib.ExitStack() as es2:
        C.es = es2
        try:
            yield
        finally:
            C.es = outer


def build_fused():
    nc = bass.Bass("TRN2", target_bir_lowering=False)
    with contextlib.ExitStack() as es:
        C = Ctx(nc, es)
        P = C.P
        ext = lambda name, shape, dt=F32: C.dram(name, shape, dt, "ExternalInput")
        x_in = ext("x_in", [T, D])
        crepL = ext("crepL", [128, 8, 128]); crepC = ext("crepC", [128, 8, 128]); ident_d = ext("ident", [128, 128])
        cos_d = ext("cosT", [128, T]); sin_d = ext("sinT", [128, T]); tri_d = ext("tri", [128, 384]); maskc_d = ext("maskc", [128, 6400])
        fg_d = ext("finalg", [128, D])
        L = []
        for l in range(DEPTH):
            L.append(dict(
                wmod=ext(f"wmod{l}", [D, 3 * D]), bmod=ext(f"bmodrep{l}", [128, 3 * D]), normg=ext(f"normgrep{l}", [128, D]),
                wA=ext(f"wA{l}", [4, D, 768]), wB=ext(f"wB{l}", [4, D, 644]), wC=ext(f"wC{l}", [4, D, 512]),
                lamq=ext(f"lamq{l}", [128, 128]), lamk=ext(f"lamk{l}", [128, 128]), dag=ext(f"dag{l}", [128, 128]),
                convp=ext(f"convp{l}", [4, 128, 8]), gateb=ext(f"gateb{l}", [4, 128, 4]), mlg=ext(f"mlg{l}", [128, 128]),
                biasg=ext(f"biasg{l}", [4, 128, 6400]),
                wgm=ext(f"wgm{l}", [D, 3 * D]), wbr=ext(f"wbr{l}", [1536, D]), wout=ext(f"wout{l}", [D, D])))
        out_d = C.dram("out", [T, D], F32, "ExternalOutput")
        hTs = [nc.dram_tensor(f"hT_scr{i}", [D, T], BF16).ap() for i in range(2)]
        yT = nc.dram_tensor("yT_scr", [1536, T], BF16).ap()
        xbuf = nc.dram_tensor("x_scr", [T, D], F32).ap()

        G = alloc_common(C)
        load_ident(C, G, ident_d)
        with scope(C):
            M = alloc_m2(C, G)
            alloc_mod_out(C, M, "n", [0, 1])
            with scope(C):
                alloc_mod_tmp(C, M)
                mod_vectors(C, M, "n", crepL, crepC, L[0]["wmod"], L[0]["bmod"], [0, 1], L[0]["normg"])
            P.barrier()
            stage_N(C, M, "n", x_in, hTs[0], ntok=T)
        P.barrier()
        for l in range(DEPTH):
            last = (l == DEPTH - 1)
            W = L[l]
            hT = hTs[l % 2]
            with scope(C):
                R = alloc_m1(C, G)
                for g in range(4):
                    with scope(C):
                        pass_A(C, R, l, hT, W["wA"][g], cos_d, sin_d, W["lamq"], W["lamk"], W["dag"], yT, yrow=0 * 512 + g * 128)
                    P.barrier()
                    with scope(C):
                        pass_B(C, R, l, hT, W["wB"][g], W["convp"][g], W["gateb"][g], W["mlg"], tri_d, yT, yrow=1 * 512 + g * 128)
                    P.barrier()
                    with scope(C):
                        pass_C(C, R, l, hT, W["wC"][g], W["biasg"][g], maskc_d, yT, yrow=2 * 512 + g * 128)
                    P.barrier()
            with scope(C):
                M = alloc_m2(C, G)
                alloc_mod_out(C, M, "g", [2])
                if not last:
                    alloc_mod_out(C, M, "n", [0, 1])
                with scope(C):
                    alloc_mod_tmp(C, M)
                    mod_vectors(C, M, "g", crepL, crepC, W["wmod"], W["bmod"], [2])
                    if not last:
                        mod_vectors(C, M, "n", crepL, crepC, L[l + 1]["wmod"], L[l + 1]["bmod"], [0, 1], L[l + 1]["normg"])
                P.barrier()
                x_src = x_in if l == 0 else xbuf
                if not last:
                    stage_M2(C, M, "g", x_src, hT, yT, W["wgm"], W["wbr"], W["wout"], xbuf, next_key="n", hTn_d=hTs[(l + 1) % 2], chunks=CHUNKS)
                else:
                    stage_M2(C, M, "g", x_src, hT, yT, W["wgm"], W["wbr"], W["wout"], None, finalg_d=fg_d, out_d=out_d, chunks=CHUNKS)
            P.barrier()
        print("[kernel] instruction counts", {k: len(v) for k, v in P.q.items()}, flush=True)
        P.emit()
    return nc


def kernel(x, c, ctx, c_ctx, w_mod, b_mod, norm_g, w_in, da_lam_q, da_lam_k, da_norm_g,
           ml_conv_w, ml_conv_b, ml_gate_b, ml_norm_g, na_rpb, w_br, w_out, final_g):
    f32 = lambda a: np.ascontiguousarray(np.asarray(a, np.float32))
    x = f32(x); c = f32(c); ctx = f32(ctx); c_ctx = f32(c_ctx); w_mod = f32(w_mod); b_mod = f32(b_mod); norm_g = f32(norm_g)
    w_in = f32(w_in); w_br = f32(w_br); w_out = f32(w_out); final_g = f32(final_g)
    cosT, sinT = rope_tables()
    shared = dict(ident=np.eye(128, dtype=np.float32), cosT=cosT, sinT=sinT, tri=tri_host(), maskc=na_mask_host(),
                  finalg=rep128(final_g), crepC=crep_host(c_ctx))
    for l in range(DEPTH):
        gb = np.asarray(ml_gate_b[l], np.float32)
        shared.update({
            f"wmod{l}": w_mod[l], f"bmodrep{l}": rep128(b_mod[l]), f"normgrep{l}": rep128(norm_g[l]),
            f"wA{l}": np.ascontiguousarray(np.stack([w_in[l][:, wA_cols(g)] for g in range(4)])),
            f"wB{l}": np.ascontiguousarray(np.stack([w_in[l][:, wB_cols(g)] for g in range(4)])),
            f"wC{l}": np.ascontiguousarray(np.stack([w_in[l][:, wC_cols(g)] for g in range(4)])),
            f"lamq{l}": rep128(np.asarray(da_lam_q[l], np.float32).reshape(-1)), f"lamk{l}": rep128(np.asarray(da_lam_k[l], np.float32).reshape(-1)),
            f"dag{l}": col128(np.asarray(da_norm_g[l], np.float32)),
            f"convp{l}": np.stack([convp_host(np.asarray(ml_conv_w[l], np.float32), np.asarray(ml_conv_b[l], np.float32), g) for g in range(4)]),
            f"gateb{l}": np.stack([rep128(np.array([gb[0, 0, g], gb[0, 1, g], gb[1, 0, g], gb[1, 1, g]], np.float32)) for g in range(4)]),
            f"mlg{l}": rep128(np.asarray(ml_norm_g[l], np.float32)),
            f"biasg{l}": np.stack([na_bias_host(np.asarray(na_rpb[l], np.float32), g) for g in range(4)]),
            f"wgm{l}": np.ascontiguousarray(w_in[l][:, OFF["gm"]:]), f"wbr{l}": np.ascontiguousarray(w_br[l].reshape(1536, D)), f"wout{l}": w_out[l]})
    im = []
    for b in range(B):
        d = dict(shared)
        d["x_in"] = np.ascontiguousarray(np.concatenate([ctx[b], x[b]], axis=0))
        d["crepL"] = crep_host(c[b])
        im.append(d)
    nc = build_fused()
    res = run_bass_kernel_spmd(nc, im, core_ids=list(range(B))).results
    return np.stack([np.asarray(res[b]["out"])[LC:] for b in range(B)]).astype(np.float32)


kernel = kernel_unfused
```
